# Optimizing a Trainium2 kernel written in Bass

```python
import jax, jax.numpy as jnp
from jax import lax
import numpy as np

D_MODEL = 2048
BATCH = 2
SEQ = 4096
DEPTH = 1
DEC_BATCH = 32
DEC_SEQ = 32
PAST_LEN = 2048

CHUNK = 64
D_CONV = D_MODEL // 2
CONV_WIDTH = 3
H_ATTN = 8
HD_ATTN = 128
D_ATTN = H_ATTN * HD_ATTN
D_MIX = D_CONV + D_ATTN
D_IN = 3 * D_CONV + 3 * D_ATTN + H_ATTN
N_MEM = 256
H_X = 4
HD_X = 128
D_X = H_X * HD_X
D_FF = 4 * D_MODEL
Q_BLOCK = 128
RMS_EPS = 1e-6
FORGET_BIAS_INIT = 3.0

kernel_name = "hybrid_conv_fox_stream_step"


def rmsnorm(x, g):
    xf = x.astype(jnp.float32)
    y = xf * lax.rsqrt(jnp.mean(xf * xf, axis=-1, keepdims=True) + RMS_EPS)
    return (y * g.astype(jnp.float32)).astype(x.dtype)


def project(xn, w_in, b_f):
    N, T, _ = xn.shape
    z = xn @ w_in
    o1, o2, o3 = D_CONV, 2 * D_CONV, 3 * D_CONV
    o4, o5, o6 = o3 + D_ATTN, o3 + 2 * D_ATTN, o3 + 3 * D_ATTN
    gb, gc, hin = z[..., :o1], z[..., o1:o2], z[..., o2:o3]
    q = z[..., o3:o4].reshape(N, T, H_ATTN, HD_ATTN)
    k = z[..., o4:o5].reshape(N, T, H_ATTN, HD_ATTN)
    v = z[..., o5:o6].reshape(N, T, H_ATTN, HD_ATTN)
    logf = jax.nn.log_sigmoid(z[..., o6:].astype(jnp.float32) + b_f.astype(jnp.float32))
    return gb, gc, hin, q, k, v, logf


def short_conv(u, buf, w):
    T = u.shape[1]
    up = jnp.concatenate([buf, u], axis=1)
    y = w[0] * up[:, 0:T]
    for j in range(1, CONV_WIDTH):
        y = y + w[j] * up[:, j:j + T]
    return y, up[:, -(CONV_WIDTH - 1):]


def fox_attention(q, k, v, cq, ck, q_pos, k_pos):
    N, Tq, H, D = q.shape
    blk = Q_BLOCK if Tq % Q_BLOCK == 0 else Tq
    nb = Tq // blk
    scale = HD_ATTN ** -0.5
    qb = q.reshape(N, nb, blk, H, D).transpose(1, 0, 2, 3, 4)
    cqb = cq.reshape(N, nb, blk, H).transpose(1, 0, 3, 2)
    pb = q_pos.reshape(nb, blk)
    ckT = ck.transpose(0, 2, 1)[:, :, None, :]

    def one_block(args):
        qi, ci, pi = args
        s = jnp.einsum('nqhd,nkhd->nhqk', qi, k, preferred_element_type=jnp.float32) * scale
        s = s + (ci[..., None] - ckT)
        mask = k_pos[None, :] <= pi[:, None]
        s = jnp.where(mask, s, -jnp.inf)
        p = jax.nn.softmax(s, axis=-1)
        return jnp.einsum('nhqk,nkhd->nqhd', p.astype(v.dtype), v)

    out = lax.map(one_block, (qb, cqb, pb))
    return out.transpose(1, 0, 2, 3, 4).reshape(N, Tq, H * D)


def parallel_mixer(xn, conv_buf, k_past, v_past, logf_past, w_in, b_f, conv_w, g_conv_out, g_attn_out, w_out):
    N, T, _ = xn.shape
    P = k_past.shape[1]
    gb, gc, hin, q, k, v, logf = project(xn, w_in, b_f)
    yc, conv_new = short_conv(gc * hin, conv_buf, conv_w)
    yc = gb * yc
    k_all = jnp.concatenate([k_past, k], axis=1)
    v_all = jnp.concatenate([v_past, v], axis=1)
    c_all = jnp.cumsum(jnp.concatenate([logf_past.astype(jnp.float32), logf], axis=1), axis=1)
    k_pos = jnp.arange(P + T)
    q_pos = P + jnp.arange(T)
    ya = fox_attention(q, k_all, v_all, c_all[:, P:], c_all, q_pos, k_pos)
    y = jnp.concatenate([rmsnorm(yc, g_conv_out), rmsnorm(ya, g_attn_out)], axis=-1) @ w_out
    return y, conv_new, k, v, logf


def memory_kv(mem, g_mem, w_xkv):
    N = mem.shape[0]
    kv = rmsnorm(mem, g_mem) @ w_xkv
    mk = kv[..., :D_X].reshape(N, N_MEM, H_X, HD_X)
    mv = kv[..., D_X:].reshape(N, N_MEM, H_X, HD_X)
    return mk, mv


def cross_attention(xn, mk, mv, w_xq, w_xo):
    N, T, _ = xn.shape
    q = (xn @ w_xq).reshape(N, T, H_X, HD_X)
    s = jnp.einsum('nqhd,nkhd->nhqk', q, mk, preferred_element_type=jnp.float32) * (HD_X ** -0.5)
    p = jax.nn.softmax(s, axis=-1)
    o = jnp.einsum('nhqk,nkhd->nqhd', p.astype(mv.dtype), mv).reshape(N, T, D_X)
    return o @ w_xo


def sq_relu_mlp(xn, w_up, w_down):
    return jnp.square(jax.nn.relu(xn @ w_up)) @ w_down


def setup_inputs(seed: int = 0) -> dict:
    key = jax.random.key(seed)
    ks = jax.random.split(key, 32)
    f32 = jnp.float32

    def nrm(k, shape, scale):
        return jax.random.normal(k, shape, f32) * scale

    def gain(k, shape):
        return 1.0 + 0.01 * jax.random.normal(k, shape, f32)

    return {
        "x_prompt": nrm(ks[0], (BATCH, SEQ, D_MODEL), 1.0),
        "x_sample": nrm(ks[1], (DEC_BATCH, DEC_SEQ, D_MODEL), 1.0),
        "cache_k": nrm(ks[2], (DEPTH, DEC_BATCH, PAST_LEN, H_ATTN, HD_ATTN), 1.0),
        "cache_v": nrm(ks[3], (DEPTH, DEC_BATCH, PAST_LEN, H_ATTN, HD_ATTN), 1.0),
        "cache_logf": jax.nn.log_sigmoid(FORGET_BIAS_INIT + nrm(ks[4], (DEPTH, DEC_BATCH, PAST_LEN, H_ATTN), 1.0)),
        "cache_conv": nrm(ks[5], (DEPTH, DEC_BATCH, CONV_WIDTH - 1, D_CONV), 1.0),
        "cache_mem_k": nrm(ks[6], (DEPTH, DEC_BATCH, N_MEM, H_X, HD_X), 1.0),
        "cache_mem_v": nrm(ks[7], (DEPTH, DEC_BATCH, N_MEM, H_X, HD_X), 1.0),
        "mem_prompt": nrm(ks[8], (BATCH, N_MEM, D_MODEL), 1.0),
        "g_mix": gain(ks[9], (DEPTH, D_MODEL)),
        "w_in": nrm(ks[10], (DEPTH, D_MODEL, D_IN), D_MODEL ** -0.5),
        "b_f": FORGET_BIAS_INIT + nrm(ks[11], (DEPTH, H_ATTN), 0.1),
        "conv_w": nrm(ks[12], (DEPTH, CONV_WIDTH, D_CONV), CONV_WIDTH ** -0.5),
        "g_conv_out": gain(ks[13], (DEPTH, D_CONV)),
        "g_attn_out": gain(ks[14], (DEPTH, D_ATTN)),
        "w_out": nrm(ks[15], (DEPTH, D_MIX, D_MODEL), D_MIX ** -0.5),
        "g_xattn": gain(ks[16], (DEPTH, D_MODEL)),
        "g_mem": gain(ks[17], (DEPTH, D_MODEL)),
        "w_xq": nrm(ks[18], (DEPTH, D_MODEL, D_X), D_MODEL ** -0.5),
        "w_xkv": nrm(ks[19], (DEPTH, D_MODEL, 2 * D_X), D_MODEL ** -0.5),
        "w_xo": nrm(ks[20], (DEPTH, D_X, D_MODEL), D_X ** -0.5),
        "g_mlp": gain(ks[21], (DEPTH, D_MODEL)),
        "w_up": nrm(ks[22], (DEPTH, D_MODEL, D_FF), D_MODEL ** -0.5),
        "w_down": nrm(ks[23], (DEPTH, D_FF, D_MODEL), D_FF ** -0.5),
        "g_final": gain(ks[24], (D_MODEL,)),
    }


def reference(x_prompt, x_sample, cache_k, cache_v, cache_logf, cache_conv, cache_mem_k, cache_mem_v, mem_prompt,
              g_mix, w_in, b_f, conv_w, g_conv_out, g_attn_out, w_out, g_xattn, g_mem, w_xq, w_xkv, w_xo,
              g_mlp, w_up, w_down, g_final):
    xp, xs = x_prompt, x_sample
    Bp = xp.shape[0]
    kp_l, vp_l, fp_l, cp_l, mkp_l, mvp_l = [], [], [], [], [], []
    ks_l, vs_l, fs_l, cs_l = [], [], [], []
    for l in range(DEPTH):
        mix_w = (w_in[l], b_f[l], conv_w[l], g_conv_out[l], g_attn_out[l], w_out[l])
        conv0 = jnp.zeros((Bp, CONV_WIDTH - 1, D_CONV), xp.dtype)
        k0 = jnp.zeros((Bp, 0, H_ATTN, HD_ATTN), xp.dtype)
        f0 = jnp.zeros((Bp, 0, H_ATTN), jnp.float32)
        y, conv_p, k_p, v_p, f_p = parallel_mixer(rmsnorm(xp, g_mix[l]), conv0, k0, k0, f0, *mix_w)
        xp = xp + y
        mk_p, mv_p = memory_kv(mem_prompt, g_mem[l], w_xkv[l])
        xp = xp + cross_attention(rmsnorm(xp, g_xattn[l]), mk_p, mv_p, w_xq[l], w_xo[l])
        xp = xp + sq_relu_mlp(rmsnorm(xp, g_mlp[l]), w_up[l], w_down[l])
        y, conv_s, k_s, v_s, f_s = parallel_mixer(rmsnorm(xs, g_mix[l]), cache_conv[l], cache_k[l], cache_v[l],
                                                  cache_logf[l], *mix_w)
        xs = xs + y
        xs = xs + cross_attention(rmsnorm(xs, g_xattn[l]), cache_mem_k[l], cache_mem_v[l], w_xq[l], w_xo[l])
        xs = xs + sq_relu_mlp(rmsnorm(xs, g_mlp[l]), w_up[l], w_down[l])
        kp_l.append(k_p); vp_l.append(v_p); fp_l.append(f_p); cp_l.append(conv_p)
        mkp_l.append(mk_p); mvp_l.append(mv_p)
        ks_l.append(k_s); vs_l.append(v_s); fs_l.append(f_s); cs_l.append(conv_s)
    y_prompt = rmsnorm(xp, g_final)
    y_sample = rmsnorm(xs, g_final)
    new_k_prompt = jnp.stack(kp_l)
    new_v_prompt = jnp.stack(vp_l)
    new_logf_prompt = jnp.stack(fp_l)
    new_conv_prompt = jnp.stack(cp_l)
    new_mem_k_prompt = jnp.stack(mkp_l)
    new_mem_v_prompt = jnp.stack(mvp_l)
    new_k_sample = jnp.stack(ks_l)
    new_v_sample = jnp.stack(vs_l)
    new_logf_sample = jnp.stack(fs_l)
    new_conv_sample = jnp.stack(cs_l)
    return (y_prompt, y_sample, new_k_prompt, new_v_prompt, new_logf_prompt, new_conv_prompt,
            new_mem_k_prompt, new_mem_v_prompt, new_k_sample, new_v_sample, new_logf_sample, new_conv_sample)
```

```python
import numpy as np
from contextlib import ExitStack
import concourse.bass as bass
import concourse.mybir as mybir
from concourse.bass_utils import run_bass_kernel_spmd

F32 = mybir.dt.float32
BF16 = mybir.dt.bfloat16
AF = mybir.ActivationFunctionType
ALU = mybir.AluOpType
P = 128
D = 2048
KC = 16
NCTX = 32
NSLOT = 40
NATT = 24
TO = 1152
TX = 1280
EPS = 1e-6
SCALE = 128.0 ** -0.5
NEG = -30000.0


class Tracker:
    def __init__(self, nc, st):
        self.nc = nc
        self.st = st
        self.E = {}
        for name, e in (("pe", nc.tensor), ("act", nc.scalar), ("dve", nc.vector),
                        ("pool", nc.gpsimd), ("sp", nc.sync)):
            sem = st.enter_context(nc.semaphore("s_" + name))
            self.E[name] = dict(e=e, sem=sem, n=0, cnt=0, sig=[], seen={})
        self.lw = {}
        self.rd = {}
        self.ds = {}

    def _resolve(self, dep):
        if dep[0] == "d":
            S = self.ds[dep[1]]
            return ("d:" + dep[1], S["sem"], dep[2])
        _, en, idx = dep
        E = self.E[en]
        sig = E["sig"]
        lo, hi = 0, len(sig)
        while lo < hi:
            mid = (lo + hi) // 2
            if sig[mid][0] >= idx:
                hi = mid
            else:
                lo = mid + 1
        if lo >= len(sig):
            raise RuntimeError("dependency on unsignaled op on %s" % en)
        return ("e:" + en, E["sem"], sig[lo][1])

    def _wait(self, en, deps):
        E = self.E[en]
        need = {}
        for dep in deps:
            if dep[0] == "e" and dep[1] == en and en in ("pe", "sp"):
                continue
            sid, sem, val = self._resolve(dep)
            if val > need.get(sid, (None, 0))[1]:
                need[sid] = (sem, val)
        for sid, (sem, val) in need.items():
            if val > E["seen"].get(sid, 0):
                E["e"].wait_ge(sem, val)
                E["seen"][sid] = val

    def _deps(self, r, w):
        deps = []
        for k in r:
            if k in self.lw:
                deps.append(self.lw[k])
        for k in w:
            if k in self.lw:
                deps.append(self.lw[k])
            deps.extend(self.rd.get(k, {}).values())
        return deps

    def _record(self, me, rk, r, w):
        for k in w:
            self.lw[k] = me
            self.rd[k] = {}
        for k in r:
            self.rd.setdefault(k, {})[rk] = me

    def op(self, en, fn, r=(), w=(), sig=True):
        E = self.E[en]
        self._wait(en, self._deps(r, w))
        ins = fn(E["e"])
        idx = E["n"]
        E["n"] += 1
        if sig:
            E["cnt"] += 1
            ins.then_inc(E["sem"], 1)
            E["sig"].append((idx, E["cnt"]))
        self._record(("e", en, idx), "e:" + en, r, w)
        return ins

    def dma(self, qn, out, in_, r=(), w=(), key=None):
        E = self.E[qn]
        self._wait(qn, self._deps(r, w))
        key = key + ("|s" if qn == "pool" else "|h")
        if key not in self.ds:
            self.ds[key] = dict(sem=self.st.enter_context(self.nc.semaphore("d%d" % len(self.ds))), tot=0)
        S = self.ds[key]
        ins = E["e"].dma_start(out=out, in_=in_)
        S["tot"] += 16
        ins.then_inc(S["sem"], 16)
        E["n"] += 1
        self._record(("d", key, S["tot"]), "d:" + key, r, w)
        return ins

    def coll(self, kind, in_ap, out_ap, groups, r=(), w=(), key=None):
        E = self.E["pool"]
        self._wait("pool", self._deps(r, w))
        if key not in self.ds:
            self.ds[key] = dict(sem=self.st.enter_context(self.nc.semaphore("d%d" % len(self.ds))), tot=0)
        S = self.ds[key]
        ins = E["e"].collective_compute(kind, ALU.bypass, replica_groups=groups, ins=[in_ap], outs=[out_ap],
                                            dma_qos="P3")
        S["tot"] += 1
        ins.then_inc(S["sem"], 1)
        E["n"] += 1
        self._record(("d", key, S["tot"]), "d:" + key, r, w)
        return ins

    def barrier(self):
        for en, E in self.E.items():
            for fn, F in self.E.items():
                if fn == en or fn == "sp" or F["n"] == 0:
                    continue
                if not F["sig"]:
                    continue
                val = F["sig"][-1][1]
                sid = "e:" + fn
                if val > E["seen"].get(sid, 0):
                    E["e"].wait_ge(F["sem"], val)
                    E["seen"][sid] = val
            for k, S in self.ds.items():
                if k.startswith("wr") or k.startswith("w2s") or k == "cc":
                    continue
                sid = "d:" + k
                if S["tot"] > E["seen"].get(sid, 0):
                    E["e"].wait_ge(S["sem"], S["tot"])
                    E["seen"][sid] = S["tot"]

    def check_last_signaled(self):
        for en, E in self.E.items():
            if en == "sp" or E["n"] == 0:
                continue


def build():
    nc = bass.Bass("TRN2", target_bir_lowering=False)

    def din(name, shape):
        return nc.dram_tensor(name, list(shape), F32, kind="ExternalInput").ap()

    def dout(name, shape):
        return nc.dram_tensor(name, list(shape), F32, kind="ExternalOutput").ap()

    x_own = din("x_own", [9, P, D])
    x_halo = din("x_halo", [P, D])
    mem = din("mem", [256, D])
    ck = din("ck", [4, 2048, 1024])
    cv = din("cv", [4, 2048, 1024])
    clogf = din("clogf", [4, 2048, 8])
    cconv = din("cconv", [8, 1024])
    cmk = din("cmk", [4, 256, 512])
    cmv = din("cmv", [4, 256, 512])
    w_in = din("w_in", [D, 6152])
    w_out = din("w_out", [D, D])
    w_xq = din("w_xq", [D, 512])
    w_xkv = din("w_xkv", [D, 1024])
    w_xo = din("w_xo", [512, D])
    w_up = din("w_up", [D, 8192])
    w_down = din("w_down", [8192, D])
    gb_in = {n: din(n, [P, D]) for n in ("g_mix", "g_xattn", "g_mem", "g_mlp", "g_final")}
    cf_in = din("cf32", [P, 14, P])
    cb_in = din("cb16", [P, 4, P])
    sp_in = din("smallp", [P, 112])

    y_own = dout("y_own", [9, P, D])
    k_new = dout("k_new", [9, P, 1024])
    v_new = dout("v_new", [9, P, 1024])
    f_new = dout("f_new", [P, 9, 8])
    conv_new = dout("conv_new", [10, 1024])
    mk_new = dout("mk_new", [256, 512])
    mv_new = dout("mv_new", [256, 512])

    kloc_t = [nc.dram_tensor("kloc%d" % q, [512, 1024], BF16) for q in range(2)]
    kall_t = [nc.dram_tensor("kall%d" % q, [2048, 1024], BF16) for q in range(2)]
    vloc_t = [nc.dram_tensor("vloc%d" % q, [1024, 512], BF16) for q in range(2)]
    vall_t = [nc.dram_tensor("vall%d" % q, [4096, 512], BF16) for q in range(2)]
    floc_t = nc.dram_tensor("floc", [1024, 8], F32)
    fall_t = nc.dram_tensor("fall", [4096, 8], F32)
    kloc = [t.ap() for t in kloc_t]
    kall = [t.ap() for t in kall_t]
    vloc = [t.ap() for t in vloc_t]
    vall = [t.ap() for t in vall_t]
    floc, fall = floc_t.ap(), fall_t.ap()
    GROUPS = [[0, 1, 2, 3], [4, 5, 6, 7]]
    w2s = nc.dram_tensor("w2s", [4, P, 8192], BF16).ap()
    mkT_d = nc.dram_tensor("mkT_d", [P, 1024], BF16).ap()
    mvb_d = nc.dram_tensor("mvb_d", [P, 1024], BF16).ap()
    mkTs_d = nc.dram_tensor("mkTs_d", [P, 4096], BF16).ap()
    cmvb_d = nc.dram_tensor("cmvb_d", [P, 8, 512], BF16).ap()

    w_in_v = w_in.rearrange("(kc p) n -> p kc n", p=P)
    w_out_v = w_out.rearrange("(kc p) n -> p kc n", p=P)
    w_xq_v = w_xq.rearrange("(kc p) n -> p kc n", p=P)
    w_xkv_v = w_xkv.rearrange("(kc p) n -> p kc n", p=P)
    w_xo_v = w_xo.rearrange("(kc p) n -> p kc n", p=P)
    w_up_v = w_up.rearrange("(kc p) n -> p kc n", p=P)
    w_down_v = w_down.rearrange("(kc p) n -> p kc n", p=P)

    WSEQ = []
    for c0 in (4096, 4608, 5120, 5632):
        WSEQ.append((w_in_v[:, :, c0:c0 + 512], 16))
    for c0 in (0, 1024, 2048, 1536, 2560, 512, 3072, 3584):
        WSEQ.append((w_in_v[:, :, c0:c0 + 512], 16))
    WSEQ.append((w_xkv_v[:, :, 0:512], 16))
    WSEQ.append((w_xkv_v[:, :, 512:1024], 16))
    for nb in range(4):
        WSEQ.append((w_out_v[:, :, nb * 512:(nb + 1) * 512], 16))
    WSEQ.append((w_xq_v, 16))
    WSEQ.append((w_xo_v, 4))
    for hb in range(16):
        WSEQ.append((w_up_v[:, :, hb * 512:(hb + 1) * 512], 16))
        WSEQ.append((w_down_v[:, hb * 4:(hb + 1) * 4, :], 4))

    with ExitStack() as st:
        T = Tracker(nc, st)

        def sb(stk, name, shape, dt=F32):
            return stk.enter_context(nc.sbuf_tensor(name, list(shape), dt))

        ps, pb = [], []
        pstk = [ExitStack()]
        pgen = [0]

        def alloc_psum(nf, nbf):
            pstk[0].close()
            pstk[0] = ExitStack()
            g = pgen[0]
            pgen[0] += 1
            ps[:] = [pstk[0].enter_context(nc.psum_tensor("ps%d_%d" % (i, g), [P, 512], F32)) for i in range(nf)]
            pb[:] = [pstk[0].enter_context(nc.psum_tensor("pb%d_%d" % (i, g), [P, 1024], BF16)) for i in range(nbf)]

        alloc_psum(5, 3)
        PS = lambda i: "ps%d" % i
        PB = lambda i: "pb%d" % i

        wr = [sb(st, "wr%d" % i, [P, 8192], BF16) for i in range(4)]
        gbc = sb(st, "gbc", [P, D])
        cf = sb(st, "cf", [P, 14, P])
        cb = sb(st, "cb", [P, 4, P], BF16)
        smp = sb(st, "smp", [P, 112])
        wf = sb(st, "wf", [P, KC, 8], BF16)
        stt = sb(st, "stt", [P, 2, 8])
        rsc = sb(st, "rsc", [P, 16])
        rsa = sb(st, "rsa", [P, 16])
        sstat = sb(st, "sstat", [P, 32])
        qT_d = nc.dram_tensor("qT_d", [8, P, TO], BF16).ap()
        ycg_d = nc.dram_tensor("ycg_d", [8, P, TO], BF16).ap()
        yag_d = nc.dram_tensor("yag_d", [8, P, TO], BF16).ap()
        xres_d = nc.dram_tensor("xres_d", [9, P, D], F32).ap()
        identf = cf[:, 0, :]
        Umat = cf[:, 1, :]
        E127 = cf[:, 2, :]
        Ublk = cf[:, 3, :]
        EblkEnd = cf[:, 4, :]
        Esel = lambda s2: cf[:, 5 + s2, :]
        Epast = lambda s2: cf[:, 9 + s2, :]
        onesf = cf[:, 13, :]
        identb = cb[:, 0, :]
        maskd = cb[:, 1, :]
        masks = cb[:, 2, :]
        onesb = cb[:, 3, :]
        bfbc = smp[:, 0:8]
        gcT = smp[:, 8:16]
        gaT = smp[:, 16:24]
        cwT = smp[:, 24:48]
        kmask = smp[:, 48:80]
        valid = smp[:, 80:112]

        T.dma("sp", cf[:], cf_in, w=["constA"], key="constA")
        T.dma("sp", smp[:], sp_in, w=["constA"], key="constA")
        T.dma("pool", cb[:], cb_in, w=["constB"], key="constB")
        T.dma("pool", wf[:], w_in_v[:, :, 6144:6152], w=["constB"], key="constB")
        CONST = ["constA", "constB"]

        wstate = dict(next=0, released=set(), limit=17)

        def wpump():
            while wstate["next"] < len(WSEQ):
                i = wstate["next"]
                if i >= 4 and (i - 4) not in wstate["released"]:
                    break
                if wstate.get("limit") is not None and i >= wstate["limit"]:
                    break
                src, kcn = WSEQ[i]
                s = i % 4
                dst = wr[s][:].rearrange("p (k n) -> p k n", k=kcn)
                if i == 17:
                    T.dma("pool", dst, src, w=["wr%d" % s, "kp0", "kp1", "kpT0", "kpT1"], key="wr%d" % s)
                elif 8 <= i < 12:
                    T.dma("sp", wr[s][:], w2s[i - 8], r=["w2s"], w=["wr%d" % s], key="wr%d" % s)
                else:
                    T.dma("pool", dst, src, w=["wr%d" % s], key="wr%d" % s)
                wstate["next"] += 1

        def wview(i):
            src, kcn = WSEQ[i]
            return wr[i % 4][:].rearrange("p (k n) -> p k n", k=kcn), "wr%d" % (i % 4)

        def wrel(i):
            wstate["released"].add(i)
            wpump()

        wpump()

        def load_gbc(name):
            T.dma("sp", gbc[:], gb_in[name], w=["gbc"], key="gbc")

        def mm(out, lhsT, rhs, start, stop, r, w, sig=None):
            s = stop if sig is None else sig
            return T.op("pe", lambda e: e.matmul(out, lhsT=lhsT, rhs=rhs, start=start, stop=stop),
                        r=r, w=w, sig=s)

        def tp(out, in_, ident, r, w, sig):
            return T.op("pe", lambda e: e.transpose(out=out, in_=in_, identity=ident), r=r, w=w, sig=sig)

        def rstd_from(ssq, out, tmp, scale, keys):
            T.op("dve", lambda e: e.tensor_scalar(out=tmp, in0=ssq, scalar1=scale, scalar2=EPS,
                                                  op0=ALU.mult, op1=ALU.add), r=keys, w=keys)
            T.op("act", lambda e: e.activation(out=tmp, in_=tmp, func=AF.Ln), r=keys, w=keys)
            T.op("act", lambda e: e.activation(out=out, in_=tmp, func=AF.Exp, scale=-0.5), r=keys, w=keys)

        def norm_pre_g(xap, xkey, xsb, xsbkey, sti):
            sk = "stt%d" % sti
            T.op("dve", lambda e: e.memset(stt[:, sti, 0:4], 0.0), w=[sk])
            T.op("act", lambda e: e.activation(out=xsb, in_=xap, func=AF.Square,
                                               accum_out=stt[:, sti, 0:1]), r=[xkey], w=[sk, xsbkey])
            rstd_from(stt[:, sti, 0:1], stt[:, sti, 2:3], stt[:, sti, 1:2], 1.0 / D, [sk])
            T.op("dve", lambda e: e.scalar_tensor_tensor(out=xsb, in0=xap, scalar=stt[:, sti, 2:3], in1=gbc[:],
                                                         op0=ALU.mult, op1=ALU.mult),
                 r=[xkey, sk, "gbc"], w=[xsbkey])

        def norm_tp_g(xsb, xsbkey, dst3, dstkey):
            for kc in range(KC):
                b = kc // 8
                tp(pb[b][:, (kc % 8) * P:(kc % 8 + 1) * P], xsb[:, kc * P:(kc + 1) * P], identb,
                   r=[xsbkey] + CONST, w=[PB(b)], sig=(kc % 8 == 7))
            T.op("act", lambda e: e.copy(out=dst3[:, 0:8, :], in_=pb[0][:].rearrange("p (k t) -> p k t", k=8)),
                 r=[PB(0)], w=[dstkey])
            T.op("dve", lambda e: e.tensor_copy(out=dst3[:, 8:16, :], in_=pb[1][:].rearrange("p (k t) -> p k t", k=8)),
                 r=[PB(1)], w=[dstkey])

        def norm_tile(xap, xkey, xsb, xsbkey, sti, dst3, dstkey, junk, junkkey):
            norm_pre_g(xap, xkey, xsb, xsbkey, sti)
            norm_tp_g(xsb, xsbkey, dst3, dstkey)

        s123 = ExitStack()
        fst = sb(s123, "fst", [P, 9, 8])
        call = sb(s123, "call", [P, 8, 8])
        lfS = sb(s123, "lfS", [P, 8])
        ktS = sb(s123, "ktS", [P, 8, P], BF16)
        vS = sb(s123, "vS", [P, 8, 129], BF16)
        cpast = sb(s123, "cpast", [P, 16, 32])
        cnewS = sb(s123, "cnewS", [P, 8])
        biasNew = sb(s123, "biasNew", [P, 8])
        cendbc = sb(s123, "cendbc", [P, 32])
        biasP = sb(s123, "biasP", [P, 32, 16])
        cendP = sb(s123, "cendP", [P, 2, 8])
        cm = sb(s123, "cm", [P, NSLOT, 8])
        biasA = sb(s123, "biasA", [P, 2, NSLOT, 8])
        accc = sb(s123, "accc", [P, TO])
        s12 = ExitStack()
        xnT = sb(s12, "xnT", [P, KC, TX], BF16)
        with ExitStack() as s1:
            xst = [sb(s1, "xst%d" % i, [P, D]) for i in range(2)]
            xsbt = [sb(s1, "xsb%d" % i, [P, D], BF16) for i in range(2)]
            kst = [sb(s1, "kst%d" % i, [P, 1024]) for i in range(2)]
            vst = [sb(s1, "vst%d" % i, [P, 1024]) for i in range(2)]
            kbfs = [sb(s1, "kbf%d" % i, [P, 1024], BF16) for i in range(2)]
            vbf1 = sb(s1, "vbf0", [P, 1024], BF16)
            vbf = [vbf1, vbf1]
            ktst = [sb(s1, "ktst%d" % i, [P, 8, P], BF16) for i in range(2)]
            lft = sb(s1, "lft", [P, 4, 8])
            T.op("pool", lambda e: e.memset(vS[:], 1.0), w=["vS"])
            cmkb = sb(s1, "cmkb", [P, 8, 512], BF16)
            mkTs0 = sb(s1, "mkTs0", [P, 4, 4, 256], BF16)
            T.dma("pool", cmkb[:], cmk.rearrange("s (mt p) c -> p (s mt) c", p=P), w=["cmkb"], key="cmkb")

            load_gbc("g_mix")

            tiles = [("own", i) for i in range(8)] + [("smp", 8), ("halo", 0)]

            def xsrc(kind, i):
                if kind == "halo":
                    return x_halo
                return x_own[i]

            wk0, wk0k = wview(0)
            wk1, wk1k = wview(1)
            wv0, wv0k = wview(2)
            wv1, wv1k = wview(3)
            wkv = [(wk0, wk0k), (wk1, wk1k), (wv0, wv0k), (wv1, wv1k)]

            T.dma("sp", xst[0][:], xsrc(*tiles[0]), w=["xst0"], key="xst0")

            NT = len(tiles)

            def tinfo(n):
                kind, i = tiles[n]
                a = n % 2
                col = {"own": i * P, "smp": 1024, "halo": TO}[kind]
                dst3, dk = xnT[:, :, col:col + P], "xnT"
                xT = lambda kc, col=col: xnT[:, kc, col:col + P]
                return kind, i, a, dst3, dk, xT

            def norm_pre(n):
                a = n % 2
                sk = "stt%d" % a
                xk, xbk = "xst%d" % a, "xsb%d" % a
                T.op("dve", lambda e: e.memset(stt[:, a, 0:4], 0.0), w=[sk])
                T.op("act", lambda e: e.activation(out=xsbt[a][:], in_=xst[a][:], func=AF.Square,
                                                   accum_out=stt[:, a, 0:1]), r=[xk], w=[sk, xbk])
                rstd_from(stt[:, a, 0:1], stt[:, a, 2:3], stt[:, a, 1:2], 1.0 / D, [sk])
                T.op("dve", lambda e: e.scalar_tensor_tensor(out=xsbt[a][:], in0=xst[a][:], scalar=stt[:, a, 2:3],
                                                             in1=gbc[:], op0=ALU.mult, op1=ALU.mult),
                     r=[xk, sk, "gbc"], w=[xbk])

            def norm_tp(n):
                kind, i, a, dst3, dk, xT = tinfo(n)
                xbk = "xsb%d" % a
                for kc in range(KC):
                    b = kc // 8
                    tp(pb[b][:, (kc % 8) * P:(kc % 8 + 1) * P], xsbt[a][:, kc * P:(kc + 1) * P], identb,
                       r=[xbk] + CONST, w=[PB(b)], sig=(kc % 8 == 7))
                T.op("act", lambda e: e.copy(out=dst3[:, 0:8, :], in_=pb[0][:].rearrange("p (k t) -> p k t", k=8)),
                     r=[PB(0)], w=[dk])
                T.op("dve", lambda e: e.tensor_copy(out=dst3[:, 8:16, :],
                                                    in_=pb[1][:].rearrange("p (k t) -> p k t", k=8)),
                     r=[PB(1)], w=[dk])

            def lf_of(n):
                kind, i = tiles[n]
                if kind in ("own", "smp"):
                    return fst[:, i, :], "fst"
                return lft[:, n % 4, :], "lft%d" % (n % 4)

            def kbanks(n):
                return (0, 1) if n % 2 == 0 else (2, 3)

            def mainK(n):
                kind, i, a, dst3, dk, xT = tinfo(n)
                for j, (wt, wkk) in enumerate(wkv[0:2]):
                    b = kbanks(n)[j]
                    for kc in range(KC):
                        mm(ps[b][:], xT(kc), wt[:, kc, :], kc == 0, kc == KC - 1, r=[dk, wkk], w=[PS(b)])
                for kc in range(KC):
                    mm(ps[4][:, 0:8], xT(kc), wf[:, kc, :], kc == 0, kc == KC - 1, r=[dk] + CONST, w=[PS(4)])

            def mainV(n):
                kind, i, a, dst3, dk, xT = tinfo(n)
                for j, (wt, wkk) in enumerate(wkv[2:4]):
                    b = kbanks(n)[j]
                    for kc in range(KC):
                        mm(ps[b][:], xT(kc), wt[:, kc, :], kc == 0, kc == KC - 1, r=[dk, wkk], w=[PS(b)])

            def evacK(n):
                kind, i, a, dst3, dk, xT = tinfo(n)
                own = kind in ("own", "smp")
                ka = "kst%d" % a
                b0, b1 = kbanks(n)
                T.op("act", lambda e: e.copy(out=kst[a][:, 0:512], in_=ps[b0][:]), r=[PS(b0)], w=[ka])
                T.op("act", lambda e: e.copy(out=kst[a][:, 512:1024], in_=ps[b1][:]), r=[PS(b1)], w=[ka])
                T.op("dve", lambda e: e.tensor_copy(out=kbfs[a][:], in_=kst[a][:]), r=[ka], w=["kbf%d" % a])
                if own:
                    T.dma("sp", k_new[i], kst[a][:], r=[ka], key=ka)
                lf, lfk = lf_of(n)
                T.op("dve", lambda e: e.tensor_tensor(out=lf, in0=ps[4][:, 0:8], in1=bfbc, op=ALU.add),
                     r=[PS(4)] + CONST, w=[lfk])
                T.op("act", lambda e: e.activation(out=lf, in_=lf, func=AF.Exp, scale=-1.0), r=[lfk], w=[lfk])
                T.op("act", lambda e: e.activation(out=lf, in_=lf, func=AF.Ln, bias=1.0), r=[lfk], w=[lfk])
                T.op("dve", lambda e: e.tensor_scalar(out=lf, in0=lf, scalar1=-1.0, scalar2=None, op0=ALU.mult),
                     r=[lfk], w=[lfk])
                if kind == "smp":
                    T.op("dve", lambda e: e.tensor_copy(out=lfS[:], in_=lf), r=[lfk], w=["lfS"])

            def evacV(n):
                kind, i, a, dst3, dk, xT = tinfo(n)
                own = kind in ("own", "smp")
                va = "vst%d" % a
                b0, b1 = kbanks(n)
                T.op("act", lambda e: e.copy(out=vst[a][:, 0:512], in_=ps[b0][:]), r=[PS(b0)], w=[va])
                T.op("act", lambda e: e.copy(out=vst[a][:, 512:1024], in_=ps[b1][:]), r=[PS(b1)], w=[va])
                if own:
                    T.dma("sp", v_new[i], vst[a][:], r=[va], key=va)
                if kind == "smp":
                    T.op("dve", lambda e: e.tensor_copy(out=vS[:, :, 0:128],
                                                         in_=vst[a][:].rearrange("p (h d) -> p h d", h=8)),
                         r=[va], w=["vS"])
                else:
                    vk = "vbf0"
                    T.op("dve", lambda e: e.tensor_copy(out=vbf[a][:], in_=vst[a][:]), r=[va], w=[vk])
                    for q in range(2):
                        T.dma("sp", vloc[q][i * P:(i + 1) * P, :], vbf[a][:, q * 512:(q + 1) * 512], r=[vk],
                              w=["kvloc"], key=vk)

            def tail(n):
                kind, i = tiles[n]
                a = n % 2
                for h in range(8):
                    tp(pb[2][:, h * P:(h + 1) * P], kbfs[a][:, h * P:(h + 1) * P], identb,
                       r=["kbf%d" % a] + CONST, w=[PB(2)], sig=(h == 7))
                if kind == "smp":
                    T.op("act", lambda e: e.copy(out=ktS[:], in_=pb[2][:].rearrange("p (k t) -> p k t", k=8)),
                         r=[PB(2)], w=["ktS"])
                    return
                sl = i
                kk = "ktst%d" % a
                T.op("act", lambda e: e.copy(out=ktst[a][:], in_=pb[2][:].rearrange("p (k t) -> p k t", k=8)),
                     r=[PB(2)], w=[kk])
                for q in range(2):
                    T.dma("sp", kloc[q].rearrange("(h d) t -> d h t", h=4)[:, :, sl * P:(sl + 1) * P],
                          ktst[a][:, 4 * q:4 * q + 4, :], r=[kk], w=["kvloc"], key=kk)
                lf, lfk = lf_of(n)
                mm(ps[4][:, 8:16], Umat, lf, True, sl == 0, r=[lfk] + CONST, w=[PS(4)])
                if sl > 0:
                    mm(ps[4][:, 8:16], E127, call[:, sl - 1, :], False, True, r=["call"] + CONST, w=[PS(4)])
                T.op("act", lambda e: e.copy(out=call[:, sl, :], in_=ps[4][:, 8:16]), r=[PS(4)], w=["call"])

            T.dma("sp", xst[1][:], xsrc(*tiles[1]), w=["xst1"], key="xst1")
            lp = sb(s1, "lp", [P, 16, 32])

            def cpast_step(kt):
                mm(ps[4][:, 16:48], Umat, lp[:, kt, :], True, kt == 0, r=["lp"] + CONST, w=[PS(4)])
                if kt > 0:
                    mm(ps[4][:, 16:48], E127, cpast[:, kt - 1, :], False, True, r=["cpast"] + CONST, w=[PS(4)])
                T.op("act", lambda e: e.copy(out=cpast[:, kt, :], in_=ps[4][:, 16:48]), r=[PS(4)], w=["cpast"])

            norm_pre(0)
            T.dma("sp", xst[0][:], xsrc(*tiles[2]), w=["xst0"], key="xst0")
            norm_pre(1)
            T.dma("sp", xst[1][:], xsrc(*tiles[3]), w=["xst1"], key="xst1")
            norm_tp(0)
            for n in range(NT):
                kind = tiles[n][0]
                if n + 2 < NT:
                    norm_pre(n + 2)
                    if n + 4 < NT:
                        T.dma("sp", xst[n % 2][:], xsrc(*tiles[n + 4]), w=["xst%d" % (n % 2)], key="xst%d" % (n % 2))
                if n + 1 < NT:
                    norm_tp(n + 1)
                if n == 0:
                    for s2 in range(4):
                        T.dma("sp", lp[:, :, s2 * 8:(s2 + 1) * 8], clogf[s2].rearrange("(kt p) h -> p kt h", p=P),
                              w=["lp"], key="lp")
                if kind != "halo":
                    mainK(n)
                if n == 8:
                    wrel(0)
                    wrel(1)
                if kind != "halo":
                    evacK(n)
                if n >= 1 and tiles[n - 1][0] != "halo":
                    tail(n - 1)
                if 2 <= n <= 9:
                    cpast_step(2 * (n - 2))
                    cpast_step(2 * (n - 2) + 1)
                if n == 3:
                    for s2 in range(4):
                        for mt in range(2):
                            for h in range(4):
                                j = mt * 4 + h
                                tp(pb[2][:, j * P:(j + 1) * P], cmkb[:, s2 * 2 + mt, h * P:(h + 1) * P], identb,
                                   r=["cmkb"] + CONST, w=[PB(2)], sig=(j == 7))
                        T.op("act", lambda e: e.copy(out=mkTs0[:, s2, :, :].rearrange("p h (mt t) -> p mt h t", mt=2),
                                                     in_=pb[2][:].rearrange("p (mt h t) -> p mt h t", mt=2, h=4)),
                             r=[PB(2)], w=["mkTs0"])
                    T.dma("sp", mkTs_d, mkTs0[:].rearrange("p s h t -> p (s h t)"), r=["mkTs0"], w=["mkTs_d"],
                          key="mkTs0")
                if n == 7:
                    T.dma("pool", cmvb_d, cmv.rearrange("s (mt p) c -> p (s mt) c", p=P), r=["kbf1"], w=["cmvb_d"],
                          key="cmvb_d")
                if 3 <= n < 7:
                    T.dma("pool", w2s[n - 3].rearrange("p (kc n) -> p kc n", kc=16), WSEQ[8 + n - 3][0],
                          r=["kbf%d" % (n % 2)], w=["w2s"], key="w2s")
            for n in range(NT - 1):
                mainV(n)
                if n == 8:
                    wrel(2)
                    wrel(3)
                evacV(n)
            T.dma("sp", f_new, fst[:], r=["fst"], key="fst")
            T.dma("sp", floc.rearrange("(i p) h -> p i h", p=P), fst[:, 0:8, :], r=["fst"], w=["floc"], key="fst")

            mm(ps[3][:, 0:8], Ublk, lfS[:], True, False, r=["lfS"] + CONST, w=[PS(3)], sig=False)
            for s2 in range(4):
                mm(ps[3][:, 0:8], Epast(s2), cpast[:, 15, s2 * 8:(s2 + 1) * 8], False, s2 == 3,
                   r=["cpast"] + CONST, w=[PS(3)])
            T.op("act", lambda e: e.copy(out=cnewS[:], in_=ps[3][:, 0:8]), r=[PS(3)], w=["cnewS"])
            mm(ps[3][:, 0:8], EblkEnd, cnewS[:], True, True, r=["cnewS"] + CONST, w=[PS(3)])
            T.op("dve", lambda e: e.tensor_tensor(out=biasNew[:], in0=ps[3][:, 0:8], in1=cnewS[:], op=ALU.subtract),
                 r=[PS(3), "cnewS"], w=["biasNew"])
            for s2 in range(4):
                mm(ps[4][:, s2 * 8:(s2 + 1) * 8], Esel(s2), cnewS[:], True, True, r=["cnewS"] + CONST, w=[PS(4)])
            T.op("act", lambda e: e.copy(out=cendbc[:], in_=ps[4][:, 0:32]), r=[PS(4)], w=["cendbc"])
            T.op("dve", lambda e: e.tensor_tensor(out=biasP[:], in0=cendbc[:].unsqueeze(2).to_broadcast([P, 32, 16]),
                                                  in1=cpast[:].rearrange("p kt c -> p c kt"), op=ALU.subtract),
                 r=["cendbc", "cpast"], w=["biasP"])
            T.barrier()
        alloc_psum(6, 2)
        for q in range(2):
            T.coll("AllGather", kloc_t[q].ap().opt(), kall_t[q].ap().opt(), GROUPS, r=["kvloc"], w=["kvall"], key="cc")
            T.coll("AllGather", vloc_t[q].ap().opt(), vall_t[q].ap().opt(), GROUPS, r=["kvloc"], w=["kvall"], key="cc")
        T.coll("AllGather", floc_t.ap().opt(), fall_t.ap().opt(), GROUPS, r=["floc"], w=["fall"], key="cc")
        T.lw["kvall"] = T.lw["fall"]
        with ExitStack() as s1:
            kst = [sb(s1, "cst%d" % i, [P, 1024]) for i in range(2)]
            qs = [sb(s1, "qs%d" % i, [P, TO], BF16) for i in range(2)]
            ycs = [sb(s1, "ycs%d" % i, [P, TO], BF16) for i in range(2)]
            gcs = sb(s1, "gcs", [P, TO + 2])
            Ub = sb(s1, "Ub", [P, 1026])
            Us = sb(s1, "Us", [P, 4, 34])
            tcv = sb(s1, "tcv", [P, TO])
            yc = sb(s1, "yc", [P, TO])
            sqt = sb(s1, "sqt", [P, TO])
            ulast = sb(s1, "ulast", [P, 8, 10])
            cconvT = sb(s1, "cconvT", [P, 8, 8])

            T.dma("sp", kst[0][0:8, :], cconv, w=["kst0"], key="cst0")
            xm = sb(s1, "xm", [P, D])
            xmb = sb(s1, "xmb", [P, D], BF16)
            memT = sb(s1, "memT", [P, KC, 256], BF16)
            mkb = sb(s1, "mkb", [P, 512], BF16)
            load_gbc("g_mem")
            T.dma("sp", xm[:], mem[0:P, :], w=["xm"], key="xm")
            groups = ((0, 512), (512, 512), (1024, 130))
            for half in range(2):
                base = 4 + 3 * half
                if half == 0:
                    wgb, wgbk = wview(base)
                    wgc, wgck = wview(base + 1)
                    whin, whink = wview(base + 2)
                else:
                    wgc, wgck = wview(base)
                    whin, whink = wview(base + 1)
                    wgb, wgbk = wview(base + 2)
                for sc in range(4):
                    cc = 4 * half + sc
                    w0 = cwT[:, cc * 3 + 0:cc * 3 + 1]
                    w1 = cwT[:, cc * 3 + 1:cc * 3 + 2]
                    w2 = cwT[:, cc * 3 + 2:cc * 3 + 3]
                    bA, bB = ((0, 1, 2), (3, 4, 5)) if cc % 2 == 0 else ((3, 4, 5), (0, 1, 2))
                    for which, wt, wkk, banks in (("gc", wgc, wgck, bA), ("hin", whin, whink, bB),
                                                  ("gb", wgb, wgbk, bA)):
                        for kc in range(KC):
                            for gi, (c0, nn) in enumerate(groups):
                                mm(ps[banks[gi]][:, 0:nn], wt[:, kc, sc * P:(sc + 1) * P], xnT[:, kc, c0:c0 + nn],
                                   kc == 0, kc == KC - 1, r=[wkk, "xnT"], w=[PS(banks[gi])])
                        if which == "gc" and cc == 0:
                            for c8 in range(8):
                                tp(ps[5][:, c8 * 8:(c8 + 1) * 8], kst[0][0:8, c8 * P:(c8 + 1) * P], cf[0:8, 0, 0:8],
                                   r=["kst0"] + CONST, w=[PS(5)], sig=(c8 == 7))
                            T.op("act", lambda e: e.copy(out=cconvT[:],
                                                         in_=ps[5][:, 0:64].rearrange("p (c k) -> p c k", c=8)),
                                 r=[PS(5)], w=["cconvT"])
                        if which == "gc":
                            for gi, (c0, nn) in enumerate(groups):
                                T.op("act", lambda e: e.copy(out=gcs[:, c0:c0 + nn], in_=ps[banks[gi]][:, 0:nn]),
                                     r=[PS(banks[gi])], w=["gcs"])
                        elif which == "hin":
                            T.op("dve", lambda e: e.tensor_tensor(out=Ub[:, 2:514], in0=ps[banks[0]][:, 0:512],
                                                                  in1=gcs[:, 0:512], op=ALU.mult),
                                 r=[PS(banks[0]), "gcs"], w=["Ub"])
                            T.op("dve", lambda e: e.tensor_tensor(out=Ub[:, 514:1026], in0=ps[banks[1]][:, 0:512],
                                                                  in1=gcs[:, 512:1024], op=ALU.mult),
                                 r=[PS(banks[1]), "gcs"], w=["Ub"])
                            T.op("dve", lambda e: e.tensor_tensor(
                                out=Us[:, :, 2:34], in0=ps[banks[2]][:, 0:128].rearrange("p (s t) -> p s t", s=4),
                                in1=gcs[:, 1024:1152].rearrange("p (s t) -> p s t", s=4), op=ALU.mult),
                                r=[PS(banks[2]), "gcs"], w=["Us"])
                            T.op("dve", lambda e: e.tensor_tensor(out=Ub[:, 0:2], in0=ps[banks[2]][:, 128:130],
                                                                  in1=gcs[:, 1152:1154], op=ALU.mult),
                                 r=[PS(banks[2]), "gcs"], w=["Ub"])
                            T.op("dve", lambda e: e.tensor_copy(
                                out=Us[:, :, 0:2], in_=cconvT[:, cc, :].rearrange("p (s r) -> p s r", s=4)),
                                r=["cconvT"], w=["Us"])
                            T.op("act", lambda e: e.mul(out=tcv[:, 0:1024], in_=Ub[:, 0:1024], mul=w0),
                                 r=["Ub"] + CONST, w=["tcv"])
                            T.op("dve", lambda e: e.scalar_tensor_tensor(out=tcv[:, 0:1024], in0=Ub[:, 1:1025],
                                                                         scalar=w1, in1=tcv[:, 0:1024],
                                                                         op0=ALU.mult, op1=ALU.add),
                                 r=["Ub", "tcv"] + CONST, w=["tcv"])
                            T.op("dve", lambda e: e.scalar_tensor_tensor(out=tcv[:, 0:1024], in0=Ub[:, 2:1026],
                                                                         scalar=w2, in1=tcv[:, 0:1024],
                                                                         op0=ALU.mult, op1=ALU.add),
                                 r=["Ub", "tcv"] + CONST, w=["tcv"])
                            tS = tcv[:, 1024:1152].rearrange("p (s t) -> p s t", s=4)
                            T.op("act", lambda e: e.mul(out=tS, in_=Us[:, :, 0:32], mul=w0),
                                 r=["Us"] + CONST, w=["tcv"])
                            T.op("dve", lambda e: e.scalar_tensor_tensor(out=tS, in0=Us[:, :, 1:33], scalar=w1,
                                                                         in1=tS, op0=ALU.mult, op1=ALU.add),
                                 r=["Us", "tcv"] + CONST, w=["tcv"])
                            T.op("dve", lambda e: e.scalar_tensor_tensor(out=tS, in0=Us[:, :, 2:34], scalar=w2,
                                                                         in1=tS, op0=ALU.mult, op1=ALU.add),
                                 r=["Us", "tcv"] + CONST, w=["tcv"])
                            T.op("dve", lambda e: e.tensor_copy(out=ulast[:, cc, 0:2], in_=Ub[:, 1024:1026]),
                                 r=["Ub"], w=["ulast"])
                            T.op("dve", lambda e: e.tensor_copy(
                                out=ulast[:, cc, 2:10].rearrange("p (s r) -> p s r", s=4), in_=Us[:, :, 32:34]),
                                r=["Us"], w=["ulast"])
                        else:
                            for gi, (c0, nn) in enumerate(groups):
                                n2 = min(nn, 512 if gi < 2 else 128)
                                T.op("dve", lambda e: e.tensor_tensor(out=yc[:, c0:c0 + n2], in0=ps[banks[gi]][:, 0:n2],
                                                                      in1=tcv[:, c0:c0 + n2], op=ALU.mult),
                                     r=[PS(banks[gi]), "tcv"], w=["yc"])
                            if cc == 0:
                                T.op("act", lambda e: e.activation(out=accc[:], in_=yc[:], func=AF.Square),
                                     r=["yc"], w=["accc"])
                            else:
                                T.op("act", lambda e: e.activation(out=sqt[:], in_=yc[:], func=AF.Square),
                                     r=["yc"], w=["sqt"])
                                T.op("dve", lambda e: e.tensor_tensor(out=accc[:], in0=accc[:], in1=sqt[:], op=ALU.add),
                                     r=["sqt", "accc"], w=["accc"])
                            yk = "ycs%d" % (cc % 2)
                            T.op("act", lambda e: e.mul(out=ycs[cc % 2][:], in_=yc[:], mul=gcT[:, cc:cc + 1]),
                                 r=["yc"] + CONST, w=[yk])
                            T.dma("sp", ycg_d[cc], ycs[cc % 2][:], r=[yk], w=["ycg_d"], key=yk)
                for k in range(3):
                    wrel(base + k)

            mkT0 = ycs[0][:, 0:1024].rearrange("p (h t) -> p h t", h=4)
            mvb0 = ycs[1][:, 0:1024].rearrange("p (m c) -> p m c", m=2)
            norm_pre_g(xm[:], "xm", xmb[:], "xmb", 0)
            for qi in range(2):
                wq, wqk = wview(10 + qi)
                for hh in range(4):
                    h = 4 * qi + hh
                    if h == 2:
                        norm_tp_g(xmb[:], "xmb", memT[:, :, 0:P], "memT")
                        T.dma("sp", xm[:], mem[P:2 * P, :], w=["xm"], key="xm")
                        norm_pre_g(xm[:], "xm", xmb[:], "xmb", 1)
                    if h == 5:
                        norm_tp_g(xmb[:], "xmb", memT[:, :, P:2 * P], "memT")
                    banks = (0, 1, 2) if h % 2 == 0 else (3, 4, 5)
                    for kc in range(KC):
                        for gi, (c0, nn) in enumerate(((0, 512), (512, 512), (1024, 128))):
                            mm(ps[banks[gi]][:, 0:nn], wq[:, kc, hh * P:(hh + 1) * P], xnT[:, kc, c0:c0 + nn],
                               kc == 0, kc == KC - 1, r=[wqk, "xnT"], w=[PS(banks[gi])])
                    for gi, (c0, nn) in enumerate(((0, 512), (512, 512), (1024, 128))):
                        if h % 2 == 0:
                            T.op("act", lambda e: e.copy(out=qs[0][:, c0:c0 + nn], in_=ps[banks[gi]][:, 0:nn]),
                                 r=[PS(banks[gi])], w=["qs0"])
                        else:
                            T.op("dve", lambda e: e.tensor_copy(out=qs[1][:, c0:c0 + nn], in_=ps[banks[gi]][:, 0:nn]),
                                 r=[PS(banks[gi])], w=["qs1"])
                    T.dma("sp", qT_d[h], qs[h % 2][:], r=["qs%d" % (h % 2)], w=["qT_d"], key="qs%d" % (h % 2))
                wrel(10 + qi)
            wkk_, wkkk = wview(12)
            wvv_, wvvk = wview(13)
            for mt in range(2):
                for kc in range(KC):
                    mm(ps[0][:], memT[:, kc, mt * P:(mt + 1) * P], wkk_[:, kc, :], kc == 0, kc == KC - 1,
                       r=["memT", wkkk], w=[PS(0)])
                for kc in range(KC):
                    mm(ps[1][:], memT[:, kc, mt * P:(mt + 1) * P], wvv_[:, kc, :], kc == 0, kc == KC - 1,
                       r=["memT", wvvk], w=[PS(1)])
                T.op("act", lambda e: e.copy(out=sqt[:, 0:512], in_=ps[0][:]), r=[PS(0)], w=["sqt"])
                T.op("act", lambda e: e.copy(out=yc[:, 0:512], in_=ps[1][:]), r=[PS(1)], w=["yc"])
                T.dma("sp", mk_new[mt * P:(mt + 1) * P, :], sqt[:, 0:512], r=["sqt"], key="sqt")
                T.dma("sp", mv_new[mt * P:(mt + 1) * P, :], yc[:, 0:512], r=["yc"], key="yc")
                T.op("dve", lambda e: e.tensor_copy(out=mkb[:], in_=sqt[:, 0:512]), r=["sqt"], w=["mkb"])
                T.op("dve", lambda e: e.tensor_copy(out=mvb0[:, mt, :], in_=yc[:, 0:512]), r=["yc"], w=["ycs1"])
                for h in range(4):
                    tp(pb[0][:, h * P:(h + 1) * P], mkb[:, h * P:(h + 1) * P], identb, r=["mkb"] + CONST, w=[PB(0)],
                       sig=(h == 3))
                T.op("act", lambda e: e.copy(out=mkT0[:, :, mt * P:(mt + 1) * P],
                                             in_=pb[0][:, 0:512].rearrange("p (h t) -> p h t", h=4)),
                     r=[PB(0)], w=["ycs0"])
            wrel(12)
            wrel(13)
            T.dma("sp", mkT_d, ycs[0][:, 0:1024], r=["ycs0"], w=["mkT_d"], key="ycs0")
            T.dma("sp", mvb_d, ycs[1][:, 0:1024], r=["ycs1"], w=["mvb_d"], key="ycs1")

            for cc in range(8):
                b = cc // 4
                tp(ps[b][0:10, (cc % 4) * P:(cc % 4 + 1) * P], ulast[:, cc, :], identf,
                   r=["ulast"] + CONST, w=[PS(b)], sig=(cc % 4 == 3))
            T.op("act", lambda e: e.copy(out=kst[1][0:10, 0:512], in_=ps[0][0:10, :]), r=[PS(0)], w=["kst1"])
            T.op("act", lambda e: e.copy(out=kst[1][0:10, 512:1024], in_=ps[1][0:10, :]), r=[PS(1)], w=["kst1"])
            T.dma("sp", conv_new, kst[1][0:10, :], r=["kst1"], key="cst1")

            T.barrier()
        s12.close()

        with ExitStack() as s3:
            kTh = [sb(s3, "kTh%d" % i, [P, NSLOT * P], BF16) for i in range(2)]
            vh = [sb(s3, "vh%d" % i, [P, NSLOT, P], BF16) for i in range(2)]
            PT = [sb(s3, "PT%d" % i, [P, 512], BF16) for i in range(4)]
            rlt = sb(s3, "rlt", [P, 512])
            yat = sb(s3, "yat", [P, 512])
            sq3 = sb(s3, "sq3", [P, 512])
            acca = sb(s3, "acca", [P, 1024])
            kp = [wr[1][:, i * 2048:(i + 1) * 2048].rearrange("p (k d) -> p k d", k=16) for i in range(2)]
            vp = [sb(s3, "vp%d" % i, [P, 16, 129], BF16) for i in range(2)]
            kpT = [wr[1][:, 4096 + i * 2048:4096 + (i + 1) * 2048] for i in range(2)]
            fl = sb(s3, "fl", [P, NCTX, 8])
            flT = sb(s3, "flT", [NCTX, 1024])
            Lc = sb(s3, "Lc", [P, NCTX, 8])
            Bt = sb(s3, "Bt", [P, NCTX, 8])
            Cex = sb(s3, "Cex", [P, NCTX, 8])
            tmpv = sb(s3, "tmpv", [P, NCTX, 8])
            carry = sb(s3, "carry", [P, 8])
            PTp = [sb(s3, "PTp%d" % i, [P, 16, P], BF16) for i in range(4)]
            PTn = sb(s3, "PTn", [P, P], BF16)
            tmpS = sb(s3, "tmpS", [P, 512])
            yas = sb(s3, "yas", [P, 1024])
            yasb = sb(s3, "yasb", [P, 1024], BF16)
            qh = [sb(s3, "qh%d" % i, [P, TO], BF16) for i in range(2)]
            yags = [sb(s3, "yags%d" % i, [P, 1024], BF16) for i in range(2)]
            yagS = sb(s3, "yagS", [P, 8, P], BF16)
            for i in range(2):
                T.op("dve", lambda e: e.memset(vp[i][:], 1.0), w=["vp%d" % i])
            for i in range(4):
                T.op("act" if i % 2 else "dve", lambda e: e.memzero(PTp[i][:]) if i % 2 else e.memset(PTp[i][:], 0.0),
                     w=["PTp%d" % i])

            ck_v = ck.rearrange("s (kt p) (h d) -> s p kt h d", p=P, h=8)
            cv_v = cv.rearrange("s (kt p) (h d) -> s p kt h d", p=P, h=8)

            def load_head(h):
                a = h % 2
                q, hh = h // 4, h % 4
                kk, vk = "kTh%d" % a, "vh%d" % a
                T.dma("sp", kTh[a][:, 0:3072].rearrange("p (r t) -> p r t", r=3),
                      kall[q].rearrange("(r x) t -> x r t", r=4)[hh * P:(hh + 1) * P, 0:3, :], r=["kvall"], w=[kk], key=kk)
                T.dma("sp", kTh[a][:, 4096:5120], kloc[q][hh * P:(hh + 1) * P, :], r=["kvloc"], w=[kk], key=kk)
                T.dma("sp", vh[a][:, 0:NATT, :], vall[q][0:NATT * P, hh * P:(hh + 1) * P].rearrange("(s t) d -> t s d", t=P),
                      r=["kvall"], w=[vk], key=vk)
                T.dma("sp", vh[a][:, NCTX:NSLOT, :], vloc[q][:, hh * P:(hh + 1) * P].rearrange("(s t) d -> t s d", t=P),
                      r=["kvloc"], w=[vk], key=vk)
                T.dma("sp", qh[a][:], qT_d[h], r=["qT_d"], w=["qh%d" % a], key="qh%d" % a)

            def load_pastK(it):
                h, s2 = it // 4, it % 4
                a = it % 2
                T.dma("pool", kp[a][:], ck_v[s2][:, :, h, :], w=["kp%d" % a], key="kp%d" % a)

            def load_pastV(it):
                h, s2 = it // 4, it % 4
                a = it % 2
                T.dma("pool", vp[a][:, :, 0:128], cv_v[s2][:, :, h, :], w=["vp%d" % a], key="vp%d" % a)

            T.dma("sp", flT[:], fall.rearrange("(s p) h -> s (p h)", p=P), r=["fall"], w=["flT"], key="flT")
            for h in range(8):
                tp(ps[0][:, h * 32:(h + 1) * 32], flT[:].rearrange("s (p h) -> s h p", h=8)[:, h, :], cf[0:32, 0, 0:32],
                   r=["flT"] + CONST, w=[PS(0)], sig=(h == 7))
            T.op("act", lambda e: e.copy(out=fl[:].rearrange("p s h -> p h s"),
                                         in_=ps[0][:, 0:256].rearrange("p (h s) -> p h s", h=8)),
                 r=[PS(0)], w=["fl"])
            load_head(0)
            load_pastK(0)
            load_pastV(0)
            load_pastK(1)
            load_pastV(1)
            mm(ps[0][:, 0:256], Umat, fl[:].rearrange("p s h -> p (s h)"), True, True, r=["fl"] + CONST, w=[PS(0)])
            T.op("act", lambda e: e.copy(out=Lc[:].rearrange("p s h -> p (s h)"), in_=ps[0][:, 0:256]),
                 r=[PS(0)], w=["Lc"])
            mm(ps[1][:, 0:256], E127, Lc[:].rearrange("p s h -> p (s h)"), True, True, r=["Lc"] + CONST, w=[PS(1)])
            T.op("act", lambda e: e.copy(out=Bt[:].rearrange("p s h -> p (s h)"), in_=ps[1][:, 0:256]),
                 r=[PS(1)], w=["Bt"])
            T.op("dve", lambda e: e.memset(Cex[:, 0, :], 0.0), w=["Cex"])
            T.op("dve", lambda e: e.tensor_copy(out=Cex[:, 1:NCTX, :], in_=Bt[:, 0:NCTX - 1, :]), r=["Bt"], w=["Cex"])
            cur, oth, ck_, ok_ = Cex, tmpv, "Cex", "tmpv"
            for k in (1, 2, 4, 8, 16):
                T.op("dve", lambda e: e.tensor_copy(out=oth[:, 0:k, :], in_=cur[:, 0:k, :]), r=[ck_], w=[ok_])
                T.op("dve", lambda e: e.tensor_tensor(out=oth[:, k:NCTX, :], in0=cur[:, k:NCTX, :],
                                                      in1=cur[:, 0:NCTX - k, :], op=ALU.add), r=[ck_], w=[ok_])
                cur, oth, ck_, ok_ = oth, cur, ok_, ck_
            if cur is not Cex:
                T.op("dve", lambda e: e.tensor_copy(out=Cex[:], in_=cur[:]), r=[ck_], w=["Cex"])
            T.op("dve", lambda e: e.tensor_tensor(out=cm[:, 0:NCTX, :], in0=Lc[:], in1=Cex[:], op=ALU.add),
                 r=["Lc", "Cex"], w=["cm"])
            T.op("dve", lambda e: e.tensor_tensor(out=cm[:, 0:NCTX, :], in0=cm[:, 0:NCTX, :],
                                                  in1=kmask.unsqueeze(2).to_broadcast([P, NCTX, 8]), op=ALU.add),
                 r=["cm"] + CONST, w=["cm"])
            T.op("dve", lambda e: e.tensor_tensor(out=tmpv[:], in0=Bt[:],
                                                  in1=valid.unsqueeze(2).to_broadcast([P, NCTX, 8]), op=ALU.mult),
                 r=["Bt"] + CONST, w=["tmpv"])
            T.op("dve", lambda e: e.tensor_reduce(out=carry[:], in_=tmpv[:].rearrange("p s h -> p h s"),
                                                  axis=mybir.AxisListType.X, op=ALU.add), r=["tmpv"], w=["carry"])
            T.op("dve", lambda e: e.tensor_tensor(out=cm[:, NCTX:NSLOT, :], in0=call[:],
                                                  in1=carry[:].unsqueeze(1).to_broadcast([P, 8, 8]), op=ALU.add),
                 r=["call", "carry"], w=["cm"])
            for g in range(2):
                mm(ps[2][:, 0:8], E127, cm[:, NCTX + 4 * g + 3, :], True, True, r=["cm"] + CONST, w=[PS(2)])
                T.op("act", lambda e: e.copy(out=cendP[:, g, :], in_=ps[2][:, 0:8]), r=[PS(2)], w=["cendP"])
            for g in range(2):
                T.op("dve", lambda e: e.tensor_tensor(out=biasA[:, g, :, :],
                                                      in0=cendP[:, g, :].unsqueeze(1).to_broadcast([P, NSLOT, 8]),
                                                      in1=cm[:], op=ALU.subtract),
                     r=["cendP", "cm"], w=["biasA"])

            pti = [0]

            def prompt(h, g):
                a = h % 2
                kT_, kTk, v_, vk = kTh[a], "kTh%d" % a, vh[a], "vh%d" % a
                bO, bL = 2 + 2 * g, 3 + 2 * g
                slots = ([(s, 0) for s in range(NATT)] + [(NCTX + kt, 0) for kt in range(4 * g)]
                         + [(NCTX + 4 * g + r, r * P) for r in range(4)])

                sbanks = (0, 1, 4) if g == 0 else (0, 1, 2)

                def emitS(i):
                    sl, off = slots[i]
                    nn = 512 - off
                    bS = sbanks[i % 3]
                    diag = sl >= NCTX + 4 * g
                    mm(ps[bS][:, 0:nn], kT_[:, sl * P:(sl + 1) * P], qh[a][:, g * 512 + off:(g + 1) * 512],
                       True, not diag, r=[kTk, "qh%d" % a], w=[PS(bS)])
                    if diag:
                        mm(ps[bS][:, 0:P], identb, maskd, False, True, r=CONST, w=[PS(bS)])

                emitS(0)
                emitS(1)
                for i, (sl, off) in enumerate(slots):
                    nn = 512 - off
                    bS = sbanks[i % 3]
                    if i + 2 < len(slots):
                        emitS(i + 2)
                    pk = pti[0] % 4
                    pti[0] += 1
                    T.op("act", lambda e: e.activation(out=PT[pk][:, 0:nn], in_=ps[bS][:, 0:nn], func=AF.Exp,
                                                       bias=biasA[:, g, sl, h:h + 1], scale=SCALE),
                         r=[PS(bS), "biasA"], w=["PT%d" % pk])
                    last = i == len(slots) - 1
                    mm(ps[bO][:, off:512], v_[:, sl, :], PT[pk][:, 0:nn], i == 0, last,
                       r=[vk, "PT%d" % pk], w=[PS(bO)])
                    mm(ps[bL][:, off:512], onesb, PT[pk][:, 0:nn], i == 0, last,
                       r=["PT%d" % pk] + CONST, w=[PS(bL)])
                def epi():
                    T.op("dve", lambda e: e.reciprocal(out=rlt[:], in_=ps[bL][:]), r=[PS(bL)], w=["rlt"])
                    T.op("dve", lambda e: e.tensor_tensor(out=yat[:], in0=ps[bO][:], in1=rlt[:], op=ALU.mult),
                         r=[PS(bO), "rlt"], w=["yat"])
                    cs = slice(g * 512, (g + 1) * 512)
                    if h == 0:
                        T.op("pool", lambda e: e.tensor_tensor(out=acca[:, cs], in0=yat[:], in1=yat[:], op=ALU.mult),
                             r=["yat"], w=["acca"])
                    else:
                        T.op("pool", lambda e: e.tensor_tensor(out=sq3[:], in0=yat[:], in1=yat[:], op=ALU.mult),
                             r=["yat"], w=["sq3"])
                        T.op("pool", lambda e: e.tensor_tensor(out=acca[:, cs], in0=acca[:, cs], in1=sq3[:], op=ALU.add),
                             r=["sq3", "acca"], w=["acca"])
                    T.op("dve", lambda e: e.tensor_scalar(out=yags[a][:, cs], in0=yat[:], scalar1=gaT[:, h:h + 1],
                                                          scalar2=None, op0=ALU.mult),
                         r=["yat"] + CONST, w=["yags%d" % a])
                    if g == 1:
                        T.dma("sp", yag_d[h][:, 0:1024], yags[a][:], r=["yags%d" % a], w=["yag_d"], key="yags%d" % a)
                return epi

            def sA(it):
                h, s2 = it // 4, it % 4
                a = h % 2
                pa = it % 2
                for kt in range(16):
                    b = kt // 8
                    tp(pb[b][:, (kt % 8) * P:(kt % 8 + 1) * P], kp[pa][:, kt, :], identb,
                       r=["kp%d" % pa] + CONST, w=[PB(b)], sig=(kt % 8 == 7))
                if it + 2 < 32:
                    load_pastK(it + 2)
                T.op("act", lambda e: e.copy(out=kpT[pa][:, 0:1024], in_=pb[0][:]), r=[PB(0)], w=["kpT%d" % pa])
                T.op("dve", lambda e: e.tensor_copy(out=kpT[pa][:, 1024:2048], in_=pb[1][:]),
                     r=[PB(1)], w=["kpT%d" % pa])
                bS = it % 2
                for kt in range(16):
                    mm(ps[bS][:, kt * 32:(kt + 1) * 32], kpT[pa][:, kt * P:(kt + 1) * P],
                       qh[a][:, 1024 + s2 * 32:1024 + (s2 + 1) * 32], True, True,
                       r=["kpT%d" % pa, "qh%d" % a], w=[PS(bS)], sig=(kt == 15))
                c = s2 * 8 + h
                T.op("dve", lambda e: e.scalar_tensor_tensor(
                    out=tmpS[:].rearrange("p (k q) -> p k q", k=16),
                    in0=ps[bS][:].rearrange("p (k q) -> p k q", k=16), scalar=SCALE,
                    in1=biasP[:, c, :].unsqueeze(2).to_broadcast([P, 16, 32]), op0=ALU.mult, op1=ALU.add),
                    r=[PS(bS), "biasP"], w=["tmpS"])
                T.op("act", lambda e: e.activation(out=PTp[s2][:, :, s2 * 32:(s2 + 1) * 32],
                                                   in_=tmpS[:].rearrange("p (k q) -> p k q", k=16), func=AF.Exp),
                     r=["tmpS"], w=["PTp%d" % s2])

            def sB(it):
                h, s2 = it // 4, it % 4
                pa = it % 2
                for kt in range(16):
                    mm(ps[3][:, 0:129], PTp[s2][:, kt, :], vp[pa][:, kt, :], s2 == 0 and kt == 0, False,
                       r=["PTp%d" % s2, "vp%d" % pa], w=[PS(3)], sig=(kt == 15))
                if it + 2 < 32:
                    load_pastV(it + 2)

            def snew(h):
                a = h % 2
                mm(ps[0][:, 0:P], ktS[:, h, :], qh[a][:, 1024:1152], True, False, r=["ktS", "qh%d" % a], w=[PS(0)],
                   sig=False)
                mm(ps[0][:, 0:P], identb, masks, False, True, r=CONST, w=[PS(0)])
                T.op("act", lambda e: e.activation(out=PTn[:], in_=ps[0][:, 0:P], func=AF.Exp,
                                                   bias=biasNew[:, h:h + 1], scale=SCALE),
                     r=[PS(0), "biasNew"], w=["PTn"])
                mm(ps[3][:, 0:129], PTn[:], vS[:, h, :], False, True, r=["PTn", "vS"], w=[PS(3)])
                T.op("dve", lambda e: e.reciprocal(out=sstat[:, h:h + 1], in_=ps[3][:, 128:129]),
                     r=[PS(3)], w=["sstat"])
                T.op("dve", lambda e: e.tensor_scalar(out=yas[:, h * P:(h + 1) * P], in0=ps[3][:, 0:128],
                                                      scalar1=sstat[:, h:h + 1], scalar2=None, op0=ALU.mult),
                     r=[PS(3), "sstat"], w=["yas"])

            sA(0)
            for h in range(8):
                if h + 1 < 8:
                    load_head(h + 1)
                e0 = prompt(h, 0)
                sA(4 * h + 1)
                e0()
                sB(4 * h + 0)
                sA(4 * h + 2)
                sB(4 * h + 1)
                e1 = prompt(h, 1)
                sA(4 * h + 3)
                e1()
                sB(4 * h + 2)
                if h + 1 < 8:
                    sA(4 * h + 4)
                sB(4 * h + 3)
                snew(h)
            wstate["limit"] = None
            wpump()

            T.op("dve", lambda e: e.memset(sstat[:, 16:20], 0.0), w=["sstat2"])
            T.op("act", lambda e: e.activation(out=yasb[:], in_=yas[:], func=AF.Square, accum_out=sstat[:, 16:17]),
                 r=["yas"], w=["sstat2", "yasb"])
            T.op("dve", lambda e: e.tensor_copy(out=yasb[:], in_=yas[:]), r=["yas"], w=["yasb"])
            for h in range(8):
                tp(pb[0][:, h * P:(h + 1) * P], yasb[:, h * P:(h + 1) * P], identb, r=["yasb"] + CONST, w=[PB(0)],
                   sig=(h == 7))
            for h in range(8):
                T.op("act", lambda e: e.mul(out=yagS[:, h, :], in_=pb[0][:, h * P:(h + 1) * P],
                                            mul=gaT[:, h:h + 1]), r=[PB(0)] + CONST, w=["yagS"])
            T.dma("sp", yag_d.rearrange("h p t -> p h t")[:, :, 1024:1152], yagS[:], r=["yagS"], w=["yag_d"], key="yagS")
            for t in range(9):
                mm(ps[0][:, t:t + 1], accc[:, t * P:(t + 1) * P], onesf[:, 0:1], True, True,
                   r=["accc"] + CONST, w=[PS(0)], sig=(t == 8))
            for t in range(8):
                mm(ps[1][:, t:t + 1], acca[:, t * P:(t + 1) * P], onesf[:, 0:1], True, True,
                   r=["acca"] + CONST, w=[PS(1)], sig=(t == 7))
            T.op("dve", lambda e: e.tensor_copy(out=sstat[:, 20:28], in_=ps[1][:, 0:8]), r=[PS(1)], w=["sstat3"])
            T.op("dve", lambda e: e.tensor_copy(out=sstat[:, 28:29], in_=sstat[:, 16:17]),
                 r=["sstat2", "sstat3"], w=["sstat3"])
            T.op("dve", lambda e: e.tensor_copy(out=rsc[:, 0:9], in_=ps[0][:, 0:9]), r=[PS(0)], w=["rsc"])
            rstd_from(rsc[:, 0:9], rsc[:, 0:9], rsc[:, 0:9], 1.0 / 1024, ["rsc"])
            rstd_from(sstat[:, 20:29], rsa[:, 0:9], sstat[:, 20:29], 1.0 / 1024, ["sstat3", "rsa"])
            T.barrier()

        s123.close()
        s456 = ExitStack()
        xnT = sb(s456, "xnT2", [P, KC, TO], BF16)
        tg3 = ((0, 384), (384, 384), (768, 384))

        def norm_own_pre(t, a):
            norm_pre_g(xt[a][:], "xt%d" % a, xsb2[a][:], "xsbb%d" % a, a)

        def norm_own_tp(t, a):
            norm_tp_g(xsb2[a][:], "xsbb%d" % a, xnT[:, :, t * P:(t + 1) * P], "xnT2")

        with ExitStack() as s4:
            xt = [sb(s4, "xt%d" % i, [P, D]) for i in range(2)]
            xsb2 = [sb(s4, "xsbb%d" % i, [P, D], BF16) for i in range(2)]
            ycg = sb(s4, "ycg", [P, 8, TO], BF16)
            yag = sb(s4, "yag", [P, 8, TO], BF16)
            T.dma("sp", ycg[:], ycg_d.rearrange("c p t -> p c t"), r=["ycg_d"], w=["ycg"], key="ycg")
            T.dma("sp", yag[:], yag_d.rearrange("c p t -> p c t"), r=["yag_d"], w=["yag"], key="yag")
            load_gbc("g_xattn")
            wos = [wview(14 + nb) for nb in range(4)]
            T.dma("sp", xt[0][:], x_own[0], w=["xt0"], key="xt0")
            for t in range(9):
                a = t % 2
                if t + 1 < 9:
                    T.dma("sp", xt[1 - a][:], x_own[t + 1], w=["xt%d" % (1 - a)], key="xt%d" % (1 - a))
                cs = slice(t * P, (t + 1) * P)
                xk = "xt%d" % a
                for nb in range(4):
                    wo, wok = wos[nb]
                    bA, bB = 2 * (nb % 2), 2 * (nb % 2) + 1
                    for kc in range(8):
                        mm(ps[bA][:], ycg[:, kc, cs], wo[:, kc, :], kc == 0, kc == 7, r=["ycg", wok], w=[PS(bA)])
                    for kc in range(8):
                        mm(ps[bB][:], yag[:, kc, cs], wo[:, 8 + kc, :], kc == 0, kc == 7, r=["yag", wok], w=[PS(bB)])
                    xs_ = xt[a][:, nb * 512:(nb + 1) * 512]
                    T.op("dve", lambda e: e.scalar_tensor_tensor(out=xs_, in0=ps[bA][:], scalar=rsc[:, t:t + 1], in1=xs_,
                                                                 op0=ALU.mult, op1=ALU.add),
                         r=[PS(bA), "rsc", xk], w=[xk])
                    T.op("dve", lambda e: e.scalar_tensor_tensor(out=xs_, in0=ps[bB][:], scalar=rsa[:, t:t + 1], in1=xs_,
                                                                 op0=ALU.mult, op1=ALU.add),
                         r=[PS(bB), "rsa", xk], w=[xk])
                if t >= 1:
                    norm_own_tp(t - 1, 1 - a)
                T.dma("sp", xres_d[t], xt[a][:], r=[xk], w=["xres_d"], key=xk)
                norm_own_pre(t, a)
            norm_own_tp(8, 0)
            for nb in range(4):
                wrel(14 + nb)
            T.barrier()

        with ExitStack() as s5:
            xt = [sb(s5, "xu%d" % i, [P, D]) for i in range(2)]
            xsb2 = [sb(s5, "xsbc%d" % i, [P, D], BF16) for i in range(2)]
            mkT = sb(s5, "mkT", [P, 4, 256], BF16)
            mvb = sb(s5, "mvb", [P, 2, 512], BF16)
            cmvb = sb(s5, "cmvb", [P, 8, 4, 129], BF16)
            mkTs = sb(s5, "mkTs", [P, 4, 4, 256], BF16)
            T.dma("sp", mkT[:].rearrange("p h t -> p (h t)"), mkT_d, r=["mkT_d"], w=["mkT"], key="mkT")
            T.dma("sp", mvb[:].rearrange("p m c -> p (m c)"), mvb_d, r=["mvb_d"], w=["mvb"], key="mvb")
            T.dma("sp", mkTs[:].rearrange("p s h t -> p (s h t)"), mkTs_d, r=["mkTs_d"], w=["mkTs"], key="mkTs")
            T.op("pool", lambda e: e.memset(cmvb[:], 1.0), w=["cmvb"])
            T.dma("sp", cmvb[:].rearrange("p a h d -> p (a h) d")[:, :, 0:128],
                  cmvb_d.rearrange("p a (h d) -> p (a h) d", h=4), r=["cmvb_d"], w=["cmvb"], key="cmvb")

            qxT = sb(s5, "qxT", [P, 4, TO], BF16)
            oxT = sb(s5, "oxT", [P, 4, TO], BF16)
            wxq_, wxqk = wview(18)
            for h in range(4):
                banks = (0, 1, 2) if h % 2 == 0 else (3, 4, 5)
                for kc in range(KC):
                    for gi, (c0, nn) in enumerate(tg3):
                        mm(ps[banks[gi]][:, 0:nn], wxq_[:, kc, h * P:(h + 1) * P], xnT[:, kc, c0:c0 + nn],
                           kc == 0, kc == KC - 1, r=[wxqk, "xnT2"], w=[PS(banks[gi])])
                for gi, (c0, nn) in enumerate(tg3):
                    T.op("act", lambda e: e.copy(out=qxT[:, h, c0:c0 + nn], in_=ps[banks[gi]][:, 0:nn]),
                         r=[PS(banks[gi])], w=["qxT"])
            wrel(18)

            PTx = [sb(s5, "PTx%d" % i, [P, 512], BF16) for i in range(2)]
            rlx = sb(s5, "rlx", [P, 512])
            PTs = [sb(s5, "PTs%d" % i, [P, 2, P], BF16) for i in range(4)]
            oxs = sb(s5, "oxs", [P, 512])
            oxsb = sb(s5, "oxsb", [P, 512], BF16)
            for i in range(4):
                T.op("pool", lambda e: e.memset(PTs[i][:], 0.0), w=["PTs%d" % i])
            xsteps = [(hh_, g_, mt_) for hh_ in range(4) for g_ in range(2) for mt_ in range(2)]

            def xS(i):
                hh_, g_, mt_ = xsteps[i]
                mm(ps[i % 2][:], mkT[:, hh_, mt_ * P:(mt_ + 1) * P], qxT[:, hh_, g_ * 512:(g_ + 1) * 512], True, True,
                   r=["mkT", "qxT"], w=[PS(i % 2)])

            def xstep(i):
                hh_, g_, mt_ = xsteps[i]
                bO, bL = 2 + 2 * g_, 3 + 2 * g_
                if i + 1 < len(xsteps) and xsteps[i + 1][0] == hh_:
                    xS(i + 1)
                T.op("act", lambda e: e.activation(out=PTx[i % 2][:], in_=ps[i % 2][:], func=AF.Exp, scale=SCALE),
                     r=[PS(i % 2)], w=["PTx%d" % (i % 2)])
                mm(ps[bO][:], mvb[:, mt_, hh_ * P:(hh_ + 1) * P], PTx[i % 2][:], mt_ == 0, mt_ == 1,
                   r=["mvb", "PTx%d" % (i % 2)], w=[PS(bO)])
                mm(ps[bL][:], onesb, PTx[i % 2][:], mt_ == 0, mt_ == 1, r=["PTx%d" % (i % 2)] + CONST, w=[PS(bL)])
                if mt_ == 1:
                    T.op("dve", lambda e: e.reciprocal(out=rlx[:], in_=ps[bL][:]), r=[PS(bL)], w=["rlx"])
                    T.op("dve", lambda e: e.tensor_tensor(out=oxT[:, hh_, g_ * 512:(g_ + 1) * 512], in0=ps[bO][:],
                                                          in1=rlx[:], op=ALU.mult), r=[PS(bO), "rlx"], w=["oxT"])

            for h in range(4):
                xS(4 * h)
                for i in range(4 * h, 4 * h + 4):
                    xstep(i)
                bOs = 2 + (h % 2) * 2
                for s2 in range(4):
                    bS = s2 % 2
                    for mt in range(2):
                        mm(ps[bS][:, mt * 32:(mt + 1) * 32], mkTs[:, s2, h, mt * P:(mt + 1) * P],
                           qxT[:, h, 1024 + s2 * 32:1024 + (s2 + 1) * 32], True, True, r=["mkTs", "qxT"], w=[PS(bS)],
                           sig=(mt == 1))
                    T.op("act", lambda e: e.activation(out=PTs[s2][:, :, s2 * 32:(s2 + 1) * 32],
                                                       in_=ps[bS][:, 0:64].rearrange("p (k q) -> p k q", k=2),
                                                       func=AF.Exp, scale=SCALE),
                         r=[PS(bS)], w=["PTs%d" % s2])
                    for mt in range(2):
                        mm(ps[bOs][:, 0:129], PTs[s2][:, mt, :], cmvb[:, s2 * 2 + mt, h, :],
                           s2 == 0 and mt == 0, s2 == 3 and mt == 1, r=["PTs%d" % s2, "cmvb"], w=[PS(bOs)])
                T.op("dve", lambda e: e.reciprocal(out=sstat[:, h:h + 1], in_=ps[bOs][:, 128:129]),
                     r=[PS(bOs)], w=["sstat"])
                T.op("dve", lambda e: e.tensor_scalar(out=oxs[:, h * P:(h + 1) * P], in0=ps[bOs][:, 0:128],
                                                      scalar1=sstat[:, h:h + 1], scalar2=None, op0=ALU.mult),
                     r=[PS(bOs), "sstat"], w=["oxs"])
            T.op("dve", lambda e: e.tensor_copy(out=oxsb[:], in_=oxs[:]), r=["oxs"], w=["oxsb"])
            for h in range(4):
                tp(pb[0][:, h * P:(h + 1) * P], oxsb[:, h * P:(h + 1) * P], identb, r=["oxsb"] + CONST, w=[PB(0)],
                   sig=(h == 3))
            T.op("act", lambda e: e.copy(out=oxT[:, :, 1024:1152], in_=pb[0][:, 0:512].rearrange("p (h t) -> p h t", h=4)),
                 r=[PB(0)], w=["oxT"])

            load_gbc("g_mlp")
            wxo_, wxok = wview(19)
            T.dma("sp", xt[0][:], xres_d[0], r=["xres_d"], w=["xt0"], key="xt0")
            for t in range(9):
                a = t % 2
                if t + 1 < 9:
                    T.dma("sp", xt[1 - a][:], xres_d[t + 1], r=["xres_d"], w=["xt%d" % (1 - a)], key="xt%d" % (1 - a))
                cs = slice(t * P, (t + 1) * P)
                xk = "xt%d" % a
                for nb in range(4):
                    b = nb
                    for kc in range(4):
                        mm(ps[b][:], oxT[:, kc, cs], wxo_[:, kc, nb * 512:(nb + 1) * 512], kc == 0, kc == 3,
                           r=["oxT", wxok], w=[PS(b)])
                    xs_ = xt[a][:, nb * 512:(nb + 1) * 512]
                    T.op("dve", lambda e: e.tensor_tensor(out=xs_, in0=ps[b][:], in1=xs_, op=ALU.add),
                         r=[PS(b), xk], w=[xk])
                if t >= 1:
                    norm_own_tp(t - 1, 1 - a)
                T.dma("sp", xres_d[t], xt[a][:], r=[xk], w=["xres_d"], key=xk)
                norm_own_pre(t, a)
            norm_own_tp(8, 0)
            wrel(19)
            T.barrier()

        with ExitStack() as s6:
            yacc = sb(s6, "yacc", [P, 9, D])
            hT = sb(s6, "hT", [P, 4, TO], BF16)
            rl_ = [sb(s6, "relu%d" % i, [P, 512]) for i in range(2)]
            ri = [0]
            for t in range(9):
                T.dma("sp", yacc[:, t, :], xres_d[t], r=["xres_d"], w=["yacc%d" % t], key="yacc%d" % t)
            load_gbc("g_final")
            T.op("dve", lambda e: e.memset(sstat[:, 0:16], 0.0), w=["sstat"])
            fjunk = xnT[:].rearrange("p k t -> p (k t)")[:, 0:D]

            def final_tile(t):
                yk = "yacc%d" % t
                fk = "fin%d" % t
                T.op("act", lambda e: e.activation(out=fjunk, in_=yacc[:, t, :], func=AF.Square,
                                                   accum_out=sstat[:, t:t + 1]), r=[yk, "sstat"], w=[fk, "xnT2"])
                rstd_from(sstat[:, t:t + 1], sstat[:, 16 + t:17 + t], sstat[:, t:t + 1], 1.0 / D, [fk])
                T.op("pool", lambda e: e.tensor_tensor(out=yacc[:, t, :], in0=yacc[:, t, :], in1=gbc[:],
                                                       op=ALU.mult), r=[yk, "gbc"], w=[yk])
                T.op("act", lambda e: e.mul(out=yacc[:, t, :], in_=yacc[:, t, :], mul=sstat[:, 16 + t:17 + t]),
                     r=[yk, fk], w=[yk])
                T.dma("sp", y_own[t], yacc[:, t, :], r=[yk], key=yk)
            for hb in range(16):
                wu, wuk = wview(20 + 2 * hb)
                wd, wdk = wview(21 + 2 * hb)
                for hc in range(4):
                    banks = (0, 1, 2) if hc % 2 == 0 else (3, 4, 5)
                    for kc in range(KC):
                        for gi, (c0, nn) in enumerate(tg3):
                            mm(ps[banks[gi]][:, 0:nn], wu[:, kc, hc * P:(hc + 1) * P], xnT[:, kc, c0:c0 + nn],
                               kc == 0, kc == KC - 1, r=[wuk, "xnT2"], w=[PS(banks[gi])])
                    for gi, (c0, nn) in enumerate(tg3):
                        rr = ri[0] % 2
                        ri[0] += 1
                        T.op("act", lambda e: e.activation(out=rl_[rr][:, 0:nn], in_=ps[banks[gi]][:, 0:nn], func=AF.Relu),
                             r=[PS(banks[gi])], w=["relu%d" % rr])
                        T.op("dve", lambda e: e.tensor_tensor(out=hT[:, hc, c0:c0 + nn], in0=rl_[rr][:, 0:nn],
                                                              in1=rl_[rr][:, 0:nn], op=ALU.mult),
                             r=["relu%d" % rr], w=["hT%d" % hc])
                wrel(20 + 2 * hb)
                def dmm(t, nb, hcs):
                    b = (t * 4 + nb) % 6
                    for hc in hcs:
                        mm(ps[b][:], hT[:, hc, t * P:(t + 1) * P], wd[:, hc, nb * 512:(nb + 1) * 512], hc == 0, hc == 3,
                           r=["hT%d" % hc, wdk], w=[PS(b)])

                def dadd(t, nb):
                    b = (t * 4 + nb) % 6
                    yk = "yacc%d" % t
                    ys_ = yacc[:, t, nb * 512:(nb + 1) * 512]
                    T.op("dve", lambda e: e.tensor_tensor(out=ys_, in0=ps[b][:], in1=ys_, op=ALU.add),
                         r=[PS(b), yk], w=[yk])

                grp = [(t, nb) for t in range(9) for nb in range(4)]
                for (t, nb) in grp[:6]:
                    dmm(t, nb, (0, 1, 2))
                for (t, nb) in grp[:6]:
                    dmm(t, nb, (3,))
                    dadd(t, nb)
                    if hb == 15 and nb == 3:
                        final_tile(t)
                for (t, nb) in grp[6:]:
                    dmm(t, nb, (0, 1, 2, 3))
                    dadd(t, nb)
                    if hb == 15 and nb == 3:
                        final_tile(t)
                wrel(21 + 2 * hb)

            T.barrier()
        s456.close()
        pstk[0].close()
    return nc


def _consts():
    s = np.arange(P)[:, None]
    t = np.arange(P)[None, :]
    cf = np.zeros((P, 14, P), np.float32)
    cf[:, 0] = np.eye(P)
    cf[:, 1] = (s <= t)
    cf[:, 2] = (s == 127) * np.ones((1, P))
    cf[:, 3] = ((s // 32) == (t // 32)) & (s <= t)
    cf[:, 4] = (s == (t // 32) * 32 + 31)
    for s2 in range(4):
        cf[:, 5 + s2] = (s == 32 * s2 + 31) * np.ones((1, P))
        cf[:, 9 + s2] = (s == 127) & ((t // 32) == s2)
    cf[:, 13] = 1.0
    cb = np.zeros((P, 4, P), np.float32)
    cb[:, 0] = np.eye(P)
    cb[:, 1] = np.where(s <= t, 0.0, NEG)
    cb[:, 2] = np.where(((s // 32) == (t // 32)) & (s <= t), 0.0, NEG)
    cb[:, 3] = 1.0
    return cf, cb


_NC = None


def kernel(x_prompt, x_sample, cache_k, cache_v, cache_logf, cache_conv, cache_mem_k, cache_mem_v, mem_prompt,
           g_mix, w_in, b_f, conv_w, g_conv_out, g_attn_out, w_out, g_xattn, g_mem, w_xq, w_xkv, w_xo,
           g_mlp, w_up, w_down, g_final):
    global _NC
    f = lambda a: np.ascontiguousarray(np.asarray(a, dtype=np.float32))
    x_prompt, x_sample = f(x_prompt), f(x_sample)
    cache_k, cache_v, cache_logf, cache_conv = f(cache_k), f(cache_v), f(cache_logf), f(cache_conv)
    cache_mem_k, cache_mem_v, mem_prompt = f(cache_mem_k), f(cache_mem_v), f(mem_prompt)
    cf, cb = _consts()
    bc = lambda g: np.ascontiguousarray(np.broadcast_to(f(g).reshape(1, D), (P, D)))
    shared = {
        "w_in": f(w_in)[0], "w_out": f(w_out)[0], "w_xq": f(w_xq)[0], "w_xkv": f(w_xkv)[0], "w_xo": f(w_xo)[0],
        "w_up": f(w_up)[0], "w_down": f(w_down)[0],
        "g_mix": bc(g_mix), "g_xattn": bc(g_xattn), "g_mem": bc(g_mem), "g_mlp": bc(g_mlp), "g_final": bc(g_final),
        "cf32": cf, "cb16": cb,
    }
    small_base = np.zeros((P, 112), np.float32)
    small_base[:, 0:8] = f(b_f).reshape(1, 8)
    small_base[:, 8:16] = f(g_conv_out).reshape(8, P).T
    small_base[:, 16:24] = f(g_attn_out).reshape(8, P).T
    small_base[:, 24:48] = f(conv_w).reshape(3, 8, P).transpose(2, 1, 0).reshape(P, 24)
    in_maps = []
    for c in range(8):
        b, j = c // 4, c % 4
        npast = 8 * j
        x_halo = np.zeros((P, D), np.float32)
        if j > 0:
            x_halo[0:2] = x_prompt[b, 1024 * j - 2:1024 * j]
        sm = small_base.copy()
        sm[:, 48 + npast:80] = -NEG
        sm[:, 80:80 + npast] = 1.0
        m = dict(shared)
        m.update({
            "x_own": np.ascontiguousarray(np.concatenate(
                [x_prompt[b, 1024 * j:1024 * (j + 1)].reshape(8, P, D), x_sample[4 * c:4 * c + 4].reshape(1, P, D)], 0)),
            "x_halo": x_halo, "smallp": sm,
            "mem": np.ascontiguousarray(mem_prompt[b]),
            "ck": np.ascontiguousarray(cache_k[0, 4 * c:4 * c + 4].reshape(4, 2048, 1024)),
            "cv": np.ascontiguousarray(cache_v[0, 4 * c:4 * c + 4].reshape(4, 2048, 1024)),
            "clogf": np.ascontiguousarray(cache_logf[0, 4 * c:4 * c + 4]),
            "cconv": np.ascontiguousarray(cache_conv[0, 4 * c:4 * c + 4].reshape(8, 1024)),
            "cmk": np.ascontiguousarray(cache_mem_k[0, 4 * c:4 * c + 4].reshape(4, 256, 512)),
            "cmv": np.ascontiguousarray(cache_mem_v[0, 4 * c:4 * c + 4].reshape(4, 256, 512)),
        })
        in_maps.append(m)
    if _NC is None:
        _NC = build()
    res = run_bass_kernel_spmd(_NC, in_maps, core_ids=list(range(8))).results

    y_prompt = np.zeros((2, 4096, D), np.float32)
    y_sample = np.zeros((32, 32, D), np.float32)
    nkp = np.zeros((1, 2, 4096, 8, 128), np.float32)
    nvp = np.zeros((1, 2, 4096, 8, 128), np.float32)
    nfp = np.zeros((1, 2, 4096, 8), np.float32)
    ncp = np.zeros((1, 2, 2, 1024), np.float32)
    nmk = np.zeros((1, 2, 256, 4, 128), np.float32)
    nmv = np.zeros((1, 2, 256, 4, 128), np.float32)
    nks = np.zeros((1, 32, 32, 8, 128), np.float32)
    nvs = np.zeros((1, 32, 32, 8, 128), np.float32)
    nfs = np.zeros((1, 32, 32, 8), np.float32)
    ncs = np.zeros((1, 32, 2, 1024), np.float32)
    for c in range(8):
        b, j = c // 4, c % 4
        r = res[c]
        sl = slice(1024 * j, 1024 * (j + 1))
        y_prompt[b, sl] = r["y_own"][0:8].reshape(1024, D)
        y_sample[4 * c:4 * c + 4] = r["y_own"][8].reshape(4, 32, D)
        nkp[0, b, sl] = r["k_new"][0:8].reshape(1024, 8, 128)
        nvp[0, b, sl] = r["v_new"][0:8].reshape(1024, 8, 128)
        nks[0, 4 * c:4 * c + 4] = r["k_new"][8].reshape(4, 32, 8, 128)
        nvs[0, 4 * c:4 * c + 4] = r["v_new"][8].reshape(4, 32, 8, 128)
        fn = r["f_new"]
        nfp[0, b, sl] = fn[:, 0:8, :].transpose(1, 0, 2).reshape(1024, 8)
        nfs[0, 4 * c:4 * c + 4] = fn[:, 8, :].reshape(4, 32, 8)
        cn = r["conv_new"].reshape(5, 2, 1024)
        if j == 3:
            ncp[0, b] = cn[0]
        ncs[0, 4 * c:4 * c + 4] = cn[1:5]
        if j == 0:
            nmk[0, b] = r["mk_new"].reshape(256, 4, 128)
            nmv[0, b] = r["mv_new"].reshape(256, 4, 128)
    return (y_prompt, y_sample, nkp, nvp, nfp, ncp, nmk, nmv, nks, nvs, nfs, ncs)
```

```python
import numpy as np
from contextlib import ExitStack
import concourse.bass as bass
import concourse.mybir as mybir
from concourse.bass_utils import run_bass_kernel_spmd

F32 = mybir.dt.float32
BF16 = mybir.dt.bfloat16
AF = mybir.ActivationFunctionType
ALU = mybir.AluOpType
P = 128
D = 2048
KC = 16
NCTX = 32
NSLOT = 40
NATT = 24
TO = 1152
TX = 1280
EPS = 1e-6
SCALE = 128.0 ** -0.5
NEG = -30000.0


class Tracker:
    def __init__(self, nc, st):
        self.nc = nc
        self.st = st
        self.E = {}
        for name, e in (("pe", nc.tensor), ("act", nc.scalar), ("dve", nc.vector),
                        ("pool", nc.gpsimd), ("sp", nc.sync)):
            sem = st.enter_context(nc.semaphore("s_" + name))
            self.E[name] = dict(e=e, sem=sem, n=0, cnt=0, sig=[], seen={})
        self.lw = {}
        self.rd = {}
        self.ds = {}

    def _resolve(self, dep):
        if dep[0] == "d":
            S = self.ds[dep[1]]
            return ("d:" + dep[1], S["sem"], dep[2])
        _, en, idx = dep
        E = self.E[en]
        sig = E["sig"]
        lo, hi = 0, len(sig)
        while lo < hi:
            mid = (lo + hi) // 2
            if sig[mid][0] >= idx:
                hi = mid
            else:
                lo = mid + 1
        if lo >= len(sig):
            raise RuntimeError("dependency on unsignaled op on %s" % en)
        return ("e:" + en, E["sem"], sig[lo][1])

    def _wait(self, en, deps):
        E = self.E[en]
        need = {}
        for dep in deps:
            if dep[0] == "e" and dep[1] == en and en in ("pe", "sp"):
                continue
            sid, sem, val = self._resolve(dep)
            if val > need.get(sid, (None, 0))[1]:
                need[sid] = (sem, val)
        for sid, (sem, val) in need.items():
            if val > E["seen"].get(sid, 0):
                E["e"].wait_ge(sem, val)
                E["seen"][sid] = val

    def _deps(self, r, w):
        deps = []
        for k in r:
            if k in self.lw:
                deps.append(self.lw[k])
        for k in w:
            if k in self.lw:
                deps.append(self.lw[k])
            deps.extend(self.rd.get(k, {}).values())
        return deps

    def _record(self, me, rk, r, w):
        for k in w:
            self.lw[k] = me
            self.rd[k] = {}
        for k in r:
            self.rd.setdefault(k, {})[rk] = me

    def op(self, en, fn, r=(), w=(), sig=True):
        E = self.E[en]
        self._wait(en, self._deps(r, w))
        ins = fn(E["e"])
        idx = E["n"]
        E["n"] += 1
        if sig:
            E["cnt"] += 1
            ins.then_inc(E["sem"], 1)
            E["sig"].append((idx, E["cnt"]))
        self._record(("e", en, idx), "e:" + en, r, w)
        return ins

    def dma(self, qn, out, in_, r=(), w=(), key=None):
        E = self.E[qn]
        self._wait(qn, self._deps(r, w))
        key = key + ("|s" if qn == "pool" else "|h")
        if key not in self.ds:
            self.ds[key] = dict(sem=self.st.enter_context(self.nc.semaphore("d%d" % len(self.ds))), tot=0)
        S = self.ds[key]
        ins = E["e"].dma_start(out=out, in_=in_)
        S["tot"] += 16
        ins.then_inc(S["sem"], 16)
        E["n"] += 1
        self._record(("d", key, S["tot"]), "d:" + key, r, w)
        return ins

    def coll(self, kind, in_ap, out_ap, groups, r=(), w=(), key=None):
        E = self.E["pool"]
        self._wait("pool", self._deps(r, w))
        if key not in self.ds:
            self.ds[key] = dict(sem=self.st.enter_context(self.nc.semaphore("d%d" % len(self.ds))), tot=0)
        S = self.ds[key]
        ins = E["e"].collective_compute(kind, ALU.bypass, replica_groups=groups, ins=[in_ap], outs=[out_ap],
                                            dma_qos="P3")
        S["tot"] += 1
        ins.then_inc(S["sem"], 1)
        E["n"] += 1
        self._record(("d", key, S["tot"]), "d:" + key, r, w)
        return ins

    def barrier(self):
        for en, E in self.E.items():
            for fn, F in self.E.items():
                if fn == en or fn == "sp" or F["n"] == 0:
                    continue
                if not F["sig"]:
                    continue
                val = F["sig"][-1][1]
                sid = "e:" + fn
                if val > E["seen"].get(sid, 0):
                    E["e"].wait_ge(F["sem"], val)
                    E["seen"][sid] = val
            for k, S in self.ds.items():
                if k.startswith("wr") or k.startswith("w2s") or k == "cc":
                    continue
                sid = "d:" + k
                if S["tot"] > E["seen"].get(sid, 0):
                    E["e"].wait_ge(S["sem"], S["tot"])
                    E["seen"][sid] = S["tot"]

    def check_last_signaled(self):
        for en, E in self.E.items():
            if en == "sp" or E["n"] == 0:
                continue


def build():
    nc = bass.Bass("TRN2", target_bir_lowering=False)

    def din(name, shape):
        return nc.dram_tensor(name, list(shape), F32, kind="ExternalInput").ap()

    def dout(name, shape):
        return nc.dram_tensor(name, list(shape), F32, kind="ExternalOutput").ap()

    x_own = din("x_own", [9, P, D])
    x_halo = din("x_halo", [P, D])
    mem = din("mem", [256, D])
    ck = din("ck", [4, 2048, 1024])
    cv = din("cv", [4, 2048, 1024])
    clogf = din("clogf", [4, 2048, 8])
    cconv = din("cconv", [8, 1024])
    cmk = din("cmk", [4, 256, 512])
    cmv = din("cmv", [4, 256, 512])
    w_in = din("w_in", [D, 6152])
    w_out = din("w_out", [D, D])
    w_xq = din("w_xq", [D, 512])
    w_xkv = din("w_xkv", [D, 1024])
    w_xo = din("w_xo", [512, D])
    w_up = din("w_up", [D, 8192])
    w_down = din("w_down", [8192, D])
    gb_in = {n: din(n, [P, D]) for n in ("g_mix", "g_xattn", "g_mem", "g_mlp", "g_final")}
    cf_in = din("cf32", [P, 14, P])
    cb_in = din("cb16", [P, 4, P])
    sp_in = din("smallp", [P, 112])

    y_own = dout("y_own", [9, P, D])
    k_new = dout("k_new", [9, P, 1024])
    v_new = dout("v_new", [9, P, 1024])
    f_new = dout("f_new", [P, 9, 8])
    conv_new = dout("conv_new", [10, 1024])
    mk_new = dout("mk_new", [256, 512])
    mv_new = dout("mv_new", [256, 512])

    kloc_t = [nc.dram_tensor("kloc%d" % q, [512, 1024], BF16) for q in range(2)]
    kall_t = [nc.dram_tensor("kall%d" % q, [2048, 1024], BF16) for q in range(2)]
    vloc_t = [nc.dram_tensor("vloc%d" % q, [1024, 512], BF16) for q in range(2)]
    vall_t = [nc.dram_tensor("vall%d" % q, [4096, 512], BF16) for q in range(2)]
    floc_t = nc.dram_tensor("floc", [1024, 8], F32)
    fall_t = nc.dram_tensor("fall", [4096, 8], F32)
    kloc = [t.ap() for t in kloc_t]
    kall = [t.ap() for t in kall_t]
    vloc = [t.ap() for t in vloc_t]
    vall = [t.ap() for t in vall_t]
    floc, fall = floc_t.ap(), fall_t.ap()
    GROUPS = [[0, 1, 2, 3], [4, 5, 6, 7]]
    w2s = nc.dram_tensor("w2s", [4, P, 8192], BF16).ap()
    mkT_d = nc.dram_tensor("mkT_d", [P, 1024], BF16).ap()
    mvb_d = nc.dram_tensor("mvb_d", [P, 1024], BF16).ap()
    mkTs_d = nc.dram_tensor("mkTs_d", [P, 4096], BF16).ap()
    cmvb_d = nc.dram_tensor("cmvb_d", [P, 8, 512], BF16).ap()

    w_in_v = w_in.rearrange("(kc p) n -> p kc n", p=P)
    w_out_v = w_out.rearrange("(kc p) n -> p kc n", p=P)
    w_xq_v = w_xq.rearrange("(kc p) n -> p kc n", p=P)
    w_xkv_v = w_xkv.rearrange("(kc p) n -> p kc n", p=P)
    w_xo_v = w_xo.rearrange("(kc p) n -> p kc n", p=P)
    w_up_v = w_up.rearrange("(kc p) n -> p kc n", p=P)
    w_down_v = w_down.rearrange("(kc p) n -> p kc n", p=P)

    WSEQ = []
    for c0 in (4096, 4608, 5120, 5632):
        WSEQ.append((w_in_v[:, :, c0:c0 + 512], 16))
    for c0 in (0, 1024, 2048, 1536, 2560, 512, 3072, 3584):
        WSEQ.append((w_in_v[:, :, c0:c0 + 512], 16))
    WSEQ.append((w_xkv_v[:, :, 0:512], 16))
    WSEQ.append((w_xkv_v[:, :, 512:1024], 16))
    for nb in range(4):
        WSEQ.append((w_out_v[:, :, nb * 512:(nb + 1) * 512], 16))
    WSEQ.append((w_xq_v, 16))
    WSEQ.append((w_xo_v, 4))
    for hb in range(16):
        WSEQ.append((w_up_v[:, :, hb * 512:(hb + 1) * 512], 16))
        WSEQ.append((w_down_v[:, hb * 4:(hb + 1) * 4, :], 4))

    with ExitStack() as st:
        T = Tracker(nc, st)

        def sb(stk, name, shape, dt=F32):
            return stk.enter_context(nc.sbuf_tensor(name, list(shape), dt))

        ps, pb = [], []
        pstk = [ExitStack()]
        pgen = [0]

        def alloc_psum(nf, nbf):
            pstk[0].close()
            pstk[0] = ExitStack()
            g = pgen[0]
            pgen[0] += 1
            ps[:] = [pstk[0].enter_context(nc.psum_tensor("ps%d_%d" % (i, g), [P, 512], F32)) for i in range(nf)]
            pb[:] = [pstk[0].enter_context(nc.psum_tensor("pb%d_%d" % (i, g), [P, 1024], BF16)) for i in range(nbf)]

        alloc_psum(5, 3)
        PS = lambda i: "ps%d" % i
        PB = lambda i: "pb%d" % i

        wr = [sb(st, "wr%d" % i, [P, 8192], BF16) for i in range(4)]
        gbc = sb(st, "gbc", [P, D])
        cf = sb(st, "cf", [P, 14, P])
        cb = sb(st, "cb", [P, 4, P], BF16)
        smp = sb(st, "smp", [P, 112])
        wf = sb(st, "wf", [P, KC, 8], BF16)
        stt = sb(st, "stt", [P, 2, 8])
        rsc = sb(st, "rsc", [P, 16])
        rsa = sb(st, "rsa", [P, 16])
        sstat = sb(st, "sstat", [P, 32])
        qT_d = nc.dram_tensor("qT_d", [8, P, TO], BF16).ap()
        ycg_d = nc.dram_tensor("ycg_d", [8, P, TO], BF16).ap()
        yag_d = nc.dram_tensor("yag_d", [8, P, TO], BF16).ap()
        xres_d = nc.dram_tensor("xres_d", [9, P, D], F32).ap()
        identf = cf[:, 0, :]
        Umat = cf[:, 1, :]
        E127 = cf[:, 2, :]
        Ublk = cf[:, 3, :]
        EblkEnd = cf[:, 4, :]
        Esel = lambda s2: cf[:, 5 + s2, :]
        Epast = lambda s2: cf[:, 9 + s2, :]
        onesf = cf[:, 13, :]
        identb = cb[:, 0, :]
        maskd = cb[:, 1, :]
        masks = cb[:, 2, :]
        onesb = cb[:, 3, :]
        bfbc = smp[:, 0:8]
        gcT = smp[:, 8:16]
        gaT = smp[:, 16:24]
        cwT = smp[:, 24:48]
        kmask = smp[:, 48:80]
        valid = smp[:, 80:112]

        T.dma("sp", cf[:], cf_in, w=["constA"], key="constA")
        T.dma("sp", smp[:], sp_in, w=["constA"], key="constA")
        T.dma("pool", cb[:], cb_in, w=["constB"], key="constB")
        T.dma("pool", wf[:], w_in_v[:, :, 6144:6152], w=["constB"], key="constB")
        CONST = ["constA", "constB"]

        wstate = dict(next=0, released=set(), limit=17)

        def wpump():
            while wstate["next"] < len(WSEQ):
                i = wstate["next"]
                if i >= 4 and (i - 4) not in wstate["released"]:
                    break
                if wstate.get("limit") is not None and i >= wstate["limit"]:
                    break
                src, kcn = WSEQ[i]
                s = i % 4
                dst = wr[s][:].rearrange("p (k n) -> p k n", k=kcn)
                if i == 17:
                    T.dma("pool", dst, src, w=["wr%d" % s, "kp0", "kp1", "kpT0", "kpT1"], key="wr%d" % s)
                elif 8 <= i < 12:
                    T.dma("sp", wr[s][:], w2s[i - 8], r=["w2s"], w=["wr%d" % s], key="wr%d" % s)
                else:
                    T.dma("pool", dst, src, w=["wr%d" % s], key="wr%d" % s)
                wstate["next"] += 1

        def wview(i):
            src, kcn = WSEQ[i]
            return wr[i % 4][:].rearrange("p (k n) -> p k n", k=kcn), "wr%d" % (i % 4)

        def wrel(i):
            wstate["released"].add(i)
            wpump()

        wpump()

        def load_gbc(name):
            T.dma("sp", gbc[:], gb_in[name], w=["gbc"], key="gbc")

        def mm(out, lhsT, rhs, start, stop, r, w, sig=None):
            s = stop if sig is None else sig
            return T.op("pe", lambda e: e.matmul(out, lhsT=lhsT, rhs=rhs, start=start, stop=stop),
                        r=r, w=w, sig=s)

        def tp(out, in_, ident, r, w, sig):
            return T.op("pe", lambda e: e.transpose(out=out, in_=in_, identity=ident), r=r, w=w, sig=sig)

        def rstd_from(ssq, out, tmp, scale, keys):
            T.op("dve", lambda e: e.tensor_scalar(out=tmp, in0=ssq, scalar1=scale, scalar2=EPS,
                                                  op0=ALU.mult, op1=ALU.add), r=keys, w=keys)
            T.op("act", lambda e: e.activation(out=tmp, in_=tmp, func=AF.Ln), r=keys, w=keys)
            T.op("act", lambda e: e.activation(out=out, in_=tmp, func=AF.Exp, scale=-0.5), r=keys, w=keys)

        def norm_pre_g(xap, xkey, xsb, xsbkey, sti):
            sk = "stt%d" % sti
            T.op("dve", lambda e: e.memset(stt[:, sti, 0:4], 0.0), w=[sk])
            T.op("act", lambda e: e.activation(out=xsb, in_=xap, func=AF.Square,
                                               accum_out=stt[:, sti, 0:1]), r=[xkey], w=[sk, xsbkey])
            rstd_from(stt[:, sti, 0:1], stt[:, sti, 2:3], stt[:, sti, 1:2], 1.0 / D, [sk])
            T.op("dve", lambda e: e.scalar_tensor_tensor(out=xsb, in0=xap, scalar=stt[:, sti, 2:3], in1=gbc[:],
                                                         op0=ALU.mult, op1=ALU.mult),
                 r=[xkey, sk, "gbc"], w=[xsbkey])

        def norm_tp_g(xsb, xsbkey, dst3, dstkey):
            for kc in range(KC):
                b = kc // 8
                tp(pb[b][:, (kc % 8) * P:(kc % 8 + 1) * P], xsb[:, kc * P:(kc + 1) * P], identb,
                   r=[xsbkey] + CONST, w=[PB(b)], sig=(kc % 8 == 7))
            T.op("act", lambda e: e.copy(out=dst3[:, 0:8, :], in_=pb[0][:].rearrange("p (k t) -> p k t", k=8)),
                 r=[PB(0)], w=[dstkey])
            T.op("dve", lambda e: e.tensor_copy(out=dst3[:, 8:16, :], in_=pb[1][:].rearrange("p (k t) -> p k t", k=8)),
                 r=[PB(1)], w=[dstkey])

        def norm_tile(xap, xkey, xsb, xsbkey, sti, dst3, dstkey, junk, junkkey):
            norm_pre_g(xap, xkey, xsb, xsbkey, sti)
            norm_tp_g(xsb, xsbkey, dst3, dstkey)

        s123 = ExitStack()
        fst = sb(s123, "fst", [P, 9, 8])
        call = sb(s123, "call", [P, 8, 8])
        lfS = sb(s123, "lfS", [P, 8])
        ktS = sb(s123, "ktS", [P, 8, P], BF16)
        vS = sb(s123, "vS", [P, 8, 129], BF16)
        cpast = sb(s123, "cpast", [P, 16, 32])
        cnewS = sb(s123, "cnewS", [P, 8])
        biasNew = sb(s123, "biasNew", [P, 8])
        cendbc = sb(s123, "cendbc", [P, 32])
        biasP = sb(s123, "biasP", [P, 32, 16])
        cendP = sb(s123, "cendP", [P, 2, 8])
        cm = sb(s123, "cm", [P, NSLOT, 8])
        biasA = sb(s123, "biasA", [P, 2, NSLOT, 8])
        accc = sb(s123, "accc", [P, TO])
        s12 = ExitStack()
        xnT = sb(s12, "xnT", [P, KC, TX], BF16)
        with ExitStack() as s1:
            xst = [sb(s1, "xst%d" % i, [P, D]) for i in range(2)]
            xsbt = [sb(s1, "xsb%d" % i, [P, D], BF16) for i in range(2)]
            kst = [sb(s1, "kst%d" % i, [P, 1024]) for i in range(2)]
            vst = [sb(s1, "vst%d" % i, [P, 1024]) for i in range(2)]
            kbfs = [sb(s1, "kbf%d" % i, [P, 1024], BF16) for i in range(2)]
            vbf1 = sb(s1, "vbf0", [P, 1024], BF16)
            vbf = [vbf1, vbf1]
            ktst = [sb(s1, "ktst%d" % i, [P, 8, P], BF16) for i in range(2)]
            lft = sb(s1, "lft", [P, 4, 8])
            T.op("pool", lambda e: e.memset(vS[:], 1.0), w=["vS"])
            cmkb = sb(s1, "cmkb", [P, 8, 512], BF16)
            mkTs0 = sb(s1, "mkTs0", [P, 4, 4, 256], BF16)
            T.dma("pool", cmkb[:], cmk.rearrange("s (mt p) c -> p (s mt) c", p=P), w=["cmkb"], key="cmkb")

            load_gbc("g_mix")

            tiles = [("own", i) for i in range(8)] + [("smp", 8), ("halo", 0)]

            def xsrc(kind, i):
                if kind == "halo":
                    return x_halo
                return x_own[i]

            wk0, wk0k = wview(0)
            wk1, wk1k = wview(1)
            wv0, wv0k = wview(2)
            wv1, wv1k = wview(3)
            wkv = [(wk0, wk0k), (wk1, wk1k), (wv0, wv0k), (wv1, wv1k)]

            T.dma("sp", xst[0][:], xsrc(*tiles[0]), w=["xst0"], key="xst0")

            NT = len(tiles)

            def tinfo(n):
                kind, i = tiles[n]
                a = n % 2
                col = {"own": i * P, "smp": 1024, "halo": TO}[kind]
                dst3, dk = xnT[:, :, col:col + P], "xnT"
                xT = lambda kc, col=col: xnT[:, kc, col:col + P]
                return kind, i, a, dst3, dk, xT

            def norm_pre(n):
                a = n % 2
                sk = "stt%d" % a
                xk, xbk = "xst%d" % a, "xsb%d" % a
                T.op("dve", lambda e: e.memset(stt[:, a, 0:4], 0.0), w=[sk])
                T.op("act", lambda e: e.activation(out=xsbt[a][:], in_=xst[a][:], func=AF.Square,
                                                   accum_out=stt[:, a, 0:1]), r=[xk], w=[sk, xbk])
                rstd_from(stt[:, a, 0:1], stt[:, a, 2:3], stt[:, a, 1:2], 1.0 / D, [sk])
                T.op("dve", lambda e: e.scalar_tensor_tensor(out=xsbt[a][:], in0=xst[a][:], scalar=stt[:, a, 2:3],
                                                             in1=gbc[:], op0=ALU.mult, op1=ALU.mult),
                     r=[xk, sk, "gbc"], w=[xbk])

            def norm_tp(n):
                kind, i, a, dst3, dk, xT = tinfo(n)
                xbk = "xsb%d" % a
                for kc in range(KC):
                    b = kc // 8
                    tp(pb[b][:, (kc % 8) * P:(kc % 8 + 1) * P], xsbt[a][:, kc * P:(kc + 1) * P], identb,
                       r=[xbk] + CONST, w=[PB(b)], sig=(kc % 8 == 7))
                T.op("act", lambda e: e.copy(out=dst3[:, 0:8, :], in_=pb[0][:].rearrange("p (k t) -> p k t", k=8)),
                     r=[PB(0)], w=[dk])
                T.op("dve", lambda e: e.tensor_copy(out=dst3[:, 8:16, :],
                                                    in_=pb[1][:].rearrange("p (k t) -> p k t", k=8)),
                     r=[PB(1)], w=[dk])

            def lf_of(n):
                kind, i = tiles[n]
                if kind in ("own", "smp"):
                    return fst[:, i, :], "fst"
                return lft[:, n % 4, :], "lft%d" % (n % 4)

            def kbanks(n):
                return (0, 1) if n % 2 == 0 else (2, 3)

            def mainK(n):
                kind, i, a, dst3, dk, xT = tinfo(n)
                for j, (wt, wkk) in enumerate(wkv[0:2]):
                    b = kbanks(n)[j]
                    for kc in range(KC):
                        mm(ps[b][:], xT(kc), wt[:, kc, :], kc == 0, kc == KC - 1, r=[dk, wkk], w=[PS(b)])
                for kc in range(KC):
                    mm(ps[4][:, 0:8], xT(kc), wf[:, kc, :], kc == 0, kc == KC - 1, r=[dk] + CONST, w=[PS(4)])

            def mainV(n):
                kind, i, a, dst3, dk, xT = tinfo(n)
                for j, (wt, wkk) in enumerate(wkv[2:4]):
                    b = kbanks(n)[j]
                    for kc in range(KC):
                        mm(ps[b][:], xT(kc), wt[:, kc, :], kc == 0, kc == KC - 1, r=[dk, wkk], w=[PS(b)])

            def evacK(n):
                kind, i, a, dst3, dk, xT = tinfo(n)
                own = kind in ("own", "smp")
                ka = "kst%d" % a
                b0, b1 = kbanks(n)
                T.op("act", lambda e: e.copy(out=kst[a][:, 0:512], in_=ps[b0][:]), r=[PS(b0)], w=[ka])
                T.op("act", lambda e: e.copy(out=kst[a][:, 512:1024], in_=ps[b1][:]), r=[PS(b1)], w=[ka])
                T.op("dve", lambda e: e.tensor_copy(out=kbfs[a][:], in_=kst[a][:]), r=[ka], w=["kbf%d" % a])
                if own:
                    T.dma("act", k_new[i], kst[a][:], r=[ka], key=ka)
                lf, lfk = lf_of(n)
                T.op("dve", lambda e: e.tensor_tensor(out=lf, in0=ps[4][:, 0:8], in1=bfbc, op=ALU.add),
                     r=[PS(4)] + CONST, w=[lfk])
                T.op("act", lambda e: e.activation(out=lf, in_=lf, func=AF.Exp, scale=-1.0), r=[lfk], w=[lfk])
                T.op("act", lambda e: e.activation(out=lf, in_=lf, func=AF.Ln, bias=1.0), r=[lfk], w=[lfk])
                T.op("dve", lambda e: e.tensor_scalar(out=lf, in0=lf, scalar1=-1.0, scalar2=None, op0=ALU.mult),
                     r=[lfk], w=[lfk])
                if kind == "smp":
                    T.op("dve", lambda e: e.tensor_copy(out=lfS[:], in_=lf), r=[lfk], w=["lfS"])

            def evacV(n):
                kind, i, a, dst3, dk, xT = tinfo(n)
                own = kind in ("own", "smp")
                va = "vst%d" % a
                b0, b1 = kbanks(n)
                T.op("act", lambda e: e.copy(out=vst[a][:, 0:512], in_=ps[b0][:]), r=[PS(b0)], w=[va])
                T.op("act", lambda e: e.copy(out=vst[a][:, 512:1024], in_=ps[b1][:]), r=[PS(b1)], w=[va])
                if own:
                    T.dma("act", v_new[i], vst[a][:], r=[va], key=va)
                if kind == "smp":
                    T.op("dve", lambda e: e.tensor_copy(out=vS[:, :, 0:128],
                                                         in_=vst[a][:].rearrange("p (h d) -> p h d", h=8)),
                         r=[va], w=["vS"])
                else:
                    vk = "vbf0"
                    T.op("dve", lambda e: e.tensor_copy(out=vbf[a][:], in_=vst[a][:]), r=[va], w=[vk])
                    for q in range(2):
                        T.dma("sp", vloc[q][i * P:(i + 1) * P, :], vbf[a][:, q * 512:(q + 1) * 512], r=[vk],
                              w=["kvloc"], key=vk)

            def tail(n):
                kind, i = tiles[n]
                a = n % 2
                for h in range(8):
                    tp(pb[2][:, h * P:(h + 1) * P], kbfs[a][:, h * P:(h + 1) * P], identb,
                       r=["kbf%d" % a] + CONST, w=[PB(2)], sig=(h == 7))
                if kind == "smp":
                    T.op("act", lambda e: e.copy(out=ktS[:], in_=pb[2][:].rearrange("p (k t) -> p k t", k=8)),
                         r=[PB(2)], w=["ktS"])
                    return
                sl = i
                kk = "ktst%d" % a
                T.op("act", lambda e: e.copy(out=ktst[a][:], in_=pb[2][:].rearrange("p (k t) -> p k t", k=8)),
                     r=[PB(2)], w=[kk])
                for q in range(2):
                    T.dma("act", kloc[q].rearrange("(h d) t -> d h t", h=4)[:, :, sl * P:(sl + 1) * P],
                          ktst[a][:, 4 * q:4 * q + 4, :], r=[kk], w=["kvloc"], key=kk)
                lf, lfk = lf_of(n)
                mm(ps[4][:, 8:16], Umat, lf, True, sl == 0, r=[lfk] + CONST, w=[PS(4)])
                if sl > 0:
                    mm(ps[4][:, 8:16], E127, call[:, sl - 1, :], False, True, r=["call"] + CONST, w=[PS(4)])
                T.op("act", lambda e: e.copy(out=call[:, sl, :], in_=ps[4][:, 8:16]), r=[PS(4)], w=["call"])

            T.dma("sp", xst[1][:], xsrc(*tiles[1]), w=["xst1"], key="xst1")
            lp = sb(s1, "lp", [P, 16, 32])

            def cpast_step(kt):
                mm(ps[4][:, 16:48], Umat, lp[:, kt, :], True, kt == 0, r=["lp"] + CONST, w=[PS(4)])
                if kt > 0:
                    mm(ps[4][:, 16:48], E127, cpast[:, kt - 1, :], False, True, r=["cpast"] + CONST, w=[PS(4)])
                T.op("act", lambda e: e.copy(out=cpast[:, kt, :], in_=ps[4][:, 16:48]), r=[PS(4)], w=["cpast"])

            norm_pre(0)
            T.dma("sp", xst[0][:], xsrc(*tiles[2]), w=["xst0"], key="xst0")
            norm_pre(1)
            T.dma("sp", xst[1][:], xsrc(*tiles[3]), w=["xst1"], key="xst1")
            norm_tp(0)
            for n in range(NT):
                kind = tiles[n][0]
                if n + 2 < NT:
                    norm_pre(n + 2)
                    if n + 4 < NT:
                        T.dma("sp", xst[n % 2][:], xsrc(*tiles[n + 4]), w=["xst%d" % (n % 2)], key="xst%d" % (n % 2))
                if n + 1 < NT:
                    norm_tp(n + 1)
                if n == 0:
                    for s2 in range(4):
                        T.dma("sp", lp[:, :, s2 * 8:(s2 + 1) * 8], clogf[s2].rearrange("(kt p) h -> p kt h", p=P),
                              w=["lp"], key="lp")
                if kind != "halo":
                    mainK(n)
                if n == 8:
                    wrel(0)
                    wrel(1)
                if kind != "halo":
                    evacK(n)
                if n >= 1 and tiles[n - 1][0] != "halo":
                    tail(n - 1)
                if 2 <= n <= 9:
                    cpast_step(2 * (n - 2))
                    cpast_step(2 * (n - 2) + 1)
                if n == 3:
                    for s2 in range(4):
                        for mt in range(2):
                            for h in range(4):
                                j = mt * 4 + h
                                tp(pb[2][:, j * P:(j + 1) * P], cmkb[:, s2 * 2 + mt, h * P:(h + 1) * P], identb,
                                   r=["cmkb"] + CONST, w=[PB(2)], sig=(j == 7))
                        T.op("act", lambda e: e.copy(out=mkTs0[:, s2, :, :].rearrange("p h (mt t) -> p mt h t", mt=2),
                                                     in_=pb[2][:].rearrange("p (mt h t) -> p mt h t", mt=2, h=4)),
                             r=[PB(2)], w=["mkTs0"])
                    T.dma("sp", mkTs_d, mkTs0[:].rearrange("p s h t -> p (s h t)"), r=["mkTs0"], w=["mkTs_d"],
                          key="mkTs0")
                if n == 7:
                    T.dma("pool", cmvb_d, cmv.rearrange("s (mt p) c -> p (s mt) c", p=P), r=["kbf1"], w=["cmvb_d"],
                          key="cmvb_d")
                if 3 <= n < 7:
                    T.dma("pool", w2s[n - 3].rearrange("p (kc n) -> p kc n", kc=16), WSEQ[8 + n - 3][0],
                          r=["kbf%d" % (n % 2)], w=["w2s"], key="w2s")
            for n in range(NT - 1):
                mainV(n)
                if n == 8:
                    wrel(2)
                    wrel(3)
                evacV(n)
            T.dma("sp", f_new, fst[:], r=["fst"], key="fst")
            T.dma("sp", floc.rearrange("(i p) h -> p i h", p=P), fst[:, 0:8, :], r=["fst"], w=["floc"], key="fst")

            mm(ps[3][:, 0:8], Ublk, lfS[:], True, False, r=["lfS"] + CONST, w=[PS(3)], sig=False)
            for s2 in range(4):
                mm(ps[3][:, 0:8], Epast(s2), cpast[:, 15, s2 * 8:(s2 + 1) * 8], False, s2 == 3,
                   r=["cpast"] + CONST, w=[PS(3)])
            T.op("act", lambda e: e.copy(out=cnewS[:], in_=ps[3][:, 0:8]), r=[PS(3)], w=["cnewS"])
            mm(ps[3][:, 0:8], EblkEnd, cnewS[:], True, True, r=["cnewS"] + CONST, w=[PS(3)])
            T.op("dve", lambda e: e.tensor_tensor(out=biasNew[:], in0=ps[3][:, 0:8], in1=cnewS[:], op=ALU.subtract),
                 r=[PS(3), "cnewS"], w=["biasNew"])
            for s2 in range(4):
                mm(ps[4][:, s2 * 8:(s2 + 1) * 8], Esel(s2), cnewS[:], True, True, r=["cnewS"] + CONST, w=[PS(4)])
            T.op("act", lambda e: e.copy(out=cendbc[:], in_=ps[4][:, 0:32]), r=[PS(4)], w=["cendbc"])
            T.op("dve", lambda e: e.tensor_tensor(out=biasP[:], in0=cendbc[:].unsqueeze(2).to_broadcast([P, 32, 16]),
                                                  in1=cpast[:].rearrange("p kt c -> p c kt"), op=ALU.subtract),
                 r=["cendbc", "cpast"], w=["biasP"])
            T.barrier()
        alloc_psum(6, 2)
        for q in range(2):
            T.coll("AllGather", kloc_t[q].ap().opt(), kall_t[q].ap().opt(), GROUPS, r=["kvloc"], w=["kvall"], key="cc")
            T.coll("AllGather", vloc_t[q].ap().opt(), vall_t[q].ap().opt(), GROUPS, r=["kvloc"], w=["kvall"], key="cc")
        T.coll("AllGather", floc_t.ap().opt(), fall_t.ap().opt(), GROUPS, r=["floc"], w=["fall"], key="cc")
        T.lw["kvall"] = T.lw["fall"]
        with ExitStack() as s1:
            kst = [sb(s1, "cst%d" % i, [P, 1024]) for i in range(2)]
            qs = [sb(s1, "qs%d" % i, [P, TO], BF16) for i in range(2)]
            ycs = [sb(s1, "ycs%d" % i, [P, TO], BF16) for i in range(2)]
            gcs = sb(s1, "gcs", [P, TO + 2])
            Ub = sb(s1, "Ub", [P, 1026])
            Us = sb(s1, "Us", [P, 4, 34])
            tcv = sb(s1, "tcv", [P, TO])
            yc = sb(s1, "yc", [P, TO])
            sqt = sb(s1, "sqt", [P, TO])
            ulast = sb(s1, "ulast", [P, 8, 10])
            cconvT = sb(s1, "cconvT", [P, 8, 8])

            T.dma("sp", kst[0][0:8, :], cconv, w=["kst0"], key="cst0")
            xm = sb(s1, "xm", [P, D])
            xmb = sb(s1, "xmb", [P, D], BF16)
            memT = sb(s1, "memT", [P, KC, 256], BF16)
            mkb = sb(s1, "mkb", [P, 512], BF16)
            load_gbc("g_mem")
            T.dma("sp", xm[:], mem[0:P, :], w=["xm"], key="xm")
            groups = ((0, 512), (512, 512), (1024, 130))
            for half in range(2):
                base = 4 + 3 * half
                if half == 0:
                    wgb, wgbk = wview(base)
                    wgc, wgck = wview(base + 1)
                    whin, whink = wview(base + 2)
                else:
                    wgc, wgck = wview(base)
                    whin, whink = wview(base + 1)
                    wgb, wgbk = wview(base + 2)
                for sc in range(4):
                    cc = 4 * half + sc
                    w0 = cwT[:, cc * 3 + 0:cc * 3 + 1]
                    w1 = cwT[:, cc * 3 + 1:cc * 3 + 2]
                    w2 = cwT[:, cc * 3 + 2:cc * 3 + 3]
                    bA, bB = ((0, 1, 2), (3, 4, 5)) if cc % 2 == 0 else ((3, 4, 5), (0, 1, 2))
                    for which, wt, wkk, banks in (("gc", wgc, wgck, bA), ("hin", whin, whink, bB),
                                                  ("gb", wgb, wgbk, bA)):
                        for kc in range(KC):
                            for gi, (c0, nn) in enumerate(groups):
                                mm(ps[banks[gi]][:, 0:nn], wt[:, kc, sc * P:(sc + 1) * P], xnT[:, kc, c0:c0 + nn],
                                   kc == 0, kc == KC - 1, r=[wkk, "xnT"], w=[PS(banks[gi])])
                        if which == "gc" and cc == 0:
                            for c8 in range(8):
                                tp(ps[5][:, c8 * 8:(c8 + 1) * 8], kst[0][0:8, c8 * P:(c8 + 1) * P], cf[0:8, 0, 0:8],
                                   r=["kst0"] + CONST, w=[PS(5)], sig=(c8 == 7))
                            T.op("act", lambda e: e.copy(out=cconvT[:],
                                                         in_=ps[5][:, 0:64].rearrange("p (c k) -> p c k", c=8)),
                                 r=[PS(5)], w=["cconvT"])
                        if which == "gc":
                            for gi, (c0, nn) in enumerate(groups):
                                T.op("act", lambda e: e.copy(out=gcs[:, c0:c0 + nn], in_=ps[banks[gi]][:, 0:nn]),
                                     r=[PS(banks[gi])], w=["gcs"])
                        elif which == "hin":
                            T.op("dve", lambda e: e.tensor_tensor(out=Ub[:, 2:514], in0=ps[banks[0]][:, 0:512],
                                                                  in1=gcs[:, 0:512], op=ALU.mult),
                                 r=[PS(banks[0]), "gcs"], w=["Ub"])
                            T.op("dve", lambda e: e.tensor_tensor(out=Ub[:, 514:1026], in0=ps[banks[1]][:, 0:512],
                                                                  in1=gcs[:, 512:1024], op=ALU.mult),
                                 r=[PS(banks[1]), "gcs"], w=["Ub"])
                            T.op("dve", lambda e: e.tensor_tensor(
                                out=Us[:, :, 2:34], in0=ps[banks[2]][:, 0:128].rearrange("p (s t) -> p s t", s=4),
                                in1=gcs[:, 1024:1152].rearrange("p (s t) -> p s t", s=4), op=ALU.mult),
                                r=[PS(banks[2]), "gcs"], w=["Us"])
                            T.op("dve", lambda e: e.tensor_tensor(out=Ub[:, 0:2], in0=ps[banks[2]][:, 128:130],
                                                                  in1=gcs[:, 1152:1154], op=ALU.mult),
                                 r=[PS(banks[2]), "gcs"], w=["Ub"])
                            T.op("dve", lambda e: e.tensor_copy(
                                out=Us[:, :, 0:2], in_=cconvT[:, cc, :].rearrange("p (s r) -> p s r", s=4)),
                                r=["cconvT"], w=["Us"])
                            T.op("act", lambda e: e.mul(out=tcv[:, 0:1024], in_=Ub[:, 0:1024], mul=w0),
                                 r=["Ub"] + CONST, w=["tcv"])
                            T.op("dve", lambda e: e.scalar_tensor_tensor(out=tcv[:, 0:1024], in0=Ub[:, 1:1025],
                                                                         scalar=w1, in1=tcv[:, 0:1024],
                                                                         op0=ALU.mult, op1=ALU.add),
                                 r=["Ub", "tcv"] + CONST, w=["tcv"])
                            T.op("dve", lambda e: e.scalar_tensor_tensor(out=tcv[:, 0:1024], in0=Ub[:, 2:1026],
                                                                         scalar=w2, in1=tcv[:, 0:1024],
                                                                         op0=ALU.mult, op1=ALU.add),
                                 r=["Ub", "tcv"] + CONST, w=["tcv"])
                            tS = tcv[:, 1024:1152].rearrange("p (s t) -> p s t", s=4)
                            T.op("act", lambda e: e.mul(out=tS, in_=Us[:, :, 0:32], mul=w0),
                                 r=["Us"] + CONST, w=["tcv"])
                            T.op("dve", lambda e: e.scalar_tensor_tensor(out=tS, in0=Us[:, :, 1:33], scalar=w1,
                                                                         in1=tS, op0=ALU.mult, op1=ALU.add),
                                 r=["Us", "tcv"] + CONST, w=["tcv"])
                            T.op("dve", lambda e: e.scalar_tensor_tensor(out=tS, in0=Us[:, :, 2:34], scalar=w2,
                                                                         in1=tS, op0=ALU.mult, op1=ALU.add),
                                 r=["Us", "tcv"] + CONST, w=["tcv"])
                            T.op("dve", lambda e: e.tensor_copy(out=ulast[:, cc, 0:2], in_=Ub[:, 1024:1026]),
                                 r=["Ub"], w=["ulast"])
                            T.op("dve", lambda e: e.tensor_copy(
                                out=ulast[:, cc, 2:10].rearrange("p (s r) -> p s r", s=4), in_=Us[:, :, 32:34]),
                                r=["Us"], w=["ulast"])
                        else:
                            for gi, (c0, nn) in enumerate(groups):
                                n2 = min(nn, 512 if gi < 2 else 128)
                                T.op("dve", lambda e: e.tensor_tensor(out=yc[:, c0:c0 + n2], in0=ps[banks[gi]][:, 0:n2],
                                                                      in1=tcv[:, c0:c0 + n2], op=ALU.mult),
                                     r=[PS(banks[gi]), "tcv"], w=["yc"])
                            if cc == 0:
                                T.op("act", lambda e: e.activation(out=accc[:], in_=yc[:], func=AF.Square),
                                     r=["yc"], w=["accc"])
                            else:
                                T.op("act", lambda e: e.activation(out=sqt[:], in_=yc[:], func=AF.Square),
                                     r=["yc"], w=["sqt"])
                                T.op("dve", lambda e: e.tensor_tensor(out=accc[:], in0=accc[:], in1=sqt[:], op=ALU.add),
                                     r=["sqt", "accc"], w=["accc"])
                            yk = "ycs%d" % (cc % 2)
                            T.op("act", lambda e: e.mul(out=ycs[cc % 2][:], in_=yc[:], mul=gcT[:, cc:cc + 1]),
                                 r=["yc"] + CONST, w=[yk])
                            T.dma("act", ycg_d[cc], ycs[cc % 2][:], r=[yk], w=["ycg_d"], key=yk)
                for k in range(3):
                    wrel(base + k)

            mkT0 = ycs[0][:, 0:1024].rearrange("p (h t) -> p h t", h=4)
            mvb0 = ycs[1][:, 0:1024].rearrange("p (m c) -> p m c", m=2)
            norm_pre_g(xm[:], "xm", xmb[:], "xmb", 0)
            for qi in range(2):
                wq, wqk = wview(10 + qi)
                for hh in range(4):
                    h = 4 * qi + hh
                    if h == 2:
                        norm_tp_g(xmb[:], "xmb", memT[:, :, 0:P], "memT")
                        T.dma("sp", xm[:], mem[P:2 * P, :], w=["xm"], key="xm")
                        norm_pre_g(xm[:], "xm", xmb[:], "xmb", 1)
                    if h == 5:
                        norm_tp_g(xmb[:], "xmb", memT[:, :, P:2 * P], "memT")
                    banks = (0, 1, 2) if h % 2 == 0 else (3, 4, 5)
                    for kc in range(KC):
                        for gi, (c0, nn) in enumerate(((0, 512), (512, 512), (1024, 128))):
                            mm(ps[banks[gi]][:, 0:nn], wq[:, kc, hh * P:(hh + 1) * P], xnT[:, kc, c0:c0 + nn],
                               kc == 0, kc == KC - 1, r=[wqk, "xnT"], w=[PS(banks[gi])])
                    for gi, (c0, nn) in enumerate(((0, 512), (512, 512), (1024, 128))):
                        if h % 2 == 0:
                            T.op("act", lambda e: e.copy(out=qs[0][:, c0:c0 + nn], in_=ps[banks[gi]][:, 0:nn]),
                                 r=[PS(banks[gi])], w=["qs0"])
                        else:
                            T.op("dve", lambda e: e.tensor_copy(out=qs[1][:, c0:c0 + nn], in_=ps[banks[gi]][:, 0:nn]),
                                 r=[PS(banks[gi])], w=["qs1"])
                    T.dma("sp", qT_d[h], qs[h % 2][:], r=["qs%d" % (h % 2)], w=["qT_d"], key="qs%d" % (h % 2))
                wrel(10 + qi)
            wkk_, wkkk = wview(12)
            wvv_, wvvk = wview(13)
            for mt in range(2):
                for kc in range(KC):
                    mm(ps[0][:], memT[:, kc, mt * P:(mt + 1) * P], wkk_[:, kc, :], kc == 0, kc == KC - 1,
                       r=["memT", wkkk], w=[PS(0)])
                for kc in range(KC):
                    mm(ps[1][:], memT[:, kc, mt * P:(mt + 1) * P], wvv_[:, kc, :], kc == 0, kc == KC - 1,
                       r=["memT", wvvk], w=[PS(1)])
                T.op("act", lambda e: e.copy(out=sqt[:, 0:512], in_=ps[0][:]), r=[PS(0)], w=["sqt"])
                T.op("act", lambda e: e.copy(out=yc[:, 0:512], in_=ps[1][:]), r=[PS(1)], w=["yc"])
                T.dma("sp", mk_new[mt * P:(mt + 1) * P, :], sqt[:, 0:512], r=["sqt"], key="sqt")
                T.dma("sp", mv_new[mt * P:(mt + 1) * P, :], yc[:, 0:512], r=["yc"], key="yc")
                T.op("dve", lambda e: e.tensor_copy(out=mkb[:], in_=sqt[:, 0:512]), r=["sqt"], w=["mkb"])
                T.op("dve", lambda e: e.tensor_copy(out=mvb0[:, mt, :], in_=yc[:, 0:512]), r=["yc"], w=["ycs1"])
                for h in range(4):
                    tp(pb[0][:, h * P:(h + 1) * P], mkb[:, h * P:(h + 1) * P], identb, r=["mkb"] + CONST, w=[PB(0)],
                       sig=(h == 3))
                T.op("act", lambda e: e.copy(out=mkT0[:, :, mt * P:(mt + 1) * P],
                                             in_=pb[0][:, 0:512].rearrange("p (h t) -> p h t", h=4)),
                     r=[PB(0)], w=["ycs0"])
            wrel(12)
            wrel(13)
            T.dma("sp", mkT_d, ycs[0][:, 0:1024], r=["ycs0"], w=["mkT_d"], key="ycs0")
            T.dma("sp", mvb_d, ycs[1][:, 0:1024], r=["ycs1"], w=["mvb_d"], key="ycs1")

            for cc in range(8):
                b = cc // 4
                tp(ps[b][0:10, (cc % 4) * P:(cc % 4 + 1) * P], ulast[:, cc, :], identf,
                   r=["ulast"] + CONST, w=[PS(b)], sig=(cc % 4 == 3))
            T.op("act", lambda e: e.copy(out=kst[1][0:10, 0:512], in_=ps[0][0:10, :]), r=[PS(0)], w=["kst1"])
            T.op("act", lambda e: e.copy(out=kst[1][0:10, 512:1024], in_=ps[1][0:10, :]), r=[PS(1)], w=["kst1"])
            T.dma("sp", conv_new, kst[1][0:10, :], r=["kst1"], key="cst1")

            T.barrier()
        s12.close()

        with ExitStack() as s3:
            kTh = [sb(s3, "kTh%d" % i, [P, NSLOT * P], BF16) for i in range(2)]
            vh = [sb(s3, "vh%d" % i, [P, NSLOT, P], BF16) for i in range(2)]
            PT = [sb(s3, "PT%d" % i, [P, 512], BF16) for i in range(4)]
            rlt = sb(s3, "rlt", [P, 512])
            yat = sb(s3, "yat", [P, 512])
            sq3 = sb(s3, "sq3", [P, 512])
            acca = sb(s3, "acca", [P, 1024])
            kp = [wr[1][:, i * 2048:(i + 1) * 2048].rearrange("p (k d) -> p k d", k=16) for i in range(2)]
            vp = [sb(s3, "vp%d" % i, [P, 16, 129], BF16) for i in range(2)]
            kpT = [wr[1][:, 4096 + i * 2048:4096 + (i + 1) * 2048] for i in range(2)]
            fl = sb(s3, "fl", [P, NCTX, 8])
            flT = sb(s3, "flT", [NCTX, 1024])
            Lc = sb(s3, "Lc", [P, NCTX, 8])
            Bt = sb(s3, "Bt", [P, NCTX, 8])
            Cex = sb(s3, "Cex", [P, NCTX, 8])
            tmpv = sb(s3, "tmpv", [P, NCTX, 8])
            carry = sb(s3, "carry", [P, 8])
            PTp = [sb(s3, "PTp%d" % i, [P, 16, P], BF16) for i in range(4)]
            PTn = sb(s3, "PTn", [P, P], BF16)
            tmpS = sb(s3, "tmpS", [P, 512])
            yas = sb(s3, "yas", [P, 1024])
            yasb = sb(s3, "yasb", [P, 1024], BF16)
            qh = [sb(s3, "qh%d" % i, [P, TO], BF16) for i in range(2)]
            yags = [sb(s3, "yags%d" % i, [P, 1024], BF16) for i in range(2)]
            yagS = sb(s3, "yagS", [P, 8, P], BF16)
            for i in range(2):
                T.op("dve", lambda e: e.memset(vp[i][:], 1.0), w=["vp%d" % i])
            for i in range(4):
                T.op("act" if i % 2 else "dve", lambda e: e.memzero(PTp[i][:]) if i % 2 else e.memset(PTp[i][:], 0.0),
                     w=["PTp%d" % i])

            ck_v = ck.rearrange("s (kt p) (h d) -> s p kt h d", p=P, h=8)
            cv_v = cv.rearrange("s (kt p) (h d) -> s p kt h d", p=P, h=8)

            def load_head(h):
                a = h % 2
                q, hh = h // 4, h % 4
                kk, vk = "kTh%d" % a, "vh%d" % a
                T.dma("sp", kTh[a][:, 0:3072].rearrange("p (r t) -> p r t", r=3),
                      kall[q].rearrange("(r x) t -> x r t", r=4)[hh * P:(hh + 1) * P, 0:3, :], r=["kvall"], w=[kk], key=kk)
                T.dma("sp", kTh[a][:, 4096:5120], kloc[q][hh * P:(hh + 1) * P, :], r=["kvloc"], w=[kk], key=kk)
                T.dma("sp", vh[a][:, 0:NATT, :], vall[q][0:NATT * P, hh * P:(hh + 1) * P].rearrange("(s t) d -> t s d", t=P),
                      r=["kvall"], w=[vk], key=vk)
                T.dma("sp", vh[a][:, NCTX:NSLOT, :], vloc[q][:, hh * P:(hh + 1) * P].rearrange("(s t) d -> t s d", t=P),
                      r=["kvloc"], w=[vk], key=vk)
                T.dma("sp", qh[a][:], qT_d[h], r=["qT_d"], w=["qh%d" % a], key="qh%d" % a)

            def load_pastK(it):
                h, s2 = it // 4, it % 4
                a = it % 2
                T.dma("pool", kp[a][:], ck_v[s2][:, :, h, :], w=["kp%d" % a], key="kp%d" % a)

            def load_pastV(it):
                h, s2 = it // 4, it % 4
                a = it % 2
                T.dma("pool", vp[a][:, :, 0:128], cv_v[s2][:, :, h, :], w=["vp%d" % a], key="vp%d" % a)

            T.dma("sp", flT[:], fall.rearrange("(s p) h -> s (p h)", p=P), r=["fall"], w=["flT"], key="flT")
            for h in range(8):
                tp(ps[0][:, h * 32:(h + 1) * 32], flT[:].rearrange("s (p h) -> s h p", h=8)[:, h, :], cf[0:32, 0, 0:32],
                   r=["flT"] + CONST, w=[PS(0)], sig=(h == 7))
            T.op("act", lambda e: e.copy(out=fl[:].rearrange("p s h -> p h s"),
                                         in_=ps[0][:, 0:256].rearrange("p (h s) -> p h s", h=8)),
                 r=[PS(0)], w=["fl"])
            load_head(0)
            load_pastK(0)
            load_pastV(0)
            load_pastK(1)
            load_pastV(1)
            mm(ps[0][:, 0:256], Umat, fl[:].rearrange("p s h -> p (s h)"), True, True, r=["fl"] + CONST, w=[PS(0)])
            T.op("act", lambda e: e.copy(out=Lc[:].rearrange("p s h -> p (s h)"), in_=ps[0][:, 0:256]),
                 r=[PS(0)], w=["Lc"])
            mm(ps[1][:, 0:256], E127, Lc[:].rearrange("p s h -> p (s h)"), True, True, r=["Lc"] + CONST, w=[PS(1)])
            T.op("act", lambda e: e.copy(out=Bt[:].rearrange("p s h -> p (s h)"), in_=ps[1][:, 0:256]),
                 r=[PS(1)], w=["Bt"])
            T.op("dve", lambda e: e.memset(Cex[:, 0, :], 0.0), w=["Cex"])
            T.op("dve", lambda e: e.tensor_copy(out=Cex[:, 1:NCTX, :], in_=Bt[:, 0:NCTX - 1, :]), r=["Bt"], w=["Cex"])
            cur, oth, ck_, ok_ = Cex, tmpv, "Cex", "tmpv"
            for k in (1, 2, 4, 8, 16):
                T.op("dve", lambda e: e.tensor_copy(out=oth[:, 0:k, :], in_=cur[:, 0:k, :]), r=[ck_], w=[ok_])
                T.op("dve", lambda e: e.tensor_tensor(out=oth[:, k:NCTX, :], in0=cur[:, k:NCTX, :],
                                                      in1=cur[:, 0:NCTX - k, :], op=ALU.add), r=[ck_], w=[ok_])
                cur, oth, ck_, ok_ = oth, cur, ok_, ck_
            if cur is not Cex:
                T.op("dve", lambda e: e.tensor_copy(out=Cex[:], in_=cur[:]), r=[ck_], w=["Cex"])
            T.op("dve", lambda e: e.tensor_tensor(out=cm[:, 0:NCTX, :], in0=Lc[:], in1=Cex[:], op=ALU.add),
                 r=["Lc", "Cex"], w=["cm"])
            T.op("dve", lambda e: e.tensor_tensor(out=cm[:, 0:NCTX, :], in0=cm[:, 0:NCTX, :],
                                                  in1=kmask.unsqueeze(2).to_broadcast([P, NCTX, 8]), op=ALU.add),
                 r=["cm"] + CONST, w=["cm"])
            T.op("dve", lambda e: e.tensor_tensor(out=tmpv[:], in0=Bt[:],
                                                  in1=valid.unsqueeze(2).to_broadcast([P, NCTX, 8]), op=ALU.mult),
                 r=["Bt"] + CONST, w=["tmpv"])
            T.op("dve", lambda e: e.tensor_reduce(out=carry[:], in_=tmpv[:].rearrange("p s h -> p h s"),
                                                  axis=mybir.AxisListType.X, op=ALU.add), r=["tmpv"], w=["carry"])
            T.op("dve", lambda e: e.tensor_tensor(out=cm[:, NCTX:NSLOT, :], in0=call[:],
                                                  in1=carry[:].unsqueeze(1).to_broadcast([P, 8, 8]), op=ALU.add),
                 r=["call", "carry"], w=["cm"])
            for g in range(2):
                mm(ps[2][:, 0:8], E127, cm[:, NCTX + 4 * g + 3, :], True, True, r=["cm"] + CONST, w=[PS(2)])
                T.op("act", lambda e: e.copy(out=cendP[:, g, :], in_=ps[2][:, 0:8]), r=[PS(2)], w=["cendP"])
            for g in range(2):
                T.op("dve", lambda e: e.tensor_tensor(out=biasA[:, g, :, :],
                                                      in0=cendP[:, g, :].unsqueeze(1).to_broadcast([P, NSLOT, 8]),
                                                      in1=cm[:], op=ALU.subtract),
                     r=["cendP", "cm"], w=["biasA"])

            pti = [0]

            def prompt(h, g):
                a = h % 2
                kT_, kTk, v_, vk = kTh[a], "kTh%d" % a, vh[a], "vh%d" % a
                bO, bL = 2 + 2 * g, 3 + 2 * g
                slots = ([(s, 0) for s in range(NATT)] + [(NCTX + kt, 0) for kt in range(4 * g)]
                         + [(NCTX + 4 * g + r, r * P) for r in range(4)])

                sbanks = (0, 1, 4) if g == 0 else (0, 1, 2)

                def emitS(i):
                    sl, off = slots[i]
                    nn = 512 - off
                    bS = sbanks[i % 3]
                    diag = sl >= NCTX + 4 * g
                    mm(ps[bS][:, 0:nn], kT_[:, sl * P:(sl + 1) * P], qh[a][:, g * 512 + off:(g + 1) * 512],
                       True, not diag, r=[kTk, "qh%d" % a], w=[PS(bS)])
                    if diag:
                        mm(ps[bS][:, 0:P], identb, maskd, False, True, r=CONST, w=[PS(bS)])

                emitS(0)
                emitS(1)
                for i, (sl, off) in enumerate(slots):
                    nn = 512 - off
                    bS = sbanks[i % 3]
                    if i + 2 < len(slots):
                        emitS(i + 2)
                    pk = pti[0] % 4
                    pti[0] += 1
                    T.op("act", lambda e: e.activation(out=PT[pk][:, 0:nn], in_=ps[bS][:, 0:nn], func=AF.Exp,
                                                       bias=biasA[:, g, sl, h:h + 1], scale=SCALE),
                         r=[PS(bS), "biasA"], w=["PT%d" % pk])
                    last = i == len(slots) - 1
                    mm(ps[bO][:, off:512], v_[:, sl, :], PT[pk][:, 0:nn], i == 0, last,
                       r=[vk, "PT%d" % pk], w=[PS(bO)])
                    mm(ps[bL][:, off:512], onesb, PT[pk][:, 0:nn], i == 0, last,
                       r=["PT%d" % pk] + CONST, w=[PS(bL)])
                def epi():
                    T.op("dve", lambda e: e.reciprocal(out=rlt[:], in_=ps[bL][:]), r=[PS(bL)], w=["rlt"])
                    T.op("dve", lambda e: e.tensor_tensor(out=yat[:], in0=ps[bO][:], in1=rlt[:], op=ALU.mult),
                         r=[PS(bO), "rlt"], w=["yat"])
                    cs = slice(g * 512, (g + 1) * 512)
                    if h == 0:
                        T.op("pool", lambda e: e.tensor_tensor(out=acca[:, cs], in0=yat[:], in1=yat[:], op=ALU.mult),
                             r=["yat"], w=["acca"])
                    else:
                        T.op("pool", lambda e: e.tensor_tensor(out=sq3[:], in0=yat[:], in1=yat[:], op=ALU.mult),
                             r=["yat"], w=["sq3"])
                        T.op("pool", lambda e: e.tensor_tensor(out=acca[:, cs], in0=acca[:, cs], in1=sq3[:], op=ALU.add),
                             r=["sq3", "acca"], w=["acca"])
                    T.op("dve", lambda e: e.tensor_scalar(out=yags[a][:, cs], in0=yat[:], scalar1=gaT[:, h:h + 1],
                                                          scalar2=None, op0=ALU.mult),
                         r=["yat"] + CONST, w=["yags%d" % a])
                    if g == 1:
                        T.dma("sp", yag_d[h][:, 0:1024], yags[a][:], r=["yags%d" % a], w=["yag_d"], key="yags%d" % a)
                return epi

            def sA(it):
                h, s2 = it // 4, it % 4
                a = h % 2
                pa = it % 2
                for kt in range(16):
                    b = kt // 8
                    tp(pb[b][:, (kt % 8) * P:(kt % 8 + 1) * P], kp[pa][:, kt, :], identb,
                       r=["kp%d" % pa] + CONST, w=[PB(b)], sig=(kt % 8 == 7))
                if it + 2 < 32:
                    load_pastK(it + 2)
                T.op("act", lambda e: e.copy(out=kpT[pa][:, 0:1024], in_=pb[0][:]), r=[PB(0)], w=["kpT%d" % pa])
                T.op("dve", lambda e: e.tensor_copy(out=kpT[pa][:, 1024:2048], in_=pb[1][:]),
                     r=[PB(1)], w=["kpT%d" % pa])
                bS = it % 2
                for kt in range(16):
                    mm(ps[bS][:, kt * 32:(kt + 1) * 32], kpT[pa][:, kt * P:(kt + 1) * P],
                       qh[a][:, 1024 + s2 * 32:1024 + (s2 + 1) * 32], True, True,
                       r=["kpT%d" % pa, "qh%d" % a], w=[PS(bS)], sig=(kt == 15))
                c = s2 * 8 + h
                T.op("dve", lambda e: e.scalar_tensor_tensor(
                    out=tmpS[:].rearrange("p (k q) -> p k q", k=16),
                    in0=ps[bS][:].rearrange("p (k q) -> p k q", k=16), scalar=SCALE,
                    in1=biasP[:, c, :].unsqueeze(2).to_broadcast([P, 16, 32]), op0=ALU.mult, op1=ALU.add),
                    r=[PS(bS), "biasP"], w=["tmpS"])
                T.op("act", lambda e: e.activation(out=PTp[s2][:, :, s2 * 32:(s2 + 1) * 32],
                                                   in_=tmpS[:].rearrange("p (k q) -> p k q", k=16), func=AF.Exp),
                     r=["tmpS"], w=["PTp%d" % s2])

            def sB(it):
                h, s2 = it // 4, it % 4
                pa = it % 2
                for kt in range(16):
                    mm(ps[3][:, 0:129], PTp[s2][:, kt, :], vp[pa][:, kt, :], s2 == 0 and kt == 0, False,
                       r=["PTp%d" % s2, "vp%d" % pa], w=[PS(3)], sig=(kt == 15))
                if it + 2 < 32:
                    load_pastV(it + 2)

            def snew(h):
                a = h % 2
                mm(ps[0][:, 0:P], ktS[:, h, :], qh[a][:, 1024:1152], True, False, r=["ktS", "qh%d" % a], w=[PS(0)],
                   sig=False)
                mm(ps[0][:, 0:P], identb, masks, False, True, r=CONST, w=[PS(0)])
                T.op("act", lambda e: e.activation(out=PTn[:], in_=ps[0][:, 0:P], func=AF.Exp,
                                                   bias=biasNew[:, h:h + 1], scale=SCALE),
                     r=[PS(0), "biasNew"], w=["PTn"])
                mm(ps[3][:, 0:129], PTn[:], vS[:, h, :], False, True, r=["PTn", "vS"], w=[PS(3)])
                T.op("dve", lambda e: e.reciprocal(out=sstat[:, h:h + 1], in_=ps[3][:, 128:129]),
                     r=[PS(3)], w=["sstat"])
                T.op("dve", lambda e: e.tensor_scalar(out=yas[:, h * P:(h + 1) * P], in0=ps[3][:, 0:128],
                                                      scalar1=sstat[:, h:h + 1], scalar2=None, op0=ALU.mult),
                     r=[PS(3), "sstat"], w=["yas"])

            sA(0)
            for h in range(8):
                if h + 1 < 8:
                    load_head(h + 1)
                e0 = prompt(h, 0)
                sA(4 * h + 1)
                e0()
                sB(4 * h + 0)
                sA(4 * h + 2)
                sB(4 * h + 1)
                e1 = prompt(h, 1)
                sA(4 * h + 3)
                e1()
                sB(4 * h + 2)
                if h + 1 < 8:
                    sA(4 * h + 4)
                sB(4 * h + 3)
                snew(h)
            wstate["limit"] = None
            wpump()

            T.op("dve", lambda e: e.memset(sstat[:, 16:20], 0.0), w=["sstat2"])
            T.op("act", lambda e: e.activation(out=yasb[:], in_=yas[:], func=AF.Square, accum_out=sstat[:, 16:17]),
                 r=["yas"], w=["sstat2", "yasb"])
            T.op("dve", lambda e: e.tensor_copy(out=yasb[:], in_=yas[:]), r=["yas"], w=["yasb"])
            for h in range(8):
                tp(pb[0][:, h * P:(h + 1) * P], yasb[:, h * P:(h + 1) * P], identb, r=["yasb"] + CONST, w=[PB(0)],
                   sig=(h == 7))
            for h in range(8):
                T.op("act", lambda e: e.mul(out=yagS[:, h, :], in_=pb[0][:, h * P:(h + 1) * P],
                                            mul=gaT[:, h:h + 1]), r=[PB(0)] + CONST, w=["yagS"])
            T.dma("sp", yag_d.rearrange("h p t -> p h t")[:, :, 1024:1152], yagS[:], r=["yagS"], w=["yag_d"], key="yagS")
            for t in range(9):
                mm(ps[0][:, t:t + 1], accc[:, t * P:(t + 1) * P], onesf[:, 0:1], True, True,
                   r=["accc"] + CONST, w=[PS(0)], sig=(t == 8))
            for t in range(8):
                mm(ps[1][:, t:t + 1], acca[:, t * P:(t + 1) * P], onesf[:, 0:1], True, True,
                   r=["acca"] + CONST, w=[PS(1)], sig=(t == 7))
            T.op("dve", lambda e: e.tensor_copy(out=sstat[:, 20:28], in_=ps[1][:, 0:8]), r=[PS(1)], w=["sstat3"])
            T.op("dve", lambda e: e.tensor_copy(out=sstat[:, 28:29], in_=sstat[:, 16:17]),
                 r=["sstat2", "sstat3"], w=["sstat3"])
            T.op("dve", lambda e: e.tensor_copy(out=rsc[:, 0:9], in_=ps[0][:, 0:9]), r=[PS(0)], w=["rsc"])
            rstd_from(rsc[:, 0:9], rsc[:, 0:9], rsc[:, 0:9], 1.0 / 1024, ["rsc"])
            rstd_from(sstat[:, 20:29], rsa[:, 0:9], sstat[:, 20:29], 1.0 / 1024, ["sstat3", "rsa"])
            T.barrier()

        s123.close()
        s456 = ExitStack()
        xnT = sb(s456, "xnT2", [P, KC, TO], BF16)
        tg3 = ((0, 384), (384, 384), (768, 384))

        def norm_own_pre(t, a):
            norm_pre_g(xt[a][:], "xt%d" % a, xsb2[a][:], "xsbb%d" % a, a)

        def norm_own_tp(t, a):
            norm_tp_g(xsb2[a][:], "xsbb%d" % a, xnT[:, :, t * P:(t + 1) * P], "xnT2")

        with ExitStack() as s4:
            xt = [sb(s4, "xt%d" % i, [P, D]) for i in range(2)]
            xsb2 = [sb(s4, "xsbb%d" % i, [P, D], BF16) for i in range(2)]
            ycg = sb(s4, "ycg", [P, 8, TO], BF16)
            yag = sb(s4, "yag", [P, 8, TO], BF16)
            T.dma("sp", ycg[:], ycg_d.rearrange("c p t -> p c t"), r=["ycg_d"], w=["ycg"], key="ycg")
            T.dma("sp", yag[:], yag_d.rearrange("c p t -> p c t"), r=["yag_d"], w=["yag"], key="yag")
            load_gbc("g_xattn")
            wos = [wview(14 + nb) for nb in range(4)]
            T.dma("sp", xt[0][:], x_own[0], w=["xt0"], key="xt0")
            for t in range(9):
                a = t % 2
                if t + 1 < 9:
                    T.dma("sp", xt[1 - a][:], x_own[t + 1], w=["xt%d" % (1 - a)], key="xt%d" % (1 - a))
                cs = slice(t * P, (t + 1) * P)
                xk = "xt%d" % a
                for nb in range(4):
                    wo, wok = wos[nb]
                    bA, bB = 2 * (nb % 2), 2 * (nb % 2) + 1
                    for kc in range(8):
                        mm(ps[bA][:], ycg[:, kc, cs], wo[:, kc, :], kc == 0, kc == 7, r=["ycg", wok], w=[PS(bA)])
                    for kc in range(8):
                        mm(ps[bB][:], yag[:, kc, cs], wo[:, 8 + kc, :], kc == 0, kc == 7, r=["yag", wok], w=[PS(bB)])
                    xs_ = xt[a][:, nb * 512:(nb + 1) * 512]
                    T.op("dve", lambda e: e.scalar_tensor_tensor(out=xs_, in0=ps[bA][:], scalar=rsc[:, t:t + 1], in1=xs_,
                                                                 op0=ALU.mult, op1=ALU.add),
                         r=[PS(bA), "rsc", xk], w=[xk])
                    T.op("dve", lambda e: e.scalar_tensor_tensor(out=xs_, in0=ps[bB][:], scalar=rsa[:, t:t + 1], in1=xs_,
                                                                 op0=ALU.mult, op1=ALU.add),
                         r=[PS(bB), "rsa", xk], w=[xk])
                if t >= 1:
                    norm_own_tp(t - 1, 1 - a)
                T.dma("sp", xres_d[t], xt[a][:], r=[xk], w=["xres_d"], key=xk)
                norm_own_pre(t, a)
            norm_own_tp(8, 0)
            for nb in range(4):
                wrel(14 + nb)
            T.barrier()

        with ExitStack() as s5:
            xt = [sb(s5, "xu%d" % i, [P, D]) for i in range(2)]
            xsb2 = [sb(s5, "xsbc%d" % i, [P, D], BF16) for i in range(2)]
            mkT = sb(s5, "mkT", [P, 4, 256], BF16)
            mvb = sb(s5, "mvb", [P, 2, 512], BF16)
            cmvb = sb(s5, "cmvb", [P, 8, 4, 129], BF16)
            mkTs = sb(s5, "mkTs", [P, 4, 4, 256], BF16)
            T.dma("sp", mkT[:].rearrange("p h t -> p (h t)"), mkT_d, r=["mkT_d"], w=["mkT"], key="mkT")
            T.dma("sp", mvb[:].rearrange("p m c -> p (m c)"), mvb_d, r=["mvb_d"], w=["mvb"], key="mvb")
            T.dma("sp", mkTs[:].rearrange("p s h t -> p (s h t)"), mkTs_d, r=["mkTs_d"], w=["mkTs"], key="mkTs")
            T.op("pool", lambda e: e.memset(cmvb[:], 1.0), w=["cmvb"])
            T.dma("sp", cmvb[:].rearrange("p a h d -> p (a h) d")[:, :, 0:128],
                  cmvb_d.rearrange("p a (h d) -> p (a h) d", h=4), r=["cmvb_d"], w=["cmvb"], key="cmvb")

            qxT = sb(s5, "qxT", [P, 4, TO], BF16)
            oxT = sb(s5, "oxT", [P, 4, TO], BF16)
            wxq_, wxqk = wview(18)
            for h in range(4):
                banks = (0, 1, 2) if h % 2 == 0 else (3, 4, 5)
                for kc in range(KC):
                    for gi, (c0, nn) in enumerate(tg3):
                        mm(ps[banks[gi]][:, 0:nn], wxq_[:, kc, h * P:(h + 1) * P], xnT[:, kc, c0:c0 + nn],
                           kc == 0, kc == KC - 1, r=[wxqk, "xnT2"], w=[PS(banks[gi])])
                for gi, (c0, nn) in enumerate(tg3):
                    T.op("act", lambda e: e.copy(out=qxT[:, h, c0:c0 + nn], in_=ps[banks[gi]][:, 0:nn]),
                         r=[PS(banks[gi])], w=["qxT"])
            wrel(18)

            PTx = [sb(s5, "PTx%d" % i, [P, 512], BF16) for i in range(2)]
            rlx = sb(s5, "rlx", [P, 512])
            PTs = [sb(s5, "PTs%d" % i, [P, 2, P], BF16) for i in range(4)]
            oxs = sb(s5, "oxs", [P, 512])
            oxsb = sb(s5, "oxsb", [P, 512], BF16)
            for i in range(4):
                T.op("pool", lambda e: e.memset(PTs[i][:], 0.0), w=["PTs%d" % i])
            xsteps = [(hh_, g_, mt_) for hh_ in range(4) for g_ in range(2) for mt_ in range(2)]

            def xS(i):
                hh_, g_, mt_ = xsteps[i]
                mm(ps[i % 2][:], mkT[:, hh_, mt_ * P:(mt_ + 1) * P], qxT[:, hh_, g_ * 512:(g_ + 1) * 512], True, True,
                   r=["mkT", "qxT"], w=[PS(i % 2)])

            def xstep(i):
                hh_, g_, mt_ = xsteps[i]
                bO, bL = 2 + 2 * g_, 3 + 2 * g_
                if i + 1 < len(xsteps) and xsteps[i + 1][0] == hh_:
                    xS(i + 1)
                T.op("act", lambda e: e.activation(out=PTx[i % 2][:], in_=ps[i % 2][:], func=AF.Exp, scale=SCALE),
                     r=[PS(i % 2)], w=["PTx%d" % (i % 2)])
                mm(ps[bO][:], mvb[:, mt_, hh_ * P:(hh_ + 1) * P], PTx[i % 2][:], mt_ == 0, mt_ == 1,
                   r=["mvb", "PTx%d" % (i % 2)], w=[PS(bO)])
                mm(ps[bL][:], onesb, PTx[i % 2][:], mt_ == 0, mt_ == 1, r=["PTx%d" % (i % 2)] + CONST, w=[PS(bL)])
                if mt_ == 1:
                    T.op("dve", lambda e: e.reciprocal(out=rlx[:], in_=ps[bL][:]), r=[PS(bL)], w=["rlx"])
                    T.op("dve", lambda e: e.tensor_tensor(out=oxT[:, hh_, g_ * 512:(g_ + 1) * 512], in0=ps[bO][:],
                                                          in1=rlx[:], op=ALU.mult), r=[PS(bO), "rlx"], w=["oxT"])

            for h in range(4):
                xS(4 * h)
                for i in range(4 * h, 4 * h + 4):
                    xstep(i)
                bOs = 2 + (h % 2) * 2
                for s2 in range(4):
                    bS = s2 % 2
                    for mt in range(2):
                        mm(ps[bS][:, mt * 32:(mt + 1) * 32], mkTs[:, s2, h, mt * P:(mt + 1) * P],
                           qxT[:, h, 1024 + s2 * 32:1024 + (s2 + 1) * 32], True, True, r=["mkTs", "qxT"], w=[PS(bS)],
                           sig=(mt == 1))
                    T.op("act", lambda e: e.activation(out=PTs[s2][:, :, s2 * 32:(s2 + 1) * 32],
                                                       in_=ps[bS][:, 0:64].rearrange("p (k q) -> p k q", k=2),
                                                       func=AF.Exp, scale=SCALE),
                         r=[PS(bS)], w=["PTs%d" % s2])
                    for mt in range(2):
                        mm(ps[bOs][:, 0:129], PTs[s2][:, mt, :], cmvb[:, s2 * 2 + mt, h, :],
                           s2 == 0 and mt == 0, s2 == 3 and mt == 1, r=["PTs%d" % s2, "cmvb"], w=[PS(bOs)])
                T.op("dve", lambda e: e.reciprocal(out=sstat[:, h:h + 1], in_=ps[bOs][:, 128:129]),
                     r=[PS(bOs)], w=["sstat"])
                T.op("dve", lambda e: e.tensor_scalar(out=oxs[:, h * P:(h + 1) * P], in0=ps[bOs][:, 0:128],
                                                      scalar1=sstat[:, h:h + 1], scalar2=None, op0=ALU.mult),
                     r=[PS(bOs), "sstat"], w=["oxs"])
            T.op("dve", lambda e: e.tensor_copy(out=oxsb[:], in_=oxs[:]), r=["oxs"], w=["oxsb"])
            for h in range(4):
                tp(pb[0][:, h * P:(h + 1) * P], oxsb[:, h * P:(h + 1) * P], identb, r=["oxsb"] + CONST, w=[PB(0)],
                   sig=(h == 3))
            T.op("act", lambda e: e.copy(out=oxT[:, :, 1024:1152], in_=pb[0][:, 0:512].rearrange("p (h t) -> p h t", h=4)),
                 r=[PB(0)], w=["oxT"])

            load_gbc("g_mlp")
            wxo_, wxok = wview(19)
            T.dma("sp", xt[0][:], xres_d[0], r=["xres_d"], w=["xt0"], key="xt0")
            for t in range(9):
                a = t % 2
                if t + 1 < 9:
                    T.dma("sp", xt[1 - a][:], xres_d[t + 1], r=["xres_d"], w=["xt%d" % (1 - a)], key="xt%d" % (1 - a))
                cs = slice(t * P, (t + 1) * P)
                xk = "xt%d" % a
                for nb in range(4):
                    b = nb
                    for kc in range(4):
                        mm(ps[b][:], oxT[:, kc, cs], wxo_[:, kc, nb * 512:(nb + 1) * 512], kc == 0, kc == 3,
                           r=["oxT", wxok], w=[PS(b)])
                    xs_ = xt[a][:, nb * 512:(nb + 1) * 512]
                    T.op("dve", lambda e: e.tensor_tensor(out=xs_, in0=ps[b][:], in1=xs_, op=ALU.add),
                         r=[PS(b), xk], w=[xk])
                if t >= 1:
                    norm_own_tp(t - 1, 1 - a)
                T.dma("sp", xres_d[t], xt[a][:], r=[xk], w=["xres_d"], key=xk)
                norm_own_pre(t, a)
            norm_own_tp(8, 0)
            wrel(19)
            T.barrier()

        with ExitStack() as s6:
            yacc = sb(s6, "yacc", [P, 9, D])
            hT = sb(s6, "hT", [P, 4, TO], BF16)
            rl_ = [sb(s6, "relu%d" % i, [P, 512]) for i in range(2)]
            ri = [0]
            for t in range(9):
                T.dma("sp", yacc[:, t, :], xres_d[t], r=["xres_d"], w=["yacc%d" % t], key="yacc%d" % t)
            load_gbc("g_final")
            T.op("dve", lambda e: e.memset(sstat[:, 0:16], 0.0), w=["sstat"])
            fjunk = xnT[:].rearrange("p k t -> p (k t)")[:, 0:D]

            def final_tile(t):
                yk = "yacc%d" % t
                fk = "fin%d" % t
                T.op("act", lambda e: e.activation(out=fjunk, in_=yacc[:, t, :], func=AF.Square,
                                                   accum_out=sstat[:, t:t + 1]), r=[yk, "sstat"], w=[fk, "xnT2"])
                rstd_from(sstat[:, t:t + 1], sstat[:, 16 + t:17 + t], sstat[:, t:t + 1], 1.0 / D, [fk])
                T.op("pool", lambda e: e.tensor_tensor(out=yacc[:, t, :], in0=yacc[:, t, :], in1=gbc[:],
                                                       op=ALU.mult), r=[yk, "gbc"], w=[yk])
                T.op("act", lambda e: e.mul(out=yacc[:, t, :], in_=yacc[:, t, :], mul=sstat[:, 16 + t:17 + t]),
                     r=[yk, fk], w=[yk])
                T.dma("sp", y_own[t], yacc[:, t, :], r=[yk], key=yk)
            for hb in range(16):
                wu, wuk = wview(20 + 2 * hb)
                wd, wdk = wview(21 + 2 * hb)
                for hc in range(4):
                    banks = (0, 1, 2) if hc % 2 == 0 else (3, 4, 5)
                    for kc in range(KC):
                        for gi, (c0, nn) in enumerate(tg3):
                            mm(ps[banks[gi]][:, 0:nn], wu[:, kc, hc * P:(hc + 1) * P], xnT[:, kc, c0:c0 + nn],
                               kc == 0, kc == KC - 1, r=[wuk, "xnT2"], w=[PS(banks[gi])])
                    for gi, (c0, nn) in enumerate(tg3):
                        rr = ri[0] % 2
                        ri[0] += 1
                        T.op("act", lambda e: e.activation(out=rl_[rr][:, 0:nn], in_=ps[banks[gi]][:, 0:nn], func=AF.Relu),
                             r=[PS(banks[gi])], w=["relu%d" % rr])
                        T.op("dve", lambda e: e.tensor_tensor(out=hT[:, hc, c0:c0 + nn], in0=rl_[rr][:, 0:nn],
                                                              in1=rl_[rr][:, 0:nn], op=ALU.mult),
                             r=["relu%d" % rr], w=["hT%d" % hc])
                wrel(20 + 2 * hb)
                def dmm(t, nb, hcs):
                    b = (t * 4 + nb) % 6
                    for hc in hcs:
                        mm(ps[b][:], hT[:, hc, t * P:(t + 1) * P], wd[:, hc, nb * 512:(nb + 1) * 512], hc == 0, hc == 3,
                           r=["hT%d" % hc, wdk], w=[PS(b)])

                def dadd(t, nb):
                    b = (t * 4 + nb) % 6
                    yk = "yacc%d" % t
                    ys_ = yacc[:, t, nb * 512:(nb + 1) * 512]
                    T.op("dve", lambda e: e.tensor_tensor(out=ys_, in0=ps[b][:], in1=ys_, op=ALU.add),
                         r=[PS(b), yk], w=[yk])

                grp = [(t, nb) for t in range(9) for nb in range(4)]
                for (t, nb) in grp[:6]:
                    dmm(t, nb, (0, 1, 2))
                for (t, nb) in grp[:6]:
                    dmm(t, nb, (3,))
                    dadd(t, nb)
                    if hb == 15 and nb == 3:
                        final_tile(t)
                for (t, nb) in grp[6:]:
                    dmm(t, nb, (0, 1, 2, 3))
                    dadd(t, nb)
                    if hb == 15 and nb == 3:
                        final_tile(t)
                wrel(21 + 2 * hb)

            T.barrier()
        s456.close()
        pstk[0].close()
    return nc


def _consts():
    s = np.arange(P)[:, None]
    t = np.arange(P)[None, :]
    cf = np.zeros((P, 14, P), np.float32)
    cf[:, 0] = np.eye(P)
    cf[:, 1] = (s <= t)
    cf[:, 2] = (s == 127) * np.ones((1, P))
    cf[:, 3] = ((s // 32) == (t // 32)) & (s <= t)
    cf[:, 4] = (s == (t // 32) * 32 + 31)
    for s2 in range(4):
        cf[:, 5 + s2] = (s == 32 * s2 + 31) * np.ones((1, P))
        cf[:, 9 + s2] = (s == 127) & ((t // 32) == s2)
    cf[:, 13] = 1.0
    cb = np.zeros((P, 4, P), np.float32)
    cb[:, 0] = np.eye(P)
    cb[:, 1] = np.where(s <= t, 0.0, NEG)
    cb[:, 2] = np.where(((s // 32) == (t // 32)) & (s <= t), 0.0, NEG)
    cb[:, 3] = 1.0
    return cf, cb


_NC = None


def kernel(x_prompt, x_sample, cache_k, cache_v, cache_logf, cache_conv, cache_mem_k, cache_mem_v, mem_prompt,
           g_mix, w_in, b_f, conv_w, g_conv_out, g_attn_out, w_out, g_xattn, g_mem, w_xq, w_xkv, w_xo,
           g_mlp, w_up, w_down, g_final):
    global _NC
    f = lambda a: np.ascontiguousarray(np.asarray(a, dtype=np.float32))
    x_prompt, x_sample = f(x_prompt), f(x_sample)
    cache_k, cache_v, cache_logf, cache_conv = f(cache_k), f(cache_v), f(cache_logf), f(cache_conv)
    cache_mem_k, cache_mem_v, mem_prompt = f(cache_mem_k), f(cache_mem_v), f(mem_prompt)
    cf, cb = _consts()
    bc = lambda g: np.ascontiguousarray(np.broadcast_to(f(g).reshape(1, D), (P, D)))
    shared = {
        "w_in": f(w_in)[0], "w_out": f(w_out)[0], "w_xq": f(w_xq)[0], "w_xkv": f(w_xkv)[0], "w_xo": f(w_xo)[0],
        "w_up": f(w_up)[0], "w_down": f(w_down)[0],
        "g_mix": bc(g_mix), "g_xattn": bc(g_xattn), "g_mem": bc(g_mem), "g_mlp": bc(g_mlp), "g_final": bc(g_final),
        "cf32": cf, "cb16": cb,
    }
    small_base = np.zeros((P, 112), np.float32)
    small_base[:, 0:8] = f(b_f).reshape(1, 8)
    small_base[:, 8:16] = f(g_conv_out).reshape(8, P).T
    small_base[:, 16:24] = f(g_attn_out).reshape(8, P).T
    small_base[:, 24:48] = f(conv_w).reshape(3, 8, P).transpose(2, 1, 0).reshape(P, 24)
    in_maps = []
    for c in range(8):
        b, j = c // 4, c % 4
        npast = 8 * j
        x_halo = np.zeros((P, D), np.float32)
        if j > 0:
            x_halo[0:2] = x_prompt[b, 1024 * j - 2:1024 * j]
        sm = small_base.copy()
        sm[:, 48 + npast:80] = -NEG
        sm[:, 80:80 + npast] = 1.0
        m = dict(shared)
        m.update({
            "x_own": np.ascontiguousarray(np.concatenate(
                [x_prompt[b, 1024 * j:1024 * (j + 1)].reshape(8, P, D), x_sample[4 * c:4 * c + 4].reshape(1, P, D)], 0)),
            "x_halo": x_halo, "smallp": sm,
            "mem": np.ascontiguousarray(mem_prompt[b]),
            "ck": np.ascontiguousarray(cache_k[0, 4 * c:4 * c + 4].reshape(4, 2048, 1024)),
            "cv": np.ascontiguousarray(cache_v[0, 4 * c:4 * c + 4].reshape(4, 2048, 1024)),
            "clogf": np.ascontiguousarray(cache_logf[0, 4 * c:4 * c + 4]),
            "cconv": np.ascontiguousarray(cache_conv[0, 4 * c:4 * c + 4].reshape(8, 1024)),
            "cmk": np.ascontiguousarray(cache_mem_k[0, 4 * c:4 * c + 4].reshape(4, 256, 512)),
            "cmv": np.ascontiguousarray(cache_mem_v[0, 4 * c:4 * c + 4].reshape(4, 256, 512)),
        })
        in_maps.append(m)
    if _NC is None:
        _NC = build()
    res = run_bass_kernel_spmd(_NC, in_maps, core_ids=list(range(8))).results

    y_prompt = np.zeros((2, 4096, D), np.float32)
    y_sample = np.zeros((32, 32, D), np.float32)
    nkp = np.zeros((1, 2, 4096, 8, 128), np.float32)
    nvp = np.zeros((1, 2, 4096, 8, 128), np.float32)
    nfp = np.zeros((1, 2, 4096, 8), np.float32)
    ncp = np.zeros((1, 2, 2, 1024), np.float32)
    nmk = np.zeros((1, 2, 256, 4, 128), np.float32)
    nmv = np.zeros((1, 2, 256, 4, 128), np.float32)
    nks = np.zeros((1, 32, 32, 8, 128), np.float32)
    nvs = np.zeros((1, 32, 32, 8, 128), np.float32)
    nfs = np.zeros((1, 32, 32, 8), np.float32)
    ncs = np.zeros((1, 32, 2, 1024), np.float32)
    for c in range(8):
        b, j = c // 4, c % 4
        r = res[c]
        sl = slice(1024 * j, 1024 * (j + 1))
        y_prompt[b, sl] = r["y_own"][0:8].reshape(1024, D)
        y_sample[4 * c:4 * c + 4] = r["y_own"][8].reshape(4, 32, D)
        nkp[0, b, sl] = r["k_new"][0:8].reshape(1024, 8, 128)
        nvp[0, b, sl] = r["v_new"][0:8].reshape(1024, 8, 128)
        nks[0, 4 * c:4 * c + 4] = r["k_new"][8].reshape(4, 32, 8, 128)
        nvs[0, 4 * c:4 * c + 4] = r["v_new"][8].reshape(4, 32, 8, 128)
        fn = r["f_new"]
        nfp[0, b, sl] = fn[:, 0:8, :].transpose(1, 0, 2).reshape(1024, 8)
        nfs[0, 4 * c:4 * c + 4] = fn[:, 8, :].reshape(4, 32, 8)
        cn = r["conv_new"].reshape(5, 2, 1024)
        if j == 3:
            ncp[0, b] = cn[0]
        ncs[0, 4 * c:4 * c + 4] = cn[1:5]
        if j == 0:
            nmk[0, b] = r["mk_new"].reshape(256, 4, 128)
            nmv[0, b] = r["mv_new"].reshape(256, 4, 128)
    return (y_prompt, y_sample, nkp, nvp, nfp, ncp, nmk, nmv, nks, nvs, nfs, ncs)
```

```python
import numpy as np
from contextlib import ExitStack
import concourse.bass as bass
import concourse.mybir as mybir
from concourse.bass_utils import run_bass_kernel_spmd

F32 = mybir.dt.float32
BF16 = mybir.dt.bfloat16
AF = mybir.ActivationFunctionType
ALU = mybir.AluOpType
P = 128
D = 2048
KC = 16
NCTX = 32
NSLOT = 40
NATT = 24
TO = 1152
TX = 1280
EPS = 1e-6
SCALE = 128.0 ** -0.5
NEG = -30000.0


class Tracker:
    def __init__(self, nc, st):
        self.nc = nc
        self.st = st
        self.E = {}
        for name, e in (("pe", nc.tensor), ("act", nc.scalar), ("dve", nc.vector),
                        ("pool", nc.gpsimd), ("sp", nc.sync)):
            sem = st.enter_context(nc.semaphore("s_" + name))
            self.E[name] = dict(e=e, sem=sem, n=0, cnt=0, sig=[], seen={})
        self.lw = {}
        self.rd = {}
        self.ds = {}

    def _resolve(self, dep):
        if dep[0] == "d":
            S = self.ds[dep[1]]
            return ("d:" + dep[1], S["sem"], dep[2])
        _, en, idx = dep
        E = self.E[en]
        sig = E["sig"]
        lo, hi = 0, len(sig)
        while lo < hi:
            mid = (lo + hi) // 2
            if sig[mid][0] >= idx:
                hi = mid
            else:
                lo = mid + 1
        if lo >= len(sig):
            raise RuntimeError("dependency on unsignaled op on %s" % en)
        return ("e:" + en, E["sem"], sig[lo][1])

    def _wait(self, en, deps):
        E = self.E[en]
        need = {}
        for dep in deps:
            if dep[0] == "e" and dep[1] == en and en in ("pe", "sp"):
                continue
            sid, sem, val = self._resolve(dep)
            if val > need.get(sid, (None, 0))[1]:
                need[sid] = (sem, val)
        for sid, (sem, val) in need.items():
            if val > E["seen"].get(sid, 0):
                E["e"].wait_ge(sem, val)
                E["seen"][sid] = val

    def _deps(self, r, w):
        deps = []
        for k in r:
            if k in self.lw:
                deps.append(self.lw[k])
        for k in w:
            if k in self.lw:
                deps.append(self.lw[k])
            deps.extend(self.rd.get(k, {}).values())
        return deps

    def _record(self, me, rk, r, w):
        for k in w:
            self.lw[k] = me
            self.rd[k] = {}
        for k in r:
            self.rd.setdefault(k, {})[rk] = me

    def op(self, en, fn, r=(), w=(), sig=True):
        E = self.E[en]
        self._wait(en, self._deps(r, w))
        ins = fn(E["e"])
        idx = E["n"]
        E["n"] += 1
        if sig:
            E["cnt"] += 1
            ins.then_inc(E["sem"], 1)
            E["sig"].append((idx, E["cnt"]))
        self._record(("e", en, idx), "e:" + en, r, w)
        return ins

    def dma(self, qn, out, in_, r=(), w=(), key=None):
        E = self.E[qn]
        self._wait(qn, self._deps(r, w))
        key = key + ("|s" if qn == "pool" else "|h")
        if key not in self.ds:
            self.ds[key] = dict(sem=self.st.enter_context(self.nc.semaphore("d%d" % len(self.ds))), tot=0)
        S = self.ds[key]
        ins = E["e"].dma_start(out=out, in_=in_)
        S["tot"] += 16
        ins.then_inc(S["sem"], 16)
        E["n"] += 1
        self._record(("d", key, S["tot"]), "d:" + key, r, w)
        return ins

    def coll(self, kind, in_ap, out_ap, groups, r=(), w=(), key=None):
        E = self.E["pool"]
        self._wait("pool", self._deps(r, w))
        if key not in self.ds:
            self.ds[key] = dict(sem=self.st.enter_context(self.nc.semaphore("d%d" % len(self.ds))), tot=0)
        S = self.ds[key]
        ins = E["e"].collective_compute(kind, ALU.bypass, replica_groups=groups, ins=[in_ap], outs=[out_ap],
                                            dma_qos="P3")
        S["tot"] += 1
        ins.then_inc(S["sem"], 1)
        E["n"] += 1
        self._record(("d", key, S["tot"]), "d:" + key, r, w)
        return ins

    def barrier(self):
        for en, E in self.E.items():
            for fn, F in self.E.items():
                if fn == en or fn == "sp" or F["n"] == 0:
                    continue
                if not F["sig"]:
                    continue
                val = F["sig"][-1][1]
                sid = "e:" + fn
                if val > E["seen"].get(sid, 0):
                    E["e"].wait_ge(F["sem"], val)
                    E["seen"][sid] = val
            for k, S in self.ds.items():
                if k.startswith("wr") or k.startswith("w2s") or k == "cc":
                    continue
                sid = "d:" + k
                if S["tot"] > E["seen"].get(sid, 0):
                    E["e"].wait_ge(S["sem"], S["tot"])
                    E["seen"][sid] = S["tot"]

    def check_last_signaled(self):
        for en, E in self.E.items():
            if en == "sp" or E["n"] == 0:
                continue


def build():
    nc = bass.Bass("TRN2", target_bir_lowering=False)

    def din(name, shape):
        return nc.dram_tensor(name, list(shape), F32, kind="ExternalInput").ap()

    def dout(name, shape):
        return nc.dram_tensor(name, list(shape), F32, kind="ExternalOutput").ap()

    x_own = din("x_own", [9, P, D])
    x_halo = din("x_halo", [P, D])
    mem = din("mem", [256, D])
    ck = din("ck", [4, 2048, 1024])
    cv = din("cv", [4, 2048, 1024])
    clogf = din("clogf", [4, 2048, 8])
    cconv = din("cconv", [8, 1024])
    cmk = din("cmk", [4, 256, 512])
    cmv = din("cmv", [4, 256, 512])
    w_in = din("w_in", [D, 6152])
    w_out = din("w_out", [D, D])
    w_xq = din("w_xq", [D, 512])
    w_xkv = din("w_xkv", [D, 1024])
    w_xo = din("w_xo", [512, D])
    w_up = din("w_up", [D, 8192])
    w_down = din("w_down", [8192, D])
    gb_in = {n: din(n, [P, D]) for n in ("g_mix", "g_xattn", "g_mem", "g_mlp", "g_final")}
    cf_in = din("cf32", [P, 14, P])
    cb_in = din("cb16", [P, 4, P])
    sp_in = din("smallp", [P, 112])

    y_own = dout("y_own", [9, P, D])
    k_new = dout("k_new", [9, P, 1024])
    v_new = dout("v_new", [9, P, 1024])
    f_new = dout("f_new", [P, 9, 8])
    conv_new = dout("conv_new", [10, 1024])
    mk_new = dout("mk_new", [256, 512])
    mv_new = dout("mv_new", [256, 512])

    kloc_t = [nc.dram_tensor("kloc%d" % q, [512, 1024], BF16) for q in range(2)]
    kall_t = [nc.dram_tensor("kall%d" % q, [2048, 1024], BF16) for q in range(2)]
    vloc_t = [nc.dram_tensor("vloc%d" % q, [1024, 512], BF16) for q in range(2)]
    vall_t = [nc.dram_tensor("vall%d" % q, [4096, 512], BF16) for q in range(2)]
    floc_t = nc.dram_tensor("floc", [1024, 8], F32)
    fall_t = nc.dram_tensor("fall", [4096, 8], F32)
    kloc = [t.ap() for t in kloc_t]
    kall = [t.ap() for t in kall_t]
    vloc = [t.ap() for t in vloc_t]
    vall = [t.ap() for t in vall_t]
    floc, fall = floc_t.ap(), fall_t.ap()
    GROUPS = [[0, 1, 2, 3], [4, 5, 6, 7]]
    w2s = nc.dram_tensor("w2s", [4, P, 8192], BF16).ap()
    mkT_d = nc.dram_tensor("mkT_d", [P, 1024], BF16).ap()
    mvb_d = nc.dram_tensor("mvb_d", [P, 1024], BF16).ap()
    mkTs_d = nc.dram_tensor("mkTs_d", [P, 4096], BF16).ap()
    cmvb_d = nc.dram_tensor("cmvb_d", [P, 8, 512], BF16).ap()

    w_in_v = w_in.rearrange("(kc p) n -> p kc n", p=P)
    w_out_v = w_out.rearrange("(kc p) n -> p kc n", p=P)
    w_xq_v = w_xq.rearrange("(kc p) n -> p kc n", p=P)
    w_xkv_v = w_xkv.rearrange("(kc p) n -> p kc n", p=P)
    w_xo_v = w_xo.rearrange("(kc p) n -> p kc n", p=P)
    w_up_v = w_up.rearrange("(kc p) n -> p kc n", p=P)
    w_down_v = w_down.rearrange("(kc p) n -> p kc n", p=P)

    WSEQ = []
    for c0 in (4096, 4608, 5120, 5632):
        WSEQ.append((w_in_v[:, :, c0:c0 + 512], 16))
    for c0 in (0, 1024, 2048, 1536, 2560, 512, 3072, 3584):
        WSEQ.append((w_in_v[:, :, c0:c0 + 512], 16))
    WSEQ.append((w_xkv_v[:, :, 0:512], 16))
    WSEQ.append((w_xkv_v[:, :, 512:1024], 16))
    for nb in range(4):
        WSEQ.append((w_out_v[:, :, nb * 512:(nb + 1) * 512], 16))
    WSEQ.append((w_xq_v, 16))
    WSEQ.append((w_xo_v, 4))
    for hb in range(16):
        WSEQ.append((w_up_v[:, :, hb * 512:(hb + 1) * 512], 16))
        WSEQ.append((w_down_v[:, hb * 4:(hb + 1) * 4, :], 4))

    with ExitStack() as st:
        T = Tracker(nc, st)

        def sb(stk, name, shape, dt=F32):
            return stk.enter_context(nc.sbuf_tensor(name, list(shape), dt))

        ps, pb = [], []
        pstk = [ExitStack()]
        pgen = [0]

        def alloc_psum(nf, nbf):
            pstk[0].close()
            pstk[0] = ExitStack()
            g = pgen[0]
            pgen[0] += 1
            ps[:] = [pstk[0].enter_context(nc.psum_tensor("ps%d_%d" % (i, g), [P, 512], F32)) for i in range(nf)]
            pb[:] = [pstk[0].enter_context(nc.psum_tensor("pb%d_%d" % (i, g), [P, 1024], BF16)) for i in range(nbf)]

        alloc_psum(5, 3)
        PS = lambda i: "ps%d" % i
        PB = lambda i: "pb%d" % i

        wr = [sb(st, "wr%d" % i, [P, 8192], BF16) for i in range(4)]
        gbc = sb(st, "gbc", [P, D])
        cf = sb(st, "cf", [P, 14, P])
        cb = sb(st, "cb", [P, 4, P], BF16)
        smp = sb(st, "smp", [P, 112])
        wf = sb(st, "wf", [P, KC, 8], BF16)
        stt = sb(st, "stt", [P, 2, 8])
        rsc = sb(st, "rsc", [P, 16])
        rsa = sb(st, "rsa", [P, 16])
        sstat = sb(st, "sstat", [P, 32])
        qT_d = nc.dram_tensor("qT_d", [8, P, TO], BF16).ap()
        ycg_d = nc.dram_tensor("ycg_d", [8, P, TO], BF16).ap()
        yag_d = nc.dram_tensor("yag_d", [8, P, TO], BF16).ap()
        xres_d = nc.dram_tensor("xres_d", [9, P, D], F32).ap()
        identf = cf[:, 0, :]
        Umat = cf[:, 1, :]
        E127 = cf[:, 2, :]
        Ublk = cf[:, 3, :]
        EblkEnd = cf[:, 4, :]
        Esel = lambda s2: cf[:, 5 + s2, :]
        Epast = lambda s2: cf[:, 9 + s2, :]
        onesf = cf[:, 13, :]
        identb = cb[:, 0, :]
        maskd = cb[:, 1, :]
        masks = cb[:, 2, :]
        onesb = cb[:, 3, :]
        bfbc = smp[:, 0:8]
        gcT = smp[:, 8:16]
        gaT = smp[:, 16:24]
        cwT = smp[:, 24:48]
        kmask = smp[:, 48:80]
        valid = smp[:, 80:112]

        T.dma("sp", cf[:], cf_in, w=["constA"], key="constA")
        T.dma("sp", smp[:], sp_in, w=["constA"], key="constA")
        T.dma("pool", cb[:], cb_in, w=["constB"], key="constB")
        T.dma("pool", wf[:], w_in_v[:, :, 6144:6152], w=["constB"], key="constB")
        CONST = ["constA", "constB"]

        wstate = dict(next=0, released=set(), limit=17)

        def wpump():
            while wstate["next"] < len(WSEQ):
                i = wstate["next"]
                if i >= 4 and (i - 4) not in wstate["released"]:
                    break
                if wstate.get("limit") is not None and i >= wstate["limit"]:
                    break
                src, kcn = WSEQ[i]
                s = i % 4
                dst = wr[s][:].rearrange("p (k n) -> p k n", k=kcn)
                if i == 17:
                    T.dma("pool", dst, src, w=["wr%d" % s, "kp0", "kp1", "kpT0", "kpT1"], key="wr%d" % s)
                elif 8 <= i < 12:
                    T.dma("sp", wr[s][:], w2s[i - 8], r=["w2s"], w=["wr%d" % s], key="wr%d" % s)
                else:
                    T.dma("pool", dst, src, w=["wr%d" % s], key="wr%d" % s)
                wstate["next"] += 1

        def wview(i):
            src, kcn = WSEQ[i]
            return wr[i % 4][:].rearrange("p (k n) -> p k n", k=kcn), "wr%d" % (i % 4)

        def wrel(i):
            wstate["released"].add(i)
            wpump()

        wpump()

        def load_gbc(name):
            T.dma("sp", gbc[:], gb_in[name], w=["gbc"], key="gbc")

        def mm(out, lhsT, rhs, start, stop, r, w, sig=None):
            s = stop if sig is None else sig
            return T.op("pe", lambda e: e.matmul(out, lhsT=lhsT, rhs=rhs, start=start, stop=stop),
                        r=r, w=w, sig=s)

        def tp(out, in_, ident, r, w, sig):
            return T.op("pe", lambda e: e.transpose(out=out, in_=in_, identity=ident), r=r, w=w, sig=sig)

        def rstd_from(ssq, out, tmp, scale, keys):
            T.op("dve", lambda e: e.tensor_scalar(out=tmp, in0=ssq, scalar1=scale, scalar2=EPS,
                                                  op0=ALU.mult, op1=ALU.add), r=keys, w=keys)
            T.op("act", lambda e: e.activation(out=tmp, in_=tmp, func=AF.Ln), r=keys, w=keys)
            T.op("act", lambda e: e.activation(out=out, in_=tmp, func=AF.Exp, scale=-0.5), r=keys, w=keys)

        def norm_pre_g(xap, xkey, xsb, xsbkey, sti):
            sk = "stt%d" % sti
            T.op("dve", lambda e: e.memset(stt[:, sti, 0:4], 0.0), w=[sk])
            T.op("act", lambda e: e.activation(out=xsb, in_=xap, func=AF.Square,
                                               accum_out=stt[:, sti, 0:1]), r=[xkey], w=[sk, xsbkey])
            rstd_from(stt[:, sti, 0:1], stt[:, sti, 2:3], stt[:, sti, 1:2], 1.0 / D, [sk])
            T.op("dve", lambda e: e.scalar_tensor_tensor(out=xsb, in0=xap, scalar=stt[:, sti, 2:3], in1=gbc[:],
                                                         op0=ALU.mult, op1=ALU.mult),
                 r=[xkey, sk, "gbc"], w=[xsbkey])

        def norm_tp_g(xsb, xsbkey, dst3, dstkey):
            for kc in range(KC):
                b = kc // 8
                tp(pb[b][:, (kc % 8) * P:(kc % 8 + 1) * P], xsb[:, kc * P:(kc + 1) * P], identb,
                   r=[xsbkey] + CONST, w=[PB(b)], sig=(kc % 8 == 7))
            T.op("act", lambda e: e.copy(out=dst3[:, 0:8, :], in_=pb[0][:].rearrange("p (k t) -> p k t", k=8)),
                 r=[PB(0)], w=[dstkey])
            T.op("dve", lambda e: e.tensor_copy(out=dst3[:, 8:16, :], in_=pb[1][:].rearrange("p (k t) -> p k t", k=8)),
                 r=[PB(1)], w=[dstkey])

        def norm_tile(xap, xkey, xsb, xsbkey, sti, dst3, dstkey, junk, junkkey):
            norm_pre_g(xap, xkey, xsb, xsbkey, sti)
            norm_tp_g(xsb, xsbkey, dst3, dstkey)

        s123 = ExitStack()
        fst = sb(s123, "fst", [P, 9, 8])
        call = sb(s123, "call", [P, 8, 8])
        lfS = sb(s123, "lfS", [P, 8])
        ktS = sb(s123, "ktS", [P, 8, P], BF16)
        vS = sb(s123, "vS", [P, 8, 129], BF16)
        cpast = sb(s123, "cpast", [P, 16, 32])
        cnewS = sb(s123, "cnewS", [P, 8])
        biasNew = sb(s123, "biasNew", [P, 8])
        cendbc = sb(s123, "cendbc", [P, 32])
        biasP = sb(s123, "biasP", [P, 32, 16])
        cendP = sb(s123, "cendP", [P, 2, 8])
        cm = sb(s123, "cm", [P, NSLOT, 8])
        biasA = sb(s123, "biasA", [P, 2, NSLOT, 8])
        accc = sb(s123, "accc", [P, TO])
        s12 = ExitStack()
        xnT = sb(s12, "xnT", [P, KC, TX], BF16)
        with ExitStack() as s1:
            xst = [sb(s1, "xst%d" % i, [P, D]) for i in range(2)]
            xsbt = [sb(s1, "xsb%d" % i, [P, D], BF16) for i in range(2)]
            kst = [sb(s1, "kst%d" % i, [P, 1024]) for i in range(2)]
            vst = [sb(s1, "vst%d" % i, [P, 1024]) for i in range(2)]
            kbfs = [sb(s1, "kbf%d" % i, [P, 1024], BF16) for i in range(2)]
            vbf1 = sb(s1, "vbf0", [P, 1024], BF16)
            vbf = [vbf1, vbf1]
            ktst = [sb(s1, "ktst%d" % i, [P, 8, P], BF16) for i in range(2)]
            lft = sb(s1, "lft", [P, 4, 8])
            T.op("pool", lambda e: e.memset(vS[:], 1.0), w=["vS"])
            cmkb = sb(s1, "cmkb", [P, 8, 512], BF16)
            mkTs0 = sb(s1, "mkTs0", [P, 4, 4, 256], BF16)
            T.dma("pool", cmkb[:], cmk.rearrange("s (mt p) c -> p (s mt) c", p=P), w=["cmkb"], key="cmkb")

            load_gbc("g_mix")

            tiles = [("own", i) for i in range(8)] + [("smp", 8), ("halo", 0)]

            def xsrc(kind, i):
                if kind == "halo":
                    return x_halo
                return x_own[i]

            wk0, wk0k = wview(0)
            wk1, wk1k = wview(1)
            wv0, wv0k = wview(2)
            wv1, wv1k = wview(3)
            wkv = [(wk0, wk0k), (wk1, wk1k), (wv0, wv0k), (wv1, wv1k)]

            T.dma("sp", xst[0][:], xsrc(*tiles[0]), w=["xst0"], key="xst0")

            NT = len(tiles)

            def tinfo(n):
                kind, i = tiles[n]
                a = n % 2
                col = {"own": i * P, "smp": 1024, "halo": TO}[kind]
                dst3, dk = xnT[:, :, col:col + P], "xnT"
                xT = lambda kc, col=col: xnT[:, kc, col:col + P]
                return kind, i, a, dst3, dk, xT

            def norm_pre(n):
                a = n % 2
                sk = "stt%d" % a
                xk, xbk = "xst%d" % a, "xsb%d" % a
                T.op("dve", lambda e: e.memset(stt[:, a, 0:4], 0.0), w=[sk])
                T.op("act", lambda e: e.activation(out=xsbt[a][:], in_=xst[a][:], func=AF.Square,
                                                   accum_out=stt[:, a, 0:1]), r=[xk], w=[sk, xbk])
                rstd_from(stt[:, a, 0:1], stt[:, a, 2:3], stt[:, a, 1:2], 1.0 / D, [sk])
                T.op("dve", lambda e: e.scalar_tensor_tensor(out=xsbt[a][:], in0=xst[a][:], scalar=stt[:, a, 2:3],
                                                             in1=gbc[:], op0=ALU.mult, op1=ALU.mult),
                     r=[xk, sk, "gbc"], w=[xbk])

            def norm_tp(n):
                kind, i, a, dst3, dk, xT = tinfo(n)
                xbk = "xsb%d" % a
                for kc in range(KC):
                    b = kc // 8
                    tp(pb[b][:, (kc % 8) * P:(kc % 8 + 1) * P], xsbt[a][:, kc * P:(kc + 1) * P], identb,
                       r=[xbk] + CONST, w=[PB(b)], sig=(kc % 8 == 7))
                T.op("act", lambda e: e.copy(out=dst3[:, 0:8, :], in_=pb[0][:].rearrange("p (k t) -> p k t", k=8)),
                     r=[PB(0)], w=[dk])
                T.op("dve", lambda e: e.tensor_copy(out=dst3[:, 8:16, :],
                                                    in_=pb[1][:].rearrange("p (k t) -> p k t", k=8)),
                     r=[PB(1)], w=[dk])

            def lf_of(n):
                kind, i = tiles[n]
                if kind in ("own", "smp"):
                    return fst[:, i, :], "fst"
                return lft[:, n % 4, :], "lft%d" % (n % 4)

            def kbanks(n):
                return (0, 1) if n % 2 == 0 else (2, 3)

            def mainK(n):
                kind, i, a, dst3, dk, xT = tinfo(n)
                for j, (wt, wkk) in enumerate(wkv[0:2]):
                    b = kbanks(n)[j]
                    for kc in range(KC):
                        mm(ps[b][:], xT(kc), wt[:, kc, :], kc == 0, kc == KC - 1, r=[dk, wkk], w=[PS(b)])
                for kc in range(KC):
                    mm(ps[4][:, 0:8], xT(kc), wf[:, kc, :], kc == 0, kc == KC - 1, r=[dk] + CONST, w=[PS(4)])

            def mainV(n):
                kind, i, a, dst3, dk, xT = tinfo(n)
                for j, (wt, wkk) in enumerate(wkv[2:4]):
                    b = kbanks(n)[j]
                    for kc in range(KC):
                        mm(ps[b][:], xT(kc), wt[:, kc, :], kc == 0, kc == KC - 1, r=[dk, wkk], w=[PS(b)])

            def evacK(n):
                kind, i, a, dst3, dk, xT = tinfo(n)
                own = kind in ("own", "smp")
                ka = "kst%d" % a
                b0, b1 = kbanks(n)
                T.op("act", lambda e: e.copy(out=kst[a][:, 0:512], in_=ps[b0][:]), r=[PS(b0)], w=[ka])
                T.op("act", lambda e: e.copy(out=kst[a][:, 512:1024], in_=ps[b1][:]), r=[PS(b1)], w=[ka])
                T.op("dve", lambda e: e.tensor_copy(out=kbfs[a][:], in_=kst[a][:]), r=[ka], w=["kbf%d" % a])
                if own:
                    T.dma("act", k_new[i], kst[a][:], r=[ka], key=ka)
                lf, lfk = lf_of(n)
                T.op("dve", lambda e: e.tensor_tensor(out=lf, in0=ps[4][:, 0:8], in1=bfbc, op=ALU.add),
                     r=[PS(4)] + CONST, w=[lfk])
                T.op("act", lambda e: e.activation(out=lf, in_=lf, func=AF.Exp, scale=-1.0), r=[lfk], w=[lfk])
                T.op("act", lambda e: e.activation(out=lf, in_=lf, func=AF.Ln, bias=1.0), r=[lfk], w=[lfk])
                T.op("dve", lambda e: e.tensor_scalar(out=lf, in0=lf, scalar1=-1.0, scalar2=None, op0=ALU.mult),
                     r=[lfk], w=[lfk])
                if kind == "smp":
                    T.op("dve", lambda e: e.tensor_copy(out=lfS[:], in_=lf), r=[lfk], w=["lfS"])

            def evacV(n):
                kind, i, a, dst3, dk, xT = tinfo(n)
                own = kind in ("own", "smp")
                va = "vst%d" % a
                b0, b1 = kbanks(n)
                T.op("act", lambda e: e.copy(out=vst[a][:, 0:512], in_=ps[b0][:]), r=[PS(b0)], w=[va])
                T.op("act", lambda e: e.copy(out=vst[a][:, 512:1024], in_=ps[b1][:]), r=[PS(b1)], w=[va])
                if own:
                    T.dma("act", v_new[i], vst[a][:], r=[va], key=va)
                if kind == "smp":
                    T.op("dve", lambda e: e.tensor_copy(out=vS[:, :, 0:128],
                                                         in_=vst[a][:].rearrange("p (h d) -> p h d", h=8)),
                         r=[va], w=["vS"])
                else:
                    vk = "vbf0"
                    T.op("dve", lambda e: e.tensor_copy(out=vbf[a][:], in_=vst[a][:]), r=[va], w=[vk])
                    for q in range(2):
                        T.dma("sp", vloc[q][i * P:(i + 1) * P, :], vbf[a][:, q * 512:(q + 1) * 512], r=[vk],
                              w=["kvloc"], key=vk)

            def tail(n):
                kind, i = tiles[n]
                a = n % 2
                for h in range(8):
                    tp(pb[2][:, h * P:(h + 1) * P], kbfs[a][:, h * P:(h + 1) * P], identb,
                       r=["kbf%d" % a] + CONST, w=[PB(2)], sig=(h == 7))
                if kind == "smp":
                    T.op("act", lambda e: e.copy(out=ktS[:], in_=pb[2][:].rearrange("p (k t) -> p k t", k=8)),
                         r=[PB(2)], w=["ktS"])
                    return
                sl = i
                kk = "ktst%d" % a
                T.op("act", lambda e: e.copy(out=ktst[a][:], in_=pb[2][:].rearrange("p (k t) -> p k t", k=8)),
                     r=[PB(2)], w=[kk])
                for q in range(2):
                    T.dma("sp", kloc[q].rearrange("(h d) t -> d h t", h=4)[:, :, sl * P:(sl + 1) * P],
                          ktst[a][:, 4 * q:4 * q + 4, :], r=[kk], w=["kvloc"], key=kk)
                lf, lfk = lf_of(n)
                mm(ps[4][:, 8:16], Umat, lf, True, sl == 0, r=[lfk] + CONST, w=[PS(4)])
                if sl > 0:
                    mm(ps[4][:, 8:16], E127, call[:, sl - 1, :], False, True, r=["call"] + CONST, w=[PS(4)])
                T.op("act", lambda e: e.copy(out=call[:, sl, :], in_=ps[4][:, 8:16]), r=[PS(4)], w=["call"])

            T.dma("sp", xst[1][:], xsrc(*tiles[1]), w=["xst1"], key="xst1")
            lp = sb(s1, "lp", [P, 16, 32])

            def cpast_step(kt):
                mm(ps[4][:, 16:48], Umat, lp[:, kt, :], True, kt == 0, r=["lp"] + CONST, w=[PS(4)])
                if kt > 0:
                    mm(ps[4][:, 16:48], E127, cpast[:, kt - 1, :], False, True, r=["cpast"] + CONST, w=[PS(4)])
                T.op("act", lambda e: e.copy(out=cpast[:, kt, :], in_=ps[4][:, 16:48]), r=[PS(4)], w=["cpast"])

            norm_pre(0)
            T.dma("sp", xst[0][:], xsrc(*tiles[2]), w=["xst0"], key="xst0")
            norm_pre(1)
            T.dma("sp", xst[1][:], xsrc(*tiles[3]), w=["xst1"], key="xst1")
            norm_tp(0)
            for n in range(NT):
                kind = tiles[n][0]
                if n + 2 < NT:
                    norm_pre(n + 2)
                    if n + 4 < NT:
                        T.dma("sp", xst[n % 2][:], xsrc(*tiles[n + 4]), w=["xst%d" % (n % 2)], key="xst%d" % (n % 2))
                if n + 1 < NT:
                    norm_tp(n + 1)
                if n == 0:
                    for s2 in range(4):
                        T.dma("sp", lp[:, :, s2 * 8:(s2 + 1) * 8], clogf[s2].rearrange("(kt p) h -> p kt h", p=P),
                              w=["lp"], key="lp")
                if kind != "halo":
                    mainK(n)
                if n == 8:
                    wrel(0)
                    wrel(1)
                if kind != "halo":
                    evacK(n)
                if n >= 1 and tiles[n - 1][0] != "halo":
                    tail(n - 1)
                if 2 <= n <= 9:
                    cpast_step(2 * (n - 2))
                    cpast_step(2 * (n - 2) + 1)
                if n == 3:
                    for s2 in range(4):
                        for mt in range(2):
                            for h in range(4):
                                j = mt * 4 + h
                                tp(pb[2][:, j * P:(j + 1) * P], cmkb[:, s2 * 2 + mt, h * P:(h + 1) * P], identb,
                                   r=["cmkb"] + CONST, w=[PB(2)], sig=(j == 7))
                        T.op("act", lambda e: e.copy(out=mkTs0[:, s2, :, :].rearrange("p h (mt t) -> p mt h t", mt=2),
                                                     in_=pb[2][:].rearrange("p (mt h t) -> p mt h t", mt=2, h=4)),
                             r=[PB(2)], w=["mkTs0"])
                    T.dma("sp", mkTs_d, mkTs0[:].rearrange("p s h t -> p (s h t)"), r=["mkTs0"], w=["mkTs_d"],
                          key="mkTs0")
                if n == 7:
                    T.dma("pool", cmvb_d, cmv.rearrange("s (mt p) c -> p (s mt) c", p=P), r=["kbf1"], w=["cmvb_d"],
                          key="cmvb_d")
                if 3 <= n < 7:
                    T.dma("pool", w2s[n - 3].rearrange("p (kc n) -> p kc n", kc=16), WSEQ[8 + n - 3][0],
                          r=["kbf%d" % (n % 2)], w=["w2s"], key="w2s")
            for n in range(NT - 1):
                mainV(n)
                if n == 8:
                    wrel(2)
                    wrel(3)
                evacV(n)
            T.dma("sp", f_new, fst[:], r=["fst"], key="fst")
            T.dma("sp", floc.rearrange("(i p) h -> p i h", p=P), fst[:, 0:8, :], r=["fst"], w=["floc"], key="fst")

            mm(ps[3][:, 0:8], Ublk, lfS[:], True, False, r=["lfS"] + CONST, w=[PS(3)], sig=False)
            for s2 in range(4):
                mm(ps[3][:, 0:8], Epast(s2), cpast[:, 15, s2 * 8:(s2 + 1) * 8], False, s2 == 3,
                   r=["cpast"] + CONST, w=[PS(3)])
            T.op("act", lambda e: e.copy(out=cnewS[:], in_=ps[3][:, 0:8]), r=[PS(3)], w=["cnewS"])
            mm(ps[3][:, 0:8], EblkEnd, cnewS[:], True, True, r=["cnewS"] + CONST, w=[PS(3)])
            T.op("dve", lambda e: e.tensor_tensor(out=biasNew[:], in0=ps[3][:, 0:8], in1=cnewS[:], op=ALU.subtract),
                 r=[PS(3), "cnewS"], w=["biasNew"])
            for s2 in range(4):
                mm(ps[4][:, s2 * 8:(s2 + 1) * 8], Esel(s2), cnewS[:], True, True, r=["cnewS"] + CONST, w=[PS(4)])
            T.op("act", lambda e: e.copy(out=cendbc[:], in_=ps[4][:, 0:32]), r=[PS(4)], w=["cendbc"])
            T.op("dve", lambda e: e.tensor_tensor(out=biasP[:], in0=cendbc[:].unsqueeze(2).to_broadcast([P, 32, 16]),
                                                  in1=cpast[:].rearrange("p kt c -> p c kt"), op=ALU.subtract),
                 r=["cendbc", "cpast"], w=["biasP"])
            T.barrier()
        alloc_psum(6, 2)
        for q in range(2):
            T.coll("AllGather", kloc_t[q].ap().opt(), kall_t[q].ap().opt(), GROUPS, r=["kvloc"], w=["kvall"], key="cc")
            T.coll("AllGather", vloc_t[q].ap().opt(), vall_t[q].ap().opt(), GROUPS, r=["kvloc"], w=["kvall"], key="cc")
        T.coll("AllGather", floc_t.ap().opt(), fall_t.ap().opt(), GROUPS, r=["floc"], w=["fall"], key="cc")
        T.lw["kvall"] = T.lw["fall"]
        with ExitStack() as s1:
            kst = [sb(s1, "cst%d" % i, [P, 1024]) for i in range(2)]
            qs = [sb(s1, "qs%d" % i, [P, TO], BF16) for i in range(2)]
            ycs = [sb(s1, "ycs%d" % i, [P, TO], BF16) for i in range(2)]
            gcs = sb(s1, "gcs", [P, TO + 2])
            Ub = sb(s1, "Ub", [P, 1026])
            Us = sb(s1, "Us", [P, 4, 34])
            tcv = sb(s1, "tcv", [P, TO])
            yc = sb(s1, "yc", [P, TO])
            sqt = sb(s1, "sqt", [P, TO])
            ulast = sb(s1, "ulast", [P, 8, 10])
            cconvT = sb(s1, "cconvT", [P, 8, 8])

            T.dma("sp", kst[0][0:8, :], cconv, w=["kst0"], key="cst0")
            xm = sb(s1, "xm", [P, D])
            xmb = sb(s1, "xmb", [P, D], BF16)
            memT = sb(s1, "memT", [P, KC, 256], BF16)
            mkb = sb(s1, "mkb", [P, 512], BF16)
            load_gbc("g_mem")
            T.dma("sp", xm[:], mem[0:P, :], w=["xm"], key="xm")
            groups = ((0, 512), (512, 512), (1024, 130))
            for half in range(2):
                base = 4 + 3 * half
                if half == 0:
                    wgb, wgbk = wview(base)
                    wgc, wgck = wview(base + 1)
                    whin, whink = wview(base + 2)
                else:
                    wgc, wgck = wview(base)
                    whin, whink = wview(base + 1)
                    wgb, wgbk = wview(base + 2)
                for sc in range(4):
                    cc = 4 * half + sc
                    w0 = cwT[:, cc * 3 + 0:cc * 3 + 1]
                    w1 = cwT[:, cc * 3 + 1:cc * 3 + 2]
                    w2 = cwT[:, cc * 3 + 2:cc * 3 + 3]
                    bA, bB = ((0, 1, 2), (3, 4, 5)) if cc % 2 == 0 else ((3, 4, 5), (0, 1, 2))
                    for which, wt, wkk, banks in (("gc", wgc, wgck, bA), ("hin", whin, whink, bB),
                                                  ("gb", wgb, wgbk, bA)):
                        for kc in range(KC):
                            for gi, (c0, nn) in enumerate(groups):
                                mm(ps[banks[gi]][:, 0:nn], wt[:, kc, sc * P:(sc + 1) * P], xnT[:, kc, c0:c0 + nn],
                                   kc == 0, kc == KC - 1, r=[wkk, "xnT"], w=[PS(banks[gi])])
                        if which == "gc" and cc == 0:
                            for c8 in range(8):
                                tp(ps[5][:, c8 * 8:(c8 + 1) * 8], kst[0][0:8, c8 * P:(c8 + 1) * P], cf[0:8, 0, 0:8],
                                   r=["kst0"] + CONST, w=[PS(5)], sig=(c8 == 7))
                            T.op("act", lambda e: e.copy(out=cconvT[:],
                                                         in_=ps[5][:, 0:64].rearrange("p (c k) -> p c k", c=8)),
                                 r=[PS(5)], w=["cconvT"])
                        if which == "gc":
                            for gi, (c0, nn) in enumerate(groups):
                                T.op("act", lambda e: e.copy(out=gcs[:, c0:c0 + nn], in_=ps[banks[gi]][:, 0:nn]),
                                     r=[PS(banks[gi])], w=["gcs"])
                        elif which == "hin":
                            T.op("dve", lambda e: e.tensor_tensor(out=Ub[:, 2:514], in0=ps[banks[0]][:, 0:512],
                                                                  in1=gcs[:, 0:512], op=ALU.mult),
                                 r=[PS(banks[0]), "gcs"], w=["Ub"])
                            T.op("dve", lambda e: e.tensor_tensor(out=Ub[:, 514:1026], in0=ps[banks[1]][:, 0:512],
                                                                  in1=gcs[:, 512:1024], op=ALU.mult),
                                 r=[PS(banks[1]), "gcs"], w=["Ub"])
                            T.op("dve", lambda e: e.tensor_tensor(
                                out=Us[:, :, 2:34], in0=ps[banks[2]][:, 0:128].rearrange("p (s t) -> p s t", s=4),
                                in1=gcs[:, 1024:1152].rearrange("p (s t) -> p s t", s=4), op=ALU.mult),
                                r=[PS(banks[2]), "gcs"], w=["Us"])
                            T.op("dve", lambda e: e.tensor_tensor(out=Ub[:, 0:2], in0=ps[banks[2]][:, 128:130],
                                                                  in1=gcs[:, 1152:1154], op=ALU.mult),
                                 r=[PS(banks[2]), "gcs"], w=["Ub"])
                            T.op("dve", lambda e: e.tensor_copy(
                                out=Us[:, :, 0:2], in_=cconvT[:, cc, :].rearrange("p (s r) -> p s r", s=4)),
                                r=["cconvT"], w=["Us"])
                            T.op("act", lambda e: e.mul(out=tcv[:, 0:1024], in_=Ub[:, 0:1024], mul=w0),
                                 r=["Ub"] + CONST, w=["tcv"])
                            T.op("dve", lambda e: e.scalar_tensor_tensor(out=tcv[:, 0:1024], in0=Ub[:, 1:1025],
                                                                         scalar=w1, in1=tcv[:, 0:1024],
                                                                         op0=ALU.mult, op1=ALU.add),
                                 r=["Ub", "tcv"] + CONST, w=["tcv"])
                            T.op("dve", lambda e: e.scalar_tensor_tensor(out=tcv[:, 0:1024], in0=Ub[:, 2:1026],
                                                                         scalar=w2, in1=tcv[:, 0:1024],
                                                                         op0=ALU.mult, op1=ALU.add),
                                 r=["Ub", "tcv"] + CONST, w=["tcv"])
                            tS = tcv[:, 1024:1152].rearrange("p (s t) -> p s t", s=4)
                            T.op("act", lambda e: e.mul(out=tS, in_=Us[:, :, 0:32], mul=w0),
                                 r=["Us"] + CONST, w=["tcv"])
                            T.op("dve", lambda e: e.scalar_tensor_tensor(out=tS, in0=Us[:, :, 1:33], scalar=w1,
                                                                         in1=tS, op0=ALU.mult, op1=ALU.add),
                                 r=["Us", "tcv"] + CONST, w=["tcv"])
                            T.op("dve", lambda e: e.scalar_tensor_tensor(out=tS, in0=Us[:, :, 2:34], scalar=w2,
                                                                         in1=tS, op0=ALU.mult, op1=ALU.add),
                                 r=["Us", "tcv"] + CONST, w=["tcv"])
                            T.op("dve", lambda e: e.tensor_copy(out=ulast[:, cc, 0:2], in_=Ub[:, 1024:1026]),
                                 r=["Ub"], w=["ulast"])
                            T.op("dve", lambda e: e.tensor_copy(
                                out=ulast[:, cc, 2:10].rearrange("p (s r) -> p s r", s=4), in_=Us[:, :, 32:34]),
                                r=["Us"], w=["ulast"])
                        else:
                            for gi, (c0, nn) in enumerate(groups):
                                n2 = min(nn, 512 if gi < 2 else 128)
                                T.op("dve", lambda e: e.tensor_tensor(out=yc[:, c0:c0 + n2], in0=ps[banks[gi]][:, 0:n2],
                                                                      in1=tcv[:, c0:c0 + n2], op=ALU.mult),
                                     r=[PS(banks[gi]), "tcv"], w=["yc"])
                            if cc == 0:
                                T.op("act", lambda e: e.activation(out=accc[:], in_=yc[:], func=AF.Square),
                                     r=["yc"], w=["accc"])
                            else:
                                T.op("act", lambda e: e.activation(out=sqt[:], in_=yc[:], func=AF.Square),
                                     r=["yc"], w=["sqt"])
                                T.op("dve", lambda e: e.tensor_tensor(out=accc[:], in0=accc[:], in1=sqt[:], op=ALU.add),
                                     r=["sqt", "accc"], w=["accc"])
                            yk = "ycs%d" % (cc % 2)
                            T.op("act", lambda e: e.mul(out=ycs[cc % 2][:], in_=yc[:], mul=gcT[:, cc:cc + 1]),
                                 r=["yc"] + CONST, w=[yk])
                            T.dma("act", ycg_d[cc], ycs[cc % 2][:], r=[yk], w=["ycg_d"], key=yk)
                for k in range(3):
                    wrel(base + k)

            mkT0 = ycs[0][:, 0:1024].rearrange("p (h t) -> p h t", h=4)
            mvb0 = ycs[1][:, 0:1024].rearrange("p (m c) -> p m c", m=2)
            norm_pre_g(xm[:], "xm", xmb[:], "xmb", 0)
            for qi in range(2):
                wq, wqk = wview(10 + qi)
                for hh in range(4):
                    h = 4 * qi + hh
                    if h == 2:
                        norm_tp_g(xmb[:], "xmb", memT[:, :, 0:P], "memT")
                        T.dma("sp", xm[:], mem[P:2 * P, :], w=["xm"], key="xm")
                        norm_pre_g(xm[:], "xm", xmb[:], "xmb", 1)
                    if h == 5:
                        norm_tp_g(xmb[:], "xmb", memT[:, :, P:2 * P], "memT")
                    banks = (0, 1, 2) if h % 2 == 0 else (3, 4, 5)
                    for kc in range(KC):
                        for gi, (c0, nn) in enumerate(((0, 512), (512, 512), (1024, 128))):
                            mm(ps[banks[gi]][:, 0:nn], wq[:, kc, hh * P:(hh + 1) * P], xnT[:, kc, c0:c0 + nn],
                               kc == 0, kc == KC - 1, r=[wqk, "xnT"], w=[PS(banks[gi])])
                    for gi, (c0, nn) in enumerate(((0, 512), (512, 512), (1024, 128))):
                        if h % 2 == 0:
                            T.op("act", lambda e: e.copy(out=qs[0][:, c0:c0 + nn], in_=ps[banks[gi]][:, 0:nn]),
                                 r=[PS(banks[gi])], w=["qs0"])
                        else:
                            T.op("dve", lambda e: e.tensor_copy(out=qs[1][:, c0:c0 + nn], in_=ps[banks[gi]][:, 0:nn]),
                                 r=[PS(banks[gi])], w=["qs1"])
                    T.dma("sp", qT_d[h], qs[h % 2][:], r=["qs%d" % (h % 2)], w=["qT_d"], key="qs%d" % (h % 2))
                wrel(10 + qi)
            wkk_, wkkk = wview(12)
            wvv_, wvvk = wview(13)
            for mt in range(2):
                for kc in range(KC):
                    mm(ps[0][:], memT[:, kc, mt * P:(mt + 1) * P], wkk_[:, kc, :], kc == 0, kc == KC - 1,
                       r=["memT", wkkk], w=[PS(0)])
                for kc in range(KC):
                    mm(ps[1][:], memT[:, kc, mt * P:(mt + 1) * P], wvv_[:, kc, :], kc == 0, kc == KC - 1,
                       r=["memT", wvvk], w=[PS(1)])
                T.op("act", lambda e: e.copy(out=sqt[:, 0:512], in_=ps[0][:]), r=[PS(0)], w=["sqt"])
                T.op("act", lambda e: e.copy(out=yc[:, 0:512], in_=ps[1][:]), r=[PS(1)], w=["yc"])
                T.dma("sp", mk_new[mt * P:(mt + 1) * P, :], sqt[:, 0:512], r=["sqt"], key="sqt")
                T.dma("sp", mv_new[mt * P:(mt + 1) * P, :], yc[:, 0:512], r=["yc"], key="yc")
                T.op("dve", lambda e: e.tensor_copy(out=mkb[:], in_=sqt[:, 0:512]), r=["sqt"], w=["mkb"])
                T.op("dve", lambda e: e.tensor_copy(out=mvb0[:, mt, :], in_=yc[:, 0:512]), r=["yc"], w=["ycs1"])
                for h in range(4):
                    tp(pb[0][:, h * P:(h + 1) * P], mkb[:, h * P:(h + 1) * P], identb, r=["mkb"] + CONST, w=[PB(0)],
                       sig=(h == 3))
                T.op("act", lambda e: e.copy(out=mkT0[:, :, mt * P:(mt + 1) * P],
                                             in_=pb[0][:, 0:512].rearrange("p (h t) -> p h t", h=4)),
                     r=[PB(0)], w=["ycs0"])
            wrel(12)
            wrel(13)
            T.dma("sp", mkT_d, ycs[0][:, 0:1024], r=["ycs0"], w=["mkT_d"], key="ycs0")
            T.dma("sp", mvb_d, ycs[1][:, 0:1024], r=["ycs1"], w=["mvb_d"], key="ycs1")

            for cc in range(8):
                b = cc // 4
                tp(ps[b][0:10, (cc % 4) * P:(cc % 4 + 1) * P], ulast[:, cc, :], identf,
                   r=["ulast"] + CONST, w=[PS(b)], sig=(cc % 4 == 3))
            T.op("act", lambda e: e.copy(out=kst[1][0:10, 0:512], in_=ps[0][0:10, :]), r=[PS(0)], w=["kst1"])
            T.op("act", lambda e: e.copy(out=kst[1][0:10, 512:1024], in_=ps[1][0:10, :]), r=[PS(1)], w=["kst1"])
            T.dma("sp", conv_new, kst[1][0:10, :], r=["kst1"], key="cst1")

            T.barrier()
        s12.close()

        with ExitStack() as s3:
            kTh = [sb(s3, "kTh%d" % i, [P, NSLOT * P], BF16) for i in range(2)]
            vh = [sb(s3, "vh%d" % i, [P, NSLOT, P], BF16) for i in range(2)]
            PT = [sb(s3, "PT%d" % i, [P, 512], BF16) for i in range(4)]
            rlt = sb(s3, "rlt", [P, 512])
            yat = sb(s3, "yat", [P, 512])
            sq3 = sb(s3, "sq3", [P, 512])
            acca = sb(s3, "acca", [P, 1024])
            kp = [wr[1][:, i * 2048:(i + 1) * 2048].rearrange("p (k d) -> p k d", k=16) for i in range(2)]
            vp = [sb(s3, "vp%d" % i, [P, 16, 129], BF16) for i in range(2)]
            kpT = [wr[1][:, 4096 + i * 2048:4096 + (i + 1) * 2048] for i in range(2)]
            fl = sb(s3, "fl", [P, NCTX, 8])
            flT = sb(s3, "flT", [NCTX, 1024])
            Lc = sb(s3, "Lc", [P, NCTX, 8])
            Bt = sb(s3, "Bt", [P, NCTX, 8])
            Cex = sb(s3, "Cex", [P, NCTX, 8])
            tmpv = sb(s3, "tmpv", [P, NCTX, 8])
            carry = sb(s3, "carry", [P, 8])
            PTp = [sb(s3, "PTp%d" % i, [P, 16, P], BF16) for i in range(4)]
            PTn = sb(s3, "PTn", [P, P], BF16)
            tmpS = sb(s3, "tmpS", [P, 512])
            yas = sb(s3, "yas", [P, 1024])
            yasb = sb(s3, "yasb", [P, 1024], BF16)
            qh = [sb(s3, "qh%d" % i, [P, TO], BF16) for i in range(2)]
            yags = [sb(s3, "yags%d" % i, [P, 1024], BF16) for i in range(2)]
            yagS = sb(s3, "yagS", [P, 8, P], BF16)
            for i in range(2):
                T.op("dve", lambda e: e.memset(vp[i][:], 1.0), w=["vp%d" % i])
            for i in range(4):
                T.op("act" if i % 2 else "dve", lambda e: e.memzero(PTp[i][:]) if i % 2 else e.memset(PTp[i][:], 0.0),
                     w=["PTp%d" % i])

            ck_v = ck.rearrange("s (kt p) (h d) -> s p kt h d", p=P, h=8)
            cv_v = cv.rearrange("s (kt p) (h d) -> s p kt h d", p=P, h=8)

            def load_head(h):
                a = h % 2
                q, hh = h // 4, h % 4
                kk, vk = "kTh%d" % a, "vh%d" % a
                T.dma("sp", kTh[a][:, 0:3072].rearrange("p (r t) -> p r t", r=3),
                      kall[q].rearrange("(r x) t -> x r t", r=4)[hh * P:(hh + 1) * P, 0:3, :], r=["kvall"], w=[kk], key=kk)
                T.dma("sp", kTh[a][:, 4096:5120], kloc[q][hh * P:(hh + 1) * P, :], r=["kvloc"], w=[kk], key=kk)
                T.dma("sp", vh[a][:, 0:NATT, :], vall[q][0:NATT * P, hh * P:(hh + 1) * P].rearrange("(s t) d -> t s d", t=P),
                      r=["kvall"], w=[vk], key=vk)
                T.dma("sp", vh[a][:, NCTX:NSLOT, :], vloc[q][:, hh * P:(hh + 1) * P].rearrange("(s t) d -> t s d", t=P),
                      r=["kvloc"], w=[vk], key=vk)
                T.dma("sp", qh[a][:], qT_d[h], r=["qT_d"], w=["qh%d" % a], key="qh%d" % a)

            def load_pastK(it):
                h, s2 = it // 4, it % 4
                a = it % 2
                T.dma("pool", kp[a][:], ck_v[s2][:, :, h, :], w=["kp%d" % a], key="kp%d" % a)

            def load_pastV(it):
                h, s2 = it // 4, it % 4
                a = it % 2
                T.dma("pool", vp[a][:, :, 0:128], cv_v[s2][:, :, h, :], w=["vp%d" % a], key="vp%d" % a)

            T.dma("sp", flT[:], fall.rearrange("(s p) h -> s (p h)", p=P), r=["fall"], w=["flT"], key="flT")
            for h in range(8):
                tp(ps[0][:, h * 32:(h + 1) * 32], flT[:].rearrange("s (p h) -> s h p", h=8)[:, h, :], cf[0:32, 0, 0:32],
                   r=["flT"] + CONST, w=[PS(0)], sig=(h == 7))
            T.op("act", lambda e: e.copy(out=fl[:].rearrange("p s h -> p h s"),
                                         in_=ps[0][:, 0:256].rearrange("p (h s) -> p h s", h=8)),
                 r=[PS(0)], w=["fl"])
            load_head(0)
            load_pastK(0)
            load_pastV(0)
            load_pastK(1)
            load_pastV(1)
            mm(ps[0][:, 0:256], Umat, fl[:].rearrange("p s h -> p (s h)"), True, True, r=["fl"] + CONST, w=[PS(0)])
            T.op("act", lambda e: e.copy(out=Lc[:].rearrange("p s h -> p (s h)"), in_=ps[0][:, 0:256]),
                 r=[PS(0)], w=["Lc"])
            mm(ps[1][:, 0:256], E127, Lc[:].rearrange("p s h -> p (s h)"), True, True, r=["Lc"] + CONST, w=[PS(1)])
            T.op("act", lambda e: e.copy(out=Bt[:].rearrange("p s h -> p (s h)"), in_=ps[1][:, 0:256]),
                 r=[PS(1)], w=["Bt"])
            T.op("dve", lambda e: e.memset(Cex[:, 0, :], 0.0), w=["Cex"])
            T.op("dve", lambda e: e.tensor_copy(out=Cex[:, 1:NCTX, :], in_=Bt[:, 0:NCTX - 1, :]), r=["Bt"], w=["Cex"])
            cur, oth, ck_, ok_ = Cex, tmpv, "Cex", "tmpv"
            for k in (1, 2, 4, 8, 16):
                T.op("dve", lambda e: e.tensor_copy(out=oth[:, 0:k, :], in_=cur[:, 0:k, :]), r=[ck_], w=[ok_])
                T.op("dve", lambda e: e.tensor_tensor(out=oth[:, k:NCTX, :], in0=cur[:, k:NCTX, :],
                                                      in1=cur[:, 0:NCTX - k, :], op=ALU.add), r=[ck_], w=[ok_])
                cur, oth, ck_, ok_ = oth, cur, ok_, ck_
            if cur is not Cex:
                T.op("dve", lambda e: e.tensor_copy(out=Cex[:], in_=cur[:]), r=[ck_], w=["Cex"])
            T.op("dve", lambda e: e.tensor_tensor(out=cm[:, 0:NCTX, :], in0=Lc[:], in1=Cex[:], op=ALU.add),
                 r=["Lc", "Cex"], w=["cm"])
            T.op("dve", lambda e: e.tensor_tensor(out=cm[:, 0:NCTX, :], in0=cm[:, 0:NCTX, :],
                                                  in1=kmask.unsqueeze(2).to_broadcast([P, NCTX, 8]), op=ALU.add),
                 r=["cm"] + CONST, w=["cm"])
            T.op("dve", lambda e: e.tensor_tensor(out=tmpv[:], in0=Bt[:],
                                                  in1=valid.unsqueeze(2).to_broadcast([P, NCTX, 8]), op=ALU.mult),
                 r=["Bt"] + CONST, w=["tmpv"])
            T.op("dve", lambda e: e.tensor_reduce(out=carry[:], in_=tmpv[:].rearrange("p s h -> p h s"),
                                                  axis=mybir.AxisListType.X, op=ALU.add), r=["tmpv"], w=["carry"])
            T.op("dve", lambda e: e.tensor_tensor(out=cm[:, NCTX:NSLOT, :], in0=call[:],
                                                  in1=carry[:].unsqueeze(1).to_broadcast([P, 8, 8]), op=ALU.add),
                 r=["call", "carry"], w=["cm"])
            for g in range(2):
                mm(ps[2][:, 0:8], E127, cm[:, NCTX + 4 * g + 3, :], True, True, r=["cm"] + CONST, w=[PS(2)])
                T.op("act", lambda e: e.copy(out=cendP[:, g, :], in_=ps[2][:, 0:8]), r=[PS(2)], w=["cendP"])
            for g in range(2):
                T.op("dve", lambda e: e.tensor_tensor(out=biasA[:, g, :, :],
                                                      in0=cendP[:, g, :].unsqueeze(1).to_broadcast([P, NSLOT, 8]),
                                                      in1=cm[:], op=ALU.subtract),
                     r=["cendP", "cm"], w=["biasA"])

            pti = [0]

            def prompt(h, g):
                a = h % 2
                kT_, kTk, v_, vk = kTh[a], "kTh%d" % a, vh[a], "vh%d" % a
                bO, bL = 2 + 2 * g, 3 + 2 * g
                slots = ([(s, 0) for s in range(NATT)] + [(NCTX + kt, 0) for kt in range(4 * g)]
                         + [(NCTX + 4 * g + r, r * P) for r in range(4)])

                sbanks = (0, 1, 4) if g == 0 else (0, 1, 2)

                def emitS(i):
                    sl, off = slots[i]
                    nn = 512 - off
                    bS = sbanks[i % 3]
                    diag = sl >= NCTX + 4 * g
                    mm(ps[bS][:, 0:nn], kT_[:, sl * P:(sl + 1) * P], qh[a][:, g * 512 + off:(g + 1) * 512],
                       True, not diag, r=[kTk, "qh%d" % a], w=[PS(bS)])
                    if diag:
                        mm(ps[bS][:, 0:P], identb, maskd, False, True, r=CONST, w=[PS(bS)])

                emitS(0)
                emitS(1)
                for i, (sl, off) in enumerate(slots):
                    nn = 512 - off
                    bS = sbanks[i % 3]
                    if i + 2 < len(slots):
                        emitS(i + 2)
                    pk = pti[0] % 4
                    pti[0] += 1
                    T.op("act", lambda e: e.activation(out=PT[pk][:, 0:nn], in_=ps[bS][:, 0:nn], func=AF.Exp,
                                                       bias=biasA[:, g, sl, h:h + 1], scale=SCALE),
                         r=[PS(bS), "biasA"], w=["PT%d" % pk])
                    last = i == len(slots) - 1
                    mm(ps[bO][:, off:512], v_[:, sl, :], PT[pk][:, 0:nn], i == 0, last,
                       r=[vk, "PT%d" % pk], w=[PS(bO)])
                    mm(ps[bL][:, off:512], onesb, PT[pk][:, 0:nn], i == 0, last,
                       r=["PT%d" % pk] + CONST, w=[PS(bL)])
                def epi():
                    T.op("dve", lambda e: e.reciprocal(out=rlt[:], in_=ps[bL][:]), r=[PS(bL)], w=["rlt"])
                    T.op("dve", lambda e: e.tensor_tensor(out=yat[:], in0=ps[bO][:], in1=rlt[:], op=ALU.mult),
                         r=[PS(bO), "rlt"], w=["yat"])
                    cs = slice(g * 512, (g + 1) * 512)
                    if h == 0:
                        T.op("pool", lambda e: e.tensor_tensor(out=acca[:, cs], in0=yat[:], in1=yat[:], op=ALU.mult),
                             r=["yat"], w=["acca"])
                    else:
                        T.op("pool", lambda e: e.tensor_tensor(out=sq3[:], in0=yat[:], in1=yat[:], op=ALU.mult),
                             r=["yat"], w=["sq3"])
                        T.op("pool", lambda e: e.tensor_tensor(out=acca[:, cs], in0=acca[:, cs], in1=sq3[:], op=ALU.add),
                             r=["sq3", "acca"], w=["acca"])
                    T.op("dve", lambda e: e.tensor_scalar(out=yags[a][:, cs], in0=yat[:], scalar1=gaT[:, h:h + 1],
                                                          scalar2=None, op0=ALU.mult),
                         r=["yat"] + CONST, w=["yags%d" % a])
                    if g == 1:
                        T.dma("sp", yag_d[h][:, 0:1024], yags[a][:], r=["yags%d" % a], w=["yag_d"], key="yags%d" % a)
                return epi

            def sA(it):
                h, s2 = it // 4, it % 4
                a = h % 2
                pa = it % 2
                for kt in range(16):
                    b = kt // 8
                    tp(pb[b][:, (kt % 8) * P:(kt % 8 + 1) * P], kp[pa][:, kt, :], identb,
                       r=["kp%d" % pa] + CONST, w=[PB(b)], sig=(kt % 8 == 7))
                if it + 2 < 32:
                    load_pastK(it + 2)
                T.op("act", lambda e: e.copy(out=kpT[pa][:, 0:1024], in_=pb[0][:]), r=[PB(0)], w=["kpT%d" % pa])
                T.op("dve", lambda e: e.tensor_copy(out=kpT[pa][:, 1024:2048], in_=pb[1][:]),
                     r=[PB(1)], w=["kpT%d" % pa])
                bS = it % 2
                for kt in range(16):
                    mm(ps[bS][:, kt * 32:(kt + 1) * 32], kpT[pa][:, kt * P:(kt + 1) * P],
                       qh[a][:, 1024 + s2 * 32:1024 + (s2 + 1) * 32], True, True,
                       r=["kpT%d" % pa, "qh%d" % a], w=[PS(bS)], sig=(kt == 15))
                c = s2 * 8 + h
                T.op("dve", lambda e: e.scalar_tensor_tensor(
                    out=tmpS[:].rearrange("p (k q) -> p k q", k=16),
                    in0=ps[bS][:].rearrange("p (k q) -> p k q", k=16), scalar=SCALE,
                    in1=biasP[:, c, :].unsqueeze(2).to_broadcast([P, 16, 32]), op0=ALU.mult, op1=ALU.add),
                    r=[PS(bS), "biasP"], w=["tmpS"])
                T.op("act", lambda e: e.activation(out=PTp[s2][:, :, s2 * 32:(s2 + 1) * 32],
                                                   in_=tmpS[:].rearrange("p (k q) -> p k q", k=16), func=AF.Exp),
                     r=["tmpS"], w=["PTp%d" % s2])

            def sB(it):
                h, s2 = it // 4, it % 4
                pa = it % 2
                for kt in range(16):
                    mm(ps[3][:, 0:129], PTp[s2][:, kt, :], vp[pa][:, kt, :], s2 == 0 and kt == 0, False,
                       r=["PTp%d" % s2, "vp%d" % pa], w=[PS(3)], sig=(kt == 15))
                if it + 2 < 32:
                    load_pastV(it + 2)

            def snew(h):
                a = h % 2
                mm(ps[0][:, 0:P], ktS[:, h, :], qh[a][:, 1024:1152], True, False, r=["ktS", "qh%d" % a], w=[PS(0)],
                   sig=False)
                mm(ps[0][:, 0:P], identb, masks, False, True, r=CONST, w=[PS(0)])
                T.op("act", lambda e: e.activation(out=PTn[:], in_=ps[0][:, 0:P], func=AF.Exp,
                                                   bias=biasNew[:, h:h + 1], scale=SCALE),
                     r=[PS(0), "biasNew"], w=["PTn"])
                mm(ps[3][:, 0:129], PTn[:], vS[:, h, :], False, True, r=["PTn", "vS"], w=[PS(3)])
                T.op("dve", lambda e: e.reciprocal(out=sstat[:, h:h + 1], in_=ps[3][:, 128:129]),
                     r=[PS(3)], w=["sstat"])
                T.op("dve", lambda e: e.tensor_scalar(out=yas[:, h * P:(h + 1) * P], in0=ps[3][:, 0:128],
                                                      scalar1=sstat[:, h:h + 1], scalar2=None, op0=ALU.mult),
                     r=[PS(3), "sstat"], w=["yas"])

            sA(0)
            for h in range(8):
                if h + 1 < 8:
                    load_head(h + 1)
                e0 = prompt(h, 0)
                sA(4 * h + 1)
                e0()
                sB(4 * h + 0)
                sA(4 * h + 2)
                sB(4 * h + 1)
                e1 = prompt(h, 1)
                sA(4 * h + 3)
                e1()
                sB(4 * h + 2)
                if h + 1 < 8:
                    sA(4 * h + 4)
                sB(4 * h + 3)
                snew(h)
            wstate["limit"] = None
            wpump()

            T.op("dve", lambda e: e.memset(sstat[:, 16:20], 0.0), w=["sstat2"])
            T.op("act", lambda e: e.activation(out=yasb[:], in_=yas[:], func=AF.Square, accum_out=sstat[:, 16:17]),
                 r=["yas"], w=["sstat2", "yasb"])
            T.op("dve", lambda e: e.tensor_copy(out=yasb[:], in_=yas[:]), r=["yas"], w=["yasb"])
            for h in range(8):
                tp(pb[0][:, h * P:(h + 1) * P], yasb[:, h * P:(h + 1) * P], identb, r=["yasb"] + CONST, w=[PB(0)],
                   sig=(h == 7))
            for h in range(8):
                T.op("act", lambda e: e.mul(out=yagS[:, h, :], in_=pb[0][:, h * P:(h + 1) * P],
                                            mul=gaT[:, h:h + 1]), r=[PB(0)] + CONST, w=["yagS"])
            T.dma("sp", yag_d.rearrange("h p t -> p h t")[:, :, 1024:1152], yagS[:], r=["yagS"], w=["yag_d"], key="yagS")
            for t in range(9):
                mm(ps[0][:, t:t + 1], accc[:, t * P:(t + 1) * P], onesf[:, 0:1], True, True,
                   r=["accc"] + CONST, w=[PS(0)], sig=(t == 8))
            for t in range(8):
                mm(ps[1][:, t:t + 1], acca[:, t * P:(t + 1) * P], onesf[:, 0:1], True, True,
                   r=["acca"] + CONST, w=[PS(1)], sig=(t == 7))
            T.op("dve", lambda e: e.tensor_copy(out=sstat[:, 20:28], in_=ps[1][:, 0:8]), r=[PS(1)], w=["sstat3"])
            T.op("dve", lambda e: e.tensor_copy(out=sstat[:, 28:29], in_=sstat[:, 16:17]),
                 r=["sstat2", "sstat3"], w=["sstat3"])
            T.op("dve", lambda e: e.tensor_copy(out=rsc[:, 0:9], in_=ps[0][:, 0:9]), r=[PS(0)], w=["rsc"])
            rstd_from(rsc[:, 0:9], rsc[:, 0:9], rsc[:, 0:9], 1.0 / 1024, ["rsc"])
            rstd_from(sstat[:, 20:29], rsa[:, 0:9], sstat[:, 20:29], 1.0 / 1024, ["sstat3", "rsa"])
            T.barrier()

        s123.close()
        s456 = ExitStack()
        xnT = sb(s456, "xnT2", [P, KC, TO], BF16)
        tg3 = ((0, 384), (384, 384), (768, 384))

        def norm_own_pre(t, a):
            norm_pre_g(xt[a][:], "xt%d" % a, xsb2[a][:], "xsbb%d" % a, a)

        def norm_own_tp(t, a):
            norm_tp_g(xsb2[a][:], "xsbb%d" % a, xnT[:, :, t * P:(t + 1) * P], "xnT2")

        with ExitStack() as s4:
            xt = [sb(s4, "xt%d" % i, [P, D]) for i in range(2)]
            xsb2 = [sb(s4, "xsbb%d" % i, [P, D], BF16) for i in range(2)]
            ycg = sb(s4, "ycg", [P, 8, TO], BF16)
            yag = sb(s4, "yag", [P, 8, TO], BF16)
            T.dma("sp", ycg[:], ycg_d.rearrange("c p t -> p c t"), r=["ycg_d"], w=["ycg"], key="ycg")
            T.dma("sp", yag[:], yag_d.rearrange("c p t -> p c t"), r=["yag_d"], w=["yag"], key="yag")
            load_gbc("g_xattn")
            wos = [wview(14 + nb) for nb in range(4)]
            T.dma("sp", xt[0][:], x_own[0], w=["xt0"], key="xt0")
            for t in range(9):
                a = t % 2
                if t + 1 < 9:
                    T.dma("sp", xt[1 - a][:], x_own[t + 1], w=["xt%d" % (1 - a)], key="xt%d" % (1 - a))
                cs = slice(t * P, (t + 1) * P)
                xk = "xt%d" % a
                for nb in range(4):
                    wo, wok = wos[nb]
                    bA, bB = 2 * (nb % 2), 2 * (nb % 2) + 1
                    for kc in range(8):
                        mm(ps[bA][:], ycg[:, kc, cs], wo[:, kc, :], kc == 0, kc == 7, r=["ycg", wok], w=[PS(bA)])
                    for kc in range(8):
                        mm(ps[bB][:], yag[:, kc, cs], wo[:, 8 + kc, :], kc == 0, kc == 7, r=["yag", wok], w=[PS(bB)])
                    xs_ = xt[a][:, nb * 512:(nb + 1) * 512]
                    T.op("dve", lambda e: e.scalar_tensor_tensor(out=xs_, in0=ps[bA][:], scalar=rsc[:, t:t + 1], in1=xs_,
                                                                 op0=ALU.mult, op1=ALU.add),
                         r=[PS(bA), "rsc", xk], w=[xk])
                    T.op("dve", lambda e: e.scalar_tensor_tensor(out=xs_, in0=ps[bB][:], scalar=rsa[:, t:t + 1], in1=xs_,
                                                                 op0=ALU.mult, op1=ALU.add),
                         r=[PS(bB), "rsa", xk], w=[xk])
                if t >= 1:
                    norm_own_tp(t - 1, 1 - a)
                T.dma("sp", xres_d[t], xt[a][:], r=[xk], w=["xres_d"], key=xk)
                norm_own_pre(t, a)
            norm_own_tp(8, 0)
            for nb in range(4):
                wrel(14 + nb)
            T.barrier()

        with ExitStack() as s5:
            xt = [sb(s5, "xu%d" % i, [P, D]) for i in range(2)]
            xsb2 = [sb(s5, "xsbc%d" % i, [P, D], BF16) for i in range(2)]
            mkT = sb(s5, "mkT", [P, 4, 256], BF16)
            mvb = sb(s5, "mvb", [P, 2, 512], BF16)
            cmvb = sb(s5, "cmvb", [P, 8, 4, 129], BF16)
            mkTs = sb(s5, "mkTs", [P, 4, 4, 256], BF16)
            T.dma("sp", mkT[:].rearrange("p h t -> p (h t)"), mkT_d, r=["mkT_d"], w=["mkT"], key="mkT")
            T.dma("sp", mvb[:].rearrange("p m c -> p (m c)"), mvb_d, r=["mvb_d"], w=["mvb"], key="mvb")
            T.dma("sp", mkTs[:].rearrange("p s h t -> p (s h t)"), mkTs_d, r=["mkTs_d"], w=["mkTs"], key="mkTs")
            T.op("pool", lambda e: e.memset(cmvb[:], 1.0), w=["cmvb"])
            T.dma("sp", cmvb[:].rearrange("p a h d -> p (a h) d")[:, :, 0:128],
                  cmvb_d.rearrange("p a (h d) -> p (a h) d", h=4), r=["cmvb_d"], w=["cmvb"], key="cmvb")

            qxT = sb(s5, "qxT", [P, 4, TO], BF16)
            oxT = sb(s5, "oxT", [P, 4, TO], BF16)
            wxq_, wxqk = wview(18)
            for h in range(4):
                banks = (0, 1, 2) if h % 2 == 0 else (3, 4, 5)
                for kc in range(KC):
                    for gi, (c0, nn) in enumerate(tg3):
                        mm(ps[banks[gi]][:, 0:nn], wxq_[:, kc, h * P:(h + 1) * P], xnT[:, kc, c0:c0 + nn],
                           kc == 0, kc == KC - 1, r=[wxqk, "xnT2"], w=[PS(banks[gi])])
                for gi, (c0, nn) in enumerate(tg3):
                    T.op("act", lambda e: e.copy(out=qxT[:, h, c0:c0 + nn], in_=ps[banks[gi]][:, 0:nn]),
                         r=[PS(banks[gi])], w=["qxT"])
            wrel(18)

            PTx = [sb(s5, "PTx%d" % i, [P, 512], BF16) for i in range(2)]
            rlx = sb(s5, "rlx", [P, 512])
            PTs = [sb(s5, "PTs%d" % i, [P, 2, P], BF16) for i in range(4)]
            oxs = sb(s5, "oxs", [P, 512])
            oxsb = sb(s5, "oxsb", [P, 512], BF16)
            for i in range(4):
                T.op("pool", lambda e: e.memset(PTs[i][:], 0.0), w=["PTs%d" % i])
            xsteps = [(hh_, g_, mt_) for hh_ in range(4) for g_ in range(2) for mt_ in range(2)]

            def xS(i):
                hh_, g_, mt_ = xsteps[i]
                mm(ps[i % 2][:], mkT[:, hh_, mt_ * P:(mt_ + 1) * P], qxT[:, hh_, g_ * 512:(g_ + 1) * 512], True, True,
                   r=["mkT", "qxT"], w=[PS(i % 2)])

            def xstep(i):
                hh_, g_, mt_ = xsteps[i]
                bO, bL = 2 + 2 * g_, 3 + 2 * g_
                if i + 1 < len(xsteps) and xsteps[i + 1][0] == hh_:
                    xS(i + 1)
                T.op("act", lambda e: e.activation(out=PTx[i % 2][:], in_=ps[i % 2][:], func=AF.Exp, scale=SCALE),
                     r=[PS(i % 2)], w=["PTx%d" % (i % 2)])
                mm(ps[bO][:], mvb[:, mt_, hh_ * P:(hh_ + 1) * P], PTx[i % 2][:], mt_ == 0, mt_ == 1,
                   r=["mvb", "PTx%d" % (i % 2)], w=[PS(bO)])
                mm(ps[bL][:], onesb, PTx[i % 2][:], mt_ == 0, mt_ == 1, r=["PTx%d" % (i % 2)] + CONST, w=[PS(bL)])
                if mt_ == 1:
                    T.op("dve", lambda e: e.reciprocal(out=rlx[:], in_=ps[bL][:]), r=[PS(bL)], w=["rlx"])
                    T.op("dve", lambda e: e.tensor_tensor(out=oxT[:, hh_, g_ * 512:(g_ + 1) * 512], in0=ps[bO][:],
                                                          in1=rlx[:], op=ALU.mult), r=[PS(bO), "rlx"], w=["oxT"])

            for h in range(4):
                xS(4 * h)
                for i in range(4 * h, 4 * h + 4):
                    xstep(i)
                bOs = 2 + (h % 2) * 2
                for s2 in range(4):
                    bS = s2 % 2
                    for mt in range(2):
                        mm(ps[bS][:, mt * 32:(mt + 1) * 32], mkTs[:, s2, h, mt * P:(mt + 1) * P],
                           qxT[:, h, 1024 + s2 * 32:1024 + (s2 + 1) * 32], True, True, r=["mkTs", "qxT"], w=[PS(bS)],
                           sig=(mt == 1))
                    T.op("act", lambda e: e.activation(out=PTs[s2][:, :, s2 * 32:(s2 + 1) * 32],
                                                       in_=ps[bS][:, 0:64].rearrange("p (k q) -> p k q", k=2),
                                                       func=AF.Exp, scale=SCALE),
                         r=[PS(bS)], w=["PTs%d" % s2])
                    for mt in range(2):
                        mm(ps[bOs][:, 0:129], PTs[s2][:, mt, :], cmvb[:, s2 * 2 + mt, h, :],
                           s2 == 0 and mt == 0, s2 == 3 and mt == 1, r=["PTs%d" % s2, "cmvb"], w=[PS(bOs)])
                T.op("dve", lambda e: e.reciprocal(out=sstat[:, h:h + 1], in_=ps[bOs][:, 128:129]),
                     r=[PS(bOs)], w=["sstat"])
                T.op("dve", lambda e: e.tensor_scalar(out=oxs[:, h * P:(h + 1) * P], in0=ps[bOs][:, 0:128],
                                                      scalar1=sstat[:, h:h + 1], scalar2=None, op0=ALU.mult),
                     r=[PS(bOs), "sstat"], w=["oxs"])
            T.op("dve", lambda e: e.tensor_copy(out=oxsb[:], in_=oxs[:]), r=["oxs"], w=["oxsb"])
            for h in range(4):
                tp(pb[0][:, h * P:(h + 1) * P], oxsb[:, h * P:(h + 1) * P], identb, r=["oxsb"] + CONST, w=[PB(0)],
                   sig=(h == 3))
            T.op("act", lambda e: e.copy(out=oxT[:, :, 1024:1152], in_=pb[0][:, 0:512].rearrange("p (h t) -> p h t", h=4)),
                 r=[PB(0)], w=["oxT"])

            load_gbc("g_mlp")
            wxo_, wxok = wview(19)
            T.dma("sp", xt[0][:], xres_d[0], r=["xres_d"], w=["xt0"], key="xt0")
            for t in range(9):
                a = t % 2
                if t + 1 < 9:
                    T.dma("sp", xt[1 - a][:], xres_d[t + 1], r=["xres_d"], w=["xt%d" % (1 - a)], key="xt%d" % (1 - a))
                cs = slice(t * P, (t + 1) * P)
                xk = "xt%d" % a
                for nb in range(4):
                    b = nb
                    for kc in range(4):
                        mm(ps[b][:], oxT[:, kc, cs], wxo_[:, kc, nb * 512:(nb + 1) * 512], kc == 0, kc == 3,
                           r=["oxT", wxok], w=[PS(b)])
                    xs_ = xt[a][:, nb * 512:(nb + 1) * 512]
                    T.op("dve", lambda e: e.tensor_tensor(out=xs_, in0=ps[b][:], in1=xs_, op=ALU.add),
                         r=[PS(b), xk], w=[xk])
                if t >= 1:
                    norm_own_tp(t - 1, 1 - a)
                T.dma("sp", xres_d[t], xt[a][:], r=[xk], w=["xres_d"], key=xk)
                norm_own_pre(t, a)
            norm_own_tp(8, 0)
            wrel(19)
            T.barrier()

        with ExitStack() as s6:
            yacc = sb(s6, "yacc", [P, 9, D])
            hT = sb(s6, "hT", [P, 4, TO], BF16)
            rl_ = [sb(s6, "relu%d" % i, [P, 512]) for i in range(2)]
            ri = [0]
            for t in range(9):
                T.dma("sp", yacc[:, t, :], xres_d[t], r=["xres_d"], w=["yacc%d" % t], key="yacc%d" % t)
            load_gbc("g_final")
            T.op("dve", lambda e: e.memset(sstat[:, 0:16], 0.0), w=["sstat"])
            fjunk = xnT[:].rearrange("p k t -> p (k t)")[:, 0:D]

            def final_tile(t):
                yk = "yacc%d" % t
                fk = "fin%d" % t
                T.op("act", lambda e: e.activation(out=fjunk, in_=yacc[:, t, :], func=AF.Square,
                                                   accum_out=sstat[:, t:t + 1]), r=[yk, "sstat"], w=[fk, "xnT2"])
                rstd_from(sstat[:, t:t + 1], sstat[:, 16 + t:17 + t], sstat[:, t:t + 1], 1.0 / D, [fk])
                T.op("pool", lambda e: e.tensor_tensor(out=yacc[:, t, :], in0=yacc[:, t, :], in1=gbc[:],
                                                       op=ALU.mult), r=[yk, "gbc"], w=[yk])
                T.op("act", lambda e: e.mul(out=yacc[:, t, :], in_=yacc[:, t, :], mul=sstat[:, 16 + t:17 + t]),
                     r=[yk, fk], w=[yk])
                T.dma("sp", y_own[t], yacc[:, t, :], r=[yk], key=yk)
            for hb in range(16):
                wu, wuk = wview(20 + 2 * hb)
                wd, wdk = wview(21 + 2 * hb)
                for hc in range(4):
                    banks = (0, 1, 2) if hc % 2 == 0 else (3, 4, 5)
                    for kc in range(KC):
                        for gi, (c0, nn) in enumerate(tg3):
                            mm(ps[banks[gi]][:, 0:nn], wu[:, kc, hc * P:(hc + 1) * P], xnT[:, kc, c0:c0 + nn],
                               kc == 0, kc == KC - 1, r=[wuk, "xnT2"], w=[PS(banks[gi])])
                    for gi, (c0, nn) in enumerate(tg3):
                        rr = ri[0] % 2
                        ri[0] += 1
                        T.op("act", lambda e: e.activation(out=rl_[rr][:, 0:nn], in_=ps[banks[gi]][:, 0:nn], func=AF.Relu),
                             r=[PS(banks[gi])], w=["relu%d" % rr])
                        T.op("dve", lambda e: e.tensor_tensor(out=hT[:, hc, c0:c0 + nn], in0=rl_[rr][:, 0:nn],
                                                              in1=rl_[rr][:, 0:nn], op=ALU.mult),
                             r=["relu%d" % rr], w=["hT%d" % hc])
                wrel(20 + 2 * hb)
                def dmm(t, nb, hcs):
                    b = (t * 4 + nb) % 6
                    for hc in hcs:
                        mm(ps[b][:], hT[:, hc, t * P:(t + 1) * P], wd[:, hc, nb * 512:(nb + 1) * 512], hc == 0, hc == 3,
                           r=["hT%d" % hc, wdk], w=[PS(b)])

                def dadd(t, nb):
                    b = (t * 4 + nb) % 6
                    yk = "yacc%d" % t
                    ys_ = yacc[:, t, nb * 512:(nb + 1) * 512]
                    T.op("dve", lambda e: e.tensor_tensor(out=ys_, in0=ps[b][:], in1=ys_, op=ALU.add),
                         r=[PS(b), yk], w=[yk])

                grp = [(t, nb) for t in range(9) for nb in range(4)]
                for (t, nb) in grp[:6]:
                    dmm(t, nb, (0, 1, 2))
                for (t, nb) in grp[:6]:
                    dmm(t, nb, (3,))
                    dadd(t, nb)
                    if hb == 15 and nb == 3:
                        final_tile(t)
                for (t, nb) in grp[6:]:
                    dmm(t, nb, (0, 1, 2, 3))
                    dadd(t, nb)
                    if hb == 15 and nb == 3:
                        final_tile(t)
                wrel(21 + 2 * hb)

            T.barrier()
        s456.close()
        pstk[0].close()
    return nc


def _consts():
    s = np.arange(P)[:, None]
    t = np.arange(P)[None, :]
    cf = np.zeros((P, 14, P), np.float32)
    cf[:, 0] = np.eye(P)
    cf[:, 1] = (s <= t)
    cf[:, 2] = (s == 127) * np.ones((1, P))
    cf[:, 3] = ((s // 32) == (t // 32)) & (s <= t)
    cf[:, 4] = (s == (t // 32) * 32 + 31)
    for s2 in range(4):
        cf[:, 5 + s2] = (s == 32 * s2 + 31) * np.ones((1, P))
        cf[:, 9 + s2] = (s == 127) & ((t // 32) == s2)
    cf[:, 13] = 1.0
    cb = np.zeros((P, 4, P), np.float32)
    cb[:, 0] = np.eye(P)
    cb[:, 1] = np.where(s <= t, 0.0, NEG)
    cb[:, 2] = np.where(((s // 32) == (t // 32)) & (s <= t), 0.0, NEG)
    cb[:, 3] = 1.0
    return cf, cb


_NC = None


def kernel(x_prompt, x_sample, cache_k, cache_v, cache_logf, cache_conv, cache_mem_k, cache_mem_v, mem_prompt,
           g_mix, w_in, b_f, conv_w, g_conv_out, g_attn_out, w_out, g_xattn, g_mem, w_xq, w_xkv, w_xo,
           g_mlp, w_up, w_down, g_final):
    global _NC
    f = lambda a: np.ascontiguousarray(np.asarray(a, dtype=np.float32))
    x_prompt, x_sample = f(x_prompt), f(x_sample)
    cache_k, cache_v, cache_logf, cache_conv = f(cache_k), f(cache_v), f(cache_logf), f(cache_conv)
    cache_mem_k, cache_mem_v, mem_prompt = f(cache_mem_k), f(cache_mem_v), f(mem_prompt)
    cf, cb = _consts()
    bc = lambda g: np.ascontiguousarray(np.broadcast_to(f(g).reshape(1, D), (P, D)))
    shared = {
        "w_in": f(w_in)[0], "w_out": f(w_out)[0], "w_xq": f(w_xq)[0], "w_xkv": f(w_xkv)[0], "w_xo": f(w_xo)[0],
        "w_up": f(w_up)[0], "w_down": f(w_down)[0],
        "g_mix": bc(g_mix), "g_xattn": bc(g_xattn), "g_mem": bc(g_mem), "g_mlp": bc(g_mlp), "g_final": bc(g_final),
        "cf32": cf, "cb16": cb,
    }
    small_base = np.zeros((P, 112), np.float32)
    small_base[:, 0:8] = f(b_f).reshape(1, 8)
    small_base[:, 8:16] = f(g_conv_out).reshape(8, P).T
    small_base[:, 16:24] = f(g_attn_out).reshape(8, P).T
    small_base[:, 24:48] = f(conv_w).reshape(3, 8, P).transpose(2, 1, 0).reshape(P, 24)
    in_maps = []
    for c in range(8):
        b, j = c // 4, c % 4
        npast = 8 * j
        x_halo = np.zeros((P, D), np.float32)
        if j > 0:
            x_halo[0:2] = x_prompt[b, 1024 * j - 2:1024 * j]
        sm = small_base.copy()
        sm[:, 48 + npast:80] = -NEG
        sm[:, 80:80 + npast] = 1.0
        m = dict(shared)
        m.update({
            "x_own": np.ascontiguousarray(np.concatenate(
                [x_prompt[b, 1024 * j:1024 * (j + 1)].reshape(8, P, D), x_sample[4 * c:4 * c + 4].reshape(1, P, D)], 0)),
            "x_halo": x_halo, "smallp": sm,
            "mem": np.ascontiguousarray(mem_prompt[b]),
            "ck": np.ascontiguousarray(cache_k[0, 4 * c:4 * c + 4].reshape(4, 2048, 1024)),
            "cv": np.ascontiguousarray(cache_v[0, 4 * c:4 * c + 4].reshape(4, 2048, 1024)),
            "clogf": np.ascontiguousarray(cache_logf[0, 4 * c:4 * c + 4]),
            "cconv": np.ascontiguousarray(cache_conv[0, 4 * c:4 * c + 4].reshape(8, 1024)),
            "cmk": np.ascontiguousarray(cache_mem_k[0, 4 * c:4 * c + 4].reshape(4, 256, 512)),
            "cmv": np.ascontiguousarray(cache_mem_v[0, 4 * c:4 * c + 4].reshape(4, 256, 512)),
        })
        in_maps.append(m)
    if _NC is None:
        _NC = build()
    res = run_bass_kernel_spmd(_NC, in_maps, core_ids=list(range(8))).results

    y_prompt = np.zeros((2, 4096, D), np.float32)
    y_sample = np.zeros((32, 32, D), np.float32)
    nkp = np.zeros((1, 2, 4096, 8, 128), np.float32)
    nvp = np.zeros((1, 2, 4096, 8, 128), np.float32)
    nfp = np.zeros((1, 2, 4096, 8), np.float32)
    ncp = np.zeros((1, 2, 2, 1024), np.float32)
    nmk = np.zeros((1, 2, 256, 4, 128), np.float32)
    nmv = np.zeros((1, 2, 256, 4, 128), np.float32)
    nks = np.zeros((1, 32, 32, 8, 128), np.float32)
    nvs = np.zeros((1, 32, 32, 8, 128), np.float32)
    nfs = np.zeros((1, 32, 32, 8), np.float32)
    ncs = np.zeros((1, 32, 2, 1024), np.float32)
    for c in range(8):
        b, j = c // 4, c % 4
        r = res[c]
        sl = slice(1024 * j, 1024 * (j + 1))
        y_prompt[b, sl] = r["y_own"][0:8].reshape(1024, D)
        y_sample[4 * c:4 * c + 4] = r["y_own"][8].reshape(4, 32, D)
        nkp[0, b, sl] = r["k_new"][0:8].reshape(1024, 8, 128)
        nvp[0, b, sl] = r["v_new"][0:8].reshape(1024, 8, 128)
        nks[0, 4 * c:4 * c + 4] = r["k_new"][8].reshape(4, 32, 8, 128)
        nvs[0, 4 * c:4 * c + 4] = r["v_new"][8].reshape(4, 32, 8, 128)
        fn = r["f_new"]
        nfp[0, b, sl] = fn[:, 0:8, :].transpose(1, 0, 2).reshape(1024, 8)
        nfs[0, 4 * c:4 * c + 4] = fn[:, 8, :].reshape(4, 32, 8)
        cn = r["conv_new"].reshape(5, 2, 1024)
        if j == 3:
            ncp[0, b] = cn[0]
        ncs[0, 4 * c:4 * c + 4] = cn[1:5]
        if j == 0:
            nmk[0, b] = r["mk_new"].reshape(256, 4, 128)
            nmv[0, b] = r["mv_new"].reshape(256, 4, 128)
    return (y_prompt, y_sample, nkp, nvp, nfp, ncp, nmk, nmv, nks, nvs, nfs, ncs)
```
